# Optimizing a Trainium2 kernel written in Bass

```python
import jax, jax.numpy as jnp
from jax import lax
import numpy as np

D_MODEL = 1024
BATCH = 16
SEQ = 2048
DEPTH = 2
DEC_BATCH = 128
DEC_SEQ = 1
PAST_LEN = 16384
PAGE_SIZE = 128

D_MIX = D_MODEL
D_POOL = D_MIX // 4
POOL_WINDOWS = (2, 4, 8, 16)
POOL_GROUPS = len(POOL_WINDOWS)
POOL_CG = D_POOL // POOL_GROUPS
POOL_BUF = max(POOL_WINDOWS) - 1
HEAD_DIM = 64
D_ATTN = D_MIX // 2
N_HEADS = D_ATTN // HEAD_DIM
N_KV = max(1, N_HEADS // 4)
Q_PER_KV = N_HEADS // N_KV
D_KV = N_KV * HEAD_DIM
WINDOW = 128
BLOCK = 128
ROPE_THETA = 10000.0
QK_EPS = 1e-6
D_RWKV = D_MIX - D_POOL - D_ATTN
RWKV_HEAD = 64
RWKV_HEADS = D_RWKV // RWKV_HEAD
LORA_W = 32
LORA_A = 32
D_SHIFT = 3 * D_RWKV + LORA_W + LORA_A
GN_EPS = 64e-5
NORM_EPS = 1e-6
D_IN = 2 * D_POOL + 2 * D_ATTN + 2 * D_KV + D_SHIFT + D_RWKV

kernel_name = "hymba_pool_swa_rwkv7_step"


def _split(z, sizes):
    idx = [int(i) for i in np.cumsum(sizes)[:-1]]
    return jnp.split(z, idx, axis=-1)


def _rmsnorm(x, g, eps):
    xf = x.astype(jnp.float32)
    y = xf * lax.rsqrt(jnp.mean(xf * xf, axis=-1, keepdims=True) + eps)
    return (y * g.astype(jnp.float32)).astype(x.dtype)


def _rope(x, pos):
    half = HEAD_DIM // 2
    freqs = ROPE_THETA ** (-jnp.arange(half, dtype=jnp.float32) / half)
    ang = pos.astype(jnp.float32)[:, None] * freqs[None, :]
    c = jnp.cos(ang)[None, :, None, :]
    s = jnp.sin(ang)[None, :, None, :]
    xf = x.astype(jnp.float32)
    x1, x2 = xf[..., :half], xf[..., half:]
    return jnp.concatenate([x1 * c - x2 * s, x1 * s + x2 * c], axis=-1).astype(x.dtype)


def _sink_softmax(s, mask, sinks):
    s = jnp.where(mask, s, -jnp.inf)
    sink = sinks.astype(jnp.float32)[:, :, None, None]
    m = jnp.maximum(jnp.max(s, axis=-1, keepdims=True), sink)
    p = jnp.exp(s - m)
    denom = jnp.sum(p, axis=-1, keepdims=True) + jnp.exp(sink - m)
    return p / denom


def _swa_prompt(q, k, v, sinks):
    B, T = q.shape[:2]
    nb = T // BLOCK
    scale = HEAD_DIM ** -0.5
    qb = q.reshape(B, nb, BLOCK, N_KV, Q_PER_KV, HEAD_DIM)

    def with_prev(t):
        tb = t.reshape(B, nb, BLOCK, N_KV, HEAD_DIM)
        prev = jnp.pad(tb, ((0, 0), (1, 0), (0, 0), (0, 0), (0, 0)))[:, :-1]
        return jnp.concatenate([prev, tb], axis=2)

    kx, vx = with_prev(k), with_prev(v)
    s = jnp.einsum('bnqkgd,bnskd->bnkgqs', qb, kx, preferred_element_type=jnp.float32) * scale
    qi = jnp.arange(BLOCK)[:, None] + BLOCK
    ki = jnp.arange(2 * BLOCK)[None, :]
    band = (ki <= qi) & (ki > qi - WINDOW)
    kabs = jnp.arange(nb)[:, None] * BLOCK - BLOCK + jnp.arange(2 * BLOCK)[None, :]
    mask = band[None, :, :] & (kabs >= 0)[:, None, :]
    mask = mask[None, :, None, None]
    p = _sink_softmax(s, mask, sinks.reshape(N_KV, Q_PER_KV))
    o = jnp.einsum('bnkgqs,bnskd->bnqkgd', p.astype(vx.dtype), vx)
    return o.reshape(B, T, N_HEADS * HEAD_DIM)


def _swa_cached(q, k, v, k_buf, v_buf, pos0, sinks):
    B, T = q.shape[:2]
    scale = HEAD_DIM ** -0.5
    kx = jnp.concatenate([k_buf.astype(k.dtype), k], axis=1)
    vx = jnp.concatenate([v_buf.astype(v.dtype), v], axis=1)
    L = kx.shape[1]
    kpos = pos0 - k_buf.shape[1] + jnp.arange(L)
    qpos = pos0 + jnp.arange(T)
    mask = (kpos[None, :] <= qpos[:, None]) & (kpos[None, :] > qpos[:, None] - WINDOW) & (kpos[None, :] >= 0)
    qg = q.reshape(B, T, N_KV, Q_PER_KV, HEAD_DIM)
    s = jnp.einsum('btkgd,bskd->bkgts', qg, kx, preferred_element_type=jnp.float32) * scale
    p = _sink_softmax(s, mask, sinks.reshape(N_KV, Q_PER_KV))
    o = jnp.einsum('bkgts,bskd->btkgd', p.astype(vx.dtype), vx).reshape(B, T, N_HEADS * HEAD_DIM)
    return o, kx[:, -WINDOW:], vx[:, -WINDOW:]


def _pool_mixer(u, buf, pos0, pool_w, pool_scale):
    B, T, C = u.shape
    P = buf.shape[1]
    ext = jnp.concatenate([buf.astype(u.dtype), u], axis=1)
    cs = jnp.cumsum(ext.astype(jnp.float32), axis=1)
    cs = jnp.pad(cs, ((0, 0), (1, 0), (0, 0)))
    pos = pos0 + jnp.arange(T)
    uf = u.astype(jnp.float32)
    outs = []
    for g, w in enumerate(POOL_WINDOWS):
        sl = slice(g * POOL_CG, (g + 1) * POOL_CG)
        win_sum = cs[:, P + 1:P + 1 + T, sl] - cs[:, P + 1 - w:P + 1 - w + T, sl]
        cnt = jnp.minimum(pos + 1, w).astype(jnp.float32)[None, :, None]
        outs.append(win_sum / cnt - uf[..., sl])
    pooled = jnp.concatenate(outs, axis=-1).reshape(B, T, POOL_GROUPS, POOL_CG)
    mixed = jnp.einsum('btgc,gcd->btgd', pooled, pool_w.astype(jnp.float32)).reshape(B, T, C)
    mixed = mixed * pool_scale.astype(jnp.float32)
    return mixed.astype(u.dtype), ext[:, -P:]


def _rwkv_mix(xin, shift_prev, wkv0, p):
    B, T, _ = xin.shape
    f32 = jnp.float32
    prev = jnp.concatenate([shift_prev[:, None].astype(xin.dtype), xin[:, :-1]], axis=1)
    xf = xin.astype(f32)
    xs = xf + (prev.astype(f32) - xf) * p['mu'].astype(f32)
    r, k, v, wd, ad = _split(xs, (D_RWKV, D_RWKV, D_RWKV, LORA_W, LORA_A))
    w_log = -jax.nn.softplus(-(p['w0'].astype(f32) + jnp.tanh(wd) @ p['w_up'].astype(f32))) - 0.5
    decay = jnp.exp(-jnp.exp(w_log))
    a = jax.nn.sigmoid(p['a0'].astype(f32) + ad @ p['a_up'].astype(f32))
    heads = lambda t: t.reshape(B, T, RWKV_HEADS, RWKV_HEAD)
    kk = heads(k * p['k_k'].astype(f32))
    kk = kk / jnp.maximum(jnp.sqrt(jnp.sum(kk * kk, axis=-1, keepdims=True)), 1e-12)
    k = k * (1.0 + (a - 1.0) * p['k_a'].astype(f32))
    r, k, v, decay, a = heads(r), heads(k), heads(v), heads(decay), heads(a)

    def step(S, inp):
        r_t, w_t, k_t, v_t, kk_t, a_t = inp
        sa = jnp.einsum('bhvk,bhk->bhv', S, kk_t)
        S = S * w_t[:, :, None, :] - sa[..., None] * (kk_t * a_t)[:, :, None, :] + v_t[..., None] * k_t[:, :, None, :]
        o = jnp.einsum('bhvk,bhk->bhv', S, r_t)
        return S, o

    seq = tuple(jnp.moveaxis(t, 1, 0) for t in (r, decay, k, v, kk, a))
    S, o = lax.scan(step, wkv0.astype(f32), seq)
    o = jnp.moveaxis(o, 0, 1)
    mu_o = jnp.mean(o, axis=-1, keepdims=True)
    var_o = jnp.mean(jnp.square(o - mu_o), axis=-1, keepdims=True)
    o = (o - mu_o) * lax.rsqrt(var_o + GN_EPS)
    o = o * p['ln_g'].astype(f32).reshape(RWKV_HEADS, RWKV_HEAD) + p['ln_b'].astype(f32).reshape(RWKV_HEADS, RWKV_HEAD)
    bonus = jnp.sum(r * k * p['r_k'].astype(f32).reshape(RWKV_HEADS, RWKV_HEAD), axis=-1, keepdims=True) * v
    o = (o + bonus).reshape(B, T, D_RWKV)
    return o.astype(xin.dtype), xin[:, -1], S.astype(wkv0.dtype)


def _layer(x, pos0, pool_buf, k_buf, v_buf, shift_prev, wkv0, p):
    B, T, _ = x.shape
    dt = x.dtype
    h = _rmsnorm(x, p['norm_g'], NORM_EPS)
    z = h @ p['w_in']
    u, g_pool, q, k, v, g_attn, rin, g_rwkv = _split(
        z, (D_POOL, D_POOL, D_ATTN, D_KV, D_KV, D_ATTN, D_SHIFT, D_RWKV))
    a_out, new_pool = _pool_mixer(u, pool_buf, pos0, p['pool_w'], p['pool_scale'])
    pos = pos0 + jnp.arange(T)
    q = _rope(_rmsnorm(q.reshape(B, T, N_HEADS, HEAD_DIM), p['q_g'], QK_EPS), pos)
    k = _rope(_rmsnorm(k.reshape(B, T, N_KV, HEAD_DIM), p['k_g'], QK_EPS), pos)
    v = v.reshape(B, T, N_KV, HEAD_DIM)
    if k_buf is None:
        b_out = _swa_prompt(q, k, v, p['sinks'])
        new_k, new_v = k[:, -WINDOW:], v[:, -WINDOW:]
    else:
        b_out, new_k, new_v = _swa_cached(q, k, v, k_buf, v_buf, pos0, p['sinks'])
    c_out, new_shift, new_wkv = _rwkv_mix(rin, shift_prev, wkv0, p)
    mix = jnp.concatenate([
        a_out.astype(dt) * jax.nn.silu(g_pool),
        b_out.astype(dt) * jax.nn.silu(g_attn),
        c_out.astype(dt) * jax.nn.silu(g_rwkv)], axis=-1)
    y = x + mix @ p['w_out']
    return y, (new_pool, new_k, new_v, new_shift, new_wkv)


def setup_inputs(seed: int = 0) -> dict:
    key = jax.random.key(seed)
    ks = jax.random.split(key, 32)
    f = jnp.float32
    nrm = lambda kk, shape, sc: jax.random.normal(kk, shape, f) * sc
    uni = lambda kk, shape, lo, hi: jax.random.uniform(kk, shape, f, lo, hi)
    return {
        'x_prompt': nrm(ks[0], (BATCH, SEQ, D_MODEL), 1.0),
        'x_sample': nrm(ks[1], (DEC_BATCH, DEC_SEQ, D_MODEL), 1.0),
        'state_pool': nrm(ks[2], (DEPTH, DEC_BATCH, POOL_BUF, D_POOL), 1.0),
        'cache_swa_k': nrm(ks[3], (DEPTH, DEC_BATCH, WINDOW, N_KV, HEAD_DIM), 1.0),
        'cache_swa_v': nrm(ks[4], (DEPTH, DEC_BATCH, WINDOW, N_KV, HEAD_DIM), 1.0),
        'state_rwkv_shift': nrm(ks[5], (DEPTH, DEC_BATCH, D_SHIFT), 1.0),
        'state_rwkv_wkv': nrm(ks[6], (DEPTH, DEC_BATCH, RWKV_HEADS, RWKV_HEAD, RWKV_HEAD), 0.3),
        'norm_g': 1.0 + nrm(ks[7], (DEPTH, D_MODEL), 0.05),
        'w_in': nrm(ks[8], (DEPTH, D_MODEL, D_IN), D_MODEL ** -0.5),
        'w_out': nrm(ks[9], (DEPTH, D_MIX, D_MODEL), D_MIX ** -0.5),
        'pool_w': nrm(ks[10], (DEPTH, POOL_GROUPS, POOL_CG, POOL_CG), POOL_CG ** -0.5),
        'pool_scale': 1.0 + nrm(ks[11], (DEPTH, D_POOL), 0.1),
        'q_norm_g': 1.0 + nrm(ks[12], (DEPTH, HEAD_DIM), 0.05),
        'k_norm_g': 1.0 + nrm(ks[13], (DEPTH, HEAD_DIM), 0.05),
        'attn_sinks': nrm(ks[14], (DEPTH, N_HEADS), 0.5),
        'rwkv_mu': uni(ks[15], (DEPTH, D_SHIFT), 0.0, 1.0),
        'rwkv_w0': uni(ks[16], (DEPTH, D_RWKV), -5.0, 1.0),
        'rwkv_w_up': nrm(ks[17], (DEPTH, LORA_W, D_RWKV), 0.5 * LORA_W ** -0.5),
        'rwkv_a0': nrm(ks[18], (DEPTH, D_RWKV), 0.5),
        'rwkv_a_up': nrm(ks[19], (DEPTH, LORA_A, D_RWKV), 0.5 * LORA_A ** -0.5),
        'rwkv_k_k': 0.85 + nrm(ks[20], (DEPTH, D_RWKV), 0.05),
        'rwkv_k_a': 1.0 + nrm(ks[21], (DEPTH, D_RWKV), 0.05),
        'rwkv_r_k': nrm(ks[22], (DEPTH, D_RWKV), 0.1),
        'rwkv_ln_g': 1.0 + nrm(ks[23], (DEPTH, D_RWKV), 0.05),
        'rwkv_ln_b': nrm(ks[24], (DEPTH, D_RWKV), 0.02),
    }


def reference(x_prompt, x_sample, state_pool, cache_swa_k, cache_swa_v, state_rwkv_shift, state_rwkv_wkv,
              norm_g, w_in, w_out, pool_w, pool_scale, q_norm_g, k_norm_g, attn_sinks,
              rwkv_mu, rwkv_w0, rwkv_w_up, rwkv_a0, rwkv_a_up, rwkv_k_k, rwkv_k_a, rwkv_r_k,
              rwkv_ln_g, rwkv_ln_b):
    Bp = x_prompt.shape[0]
    dt = x_prompt.dtype
    yp, ys = x_prompt, x_sample
    pool_p, pool_s, kp, ksmp, vp, vsmp, shp, shs, wkp, wks = [], [], [], [], [], [], [], [], [], []
    for l in range(DEPTH):
        p = {
            'norm_g': norm_g[l], 'w_in': w_in[l], 'w_out': w_out[l],
            'pool_w': pool_w[l], 'pool_scale': pool_scale[l],
            'q_g': q_norm_g[l], 'k_g': k_norm_g[l], 'sinks': attn_sinks[l],
            'mu': rwkv_mu[l], 'w0': rwkv_w0[l], 'w_up': rwkv_w_up[l], 'a0': rwkv_a0[l],
            'a_up': rwkv_a_up[l], 'k_k': rwkv_k_k[l], 'k_a': rwkv_k_a[l], 'r_k': rwkv_r_k[l],
            'ln_g': rwkv_ln_g[l], 'ln_b': rwkv_ln_b[l],
        }
        yp, (a1, b1, c1, d1, e1) = _layer(
            yp, 0,
            jnp.zeros((Bp, POOL_BUF, D_POOL), dt), None, None,
            jnp.zeros((Bp, D_SHIFT), dt),
            jnp.zeros((Bp, RWKV_HEADS, RWKV_HEAD, RWKV_HEAD), dt), p)
        ys, (a2, b2, c2, d2, e2) = _layer(
            ys, PAST_LEN, state_pool[l], cache_swa_k[l], cache_swa_v[l],
            state_rwkv_shift[l], state_rwkv_wkv[l], p)
        pool_p.append(a1); pool_s.append(a2)
        kp.append(b1); ksmp.append(b2)
        vp.append(c1); vsmp.append(c2)
        shp.append(d1); shs.append(d2)
        wkp.append(e1); wks.append(e2)
    return (yp, ys,
            jnp.stack(pool_p), jnp.stack(pool_s),
            jnp.stack(kp), jnp.stack(ksmp),
            jnp.stack(vp), jnp.stack(vsmp),
            jnp.stack(shp), jnp.stack(shs),
            jnp.stack(wkp), jnp.stack(wks))
```

```python
import contextlib
import numpy as np
import ml_dtypes
import concourse.bass as bass
import concourse.mybir as mybir
from concourse.bass_utils import run_bass_kernel_spmd

F32 = mybir.dt.float32
BF16 = mybir.dt.bfloat16
AF = mybir.ActivationFunctionType
ALU = mybir.AluOpType
AX = mybir.AxisListType

NCORES = 8
D = 1024
DIN = 2880
SEQ = 2048
NCH = SEQ // 128
GRP = 4
SP = 2
SB = 16
OU, OGP, OQ, OKK, OV, OGA, ORIN, OGR = 0, 256, 512, 1024, 1152, 1280, 1792, 2624
KAP = float(np.exp(-0.5))
NORM_EPS = 1e-6
QK_EPS = 1e-6
GN_EPS = 64e-5
PAST = 16384

ENGS = ("pe", "act", "dve", "pool", "sp")
USE_FENCE = False
USE_F32R = False
F32R = mybir.dt.float32r


class _PEProxy:
    def __init__(self, eng):
        self._e = eng

    def matmul(self, out, lhsT=None, rhs=None, **kw):
        if USE_F32R and lhsT.dtype == F32:
            lhsT = lhsT.bitcast(F32R)
            rhs = rhs.bitcast(F32R)
        return self._e.matmul(out, lhsT=lhsT, rhs=rhs, **kw)

    def __getattr__(self, n):
        return getattr(self._e, n)


class _OutProxy:
    def __init__(self, eng, names):
        self._e = eng
        self._n = names

    def _fix(self, ap):
        if ap.dtype == F32 and ap.tensor.name in self._n:
            return ap.bitcast(F32R)
        return ap

    def __getattr__(self, n):
        f = getattr(self._e, n)
        if not callable(f):
            return f

        def w(*a, **kw):
            if "out" in kw:
                kw["out"] = self._fix(kw["out"])
            return f(*a, **kw)
        return w


class Op:
    def __init__(self, eng, fn, reads, writes, is_dma):
        self.eng = eng
        self.fn = fn
        self.reads = reads
        self.writes = writes
        self.is_dma = is_dma
        self.waits = {}
        self.signal = False
        self.slot = None
        self.slotval = 0
        self.known = None


def _related(a, b):
    return a == b or a.startswith(b + "/") or b.startswith(a + "/")


class Sched:
    NSLOT = 8

    def __init__(self):
        self.ops = []
        self.per_eng = {e: [] for e in ENGS}
        self.res = {}
        self.pending_mp = []
        self.zb = None
        self.r32 = set()

    PSUM_ROOTS = ("pz0", "pz1", "ptb", "pA", "pB", "pCD", "pE")

    def _deps(self, op):
        deps = []
        for tok in op.reads:
            root = tok.split("/", 1)[0]
            for t2, rec in self.res.get(root, {}).items():
                if _related(tok, t2) and rec[0] is not None:
                    deps.append(rec[0])
                if root in self.PSUM_ROOTS and _related(tok, t2):
                    deps.extend(r for r in rec[1] if r.eng != op.eng)
        for tok in op.writes:
            root = tok.split("/", 1)[0]
            for t2, rec in self.res.get(root, {}).items():
                if _related(tok, t2):
                    if rec[0] is not None:
                        deps.append(rec[0])
                    deps.extend(rec[1])
        return deps

    def _fence(self):
        pend = self.pending_mp
        self.pending_mp = []
        last = pend[-1]
        for o in pend:
            o.mp_pending = False
        ap, m = last.mp
        zb = self.zb
        rd = tuple(dict.fromkeys(t for o in pend for t in o.reads)) + ("zb",)
        wr = tuple(dict.fromkeys(t for o in pend for t in o.writes))
        fap, fin = self.fence_aps
        self.add("pe", lambda e: e.transpose(out=fap, in_=fin, identity=fin), rd + ("identb",), wr + ("ptb",))

    def add(self, eng, fn, reads=(), writes=(), dma=False, mp=None):
        def n1(t):
            parts = t.split("/")
            if parts[0] == "pCD" and len(parts) > 1:
                return "pCD/" + parts[1]
            return parts[0] if parts[0] in self.PSUM_ROOTS else t
        norm = lambda toks: tuple(dict.fromkeys(n1(t) for t in toks))
        op = Op(eng, fn, norm(reads), norm(writes), dma)
        op.mp = mp
        op.mp_pending = mp is not None
        deps = self._deps(op)
        if (eng != "pe" or dma) and any(getattr(d, "mp_pending", False) for d in deps):
            self._fence()
            deps = self._deps(op)
        self.ops.append(op)
        self.per_eng[eng].append(op)
        if mp is not None:
            self.pending_mp.append(op)
        op.deps = deps
        for tok in op.reads:
            root = tok.split("/", 1)[0]
            rec = self.res.setdefault(root, {}).setdefault(tok, [None, []])
            rec[1].append(op)
        for tok in op.writes:
            root = tok.split("/", 1)[0]
            d = self.res.setdefault(root, {})
            for t2 in [t for t in d if t == tok or t.startswith(tok + "/")]:
                del d[t2]
            d[tok] = [op, []]
        return op

    def build(self):
        if self.pending_mp:
            self._fence()
        for e in ENGS:
            i = 0
            n = 0
            for op in self.per_eng[e]:
                if op.is_dma:
                    op.slot = (e, n % self.NSLOT)
                    op.slotval = n // self.NSLOT + 1
                    n += 1
                else:
                    i += 1
                    op.ord = i
        known_eng = {e: {} for e in ENGS}
        last_on_slot = {}
        for op in self.ops:
            kn = known_eng[op.eng]
            needs = {}
            deps = list(op.deps)
            if op.is_dma and op.slot in last_on_slot:
                deps.append(last_on_slot[op.slot])
            for d in deps:
                if d is op:
                    continue
                if (not d.is_dma) and d.eng == "pe" and op.eng == "pe" and not op.is_dma:
                    continue
                k = d.slot if d.is_dma else d.eng
                val = d.slotval if d.is_dma else d.ord
                if kn.get(k, 0) >= val:
                    continue
                if needs.get(k, (0, None))[0] < val:
                    needs[k] = (val, d)
            for k, (val, d) in needs.items():
                d.signal = True
                kn[k] = max(kn.get(k, 0), val)
                for k2, v2 in d.known.items():
                    if kn.get(k2, 0) < v2:
                        kn[k2] = v2
            op.waits = {k: d for k, (val, d) in needs.items()}
            op.known = dict(kn)
            if op.is_dma:
                last_on_slot[op.slot] = op
            op.deps = None
        for e in ENGS:
            c = 0
            for op in self.per_eng[e]:
                if not op.is_dma:
                    if op.signal:
                        c += 1
                    op.sigval = c
        return self

    def emit(self, nc):
        engmap = {"pe": "tensor", "act": "scalar", "dve": "vector", "pool": "gpsimd", "sp": "sync"}
        with contextlib.ExitStack() as st:
            sems = {}
            for e in ENGS:
                sems[e] = st.enter_context(nc.semaphore("s_" + e))
            for e in ENGS:
                nslots = min(self.NSLOT, sum(1 for o in self.per_eng[e] if o.is_dma))
                for j in range(nslots):
                    sems[(e, j)] = st.enter_context(nc.semaphore("d_%s%d" % (e, j)))
            block = st.enter_context(nc.Block())

            def mk(e):
                def body(eng):
                    for op in self.per_eng[e]:
                        for k, d in op.waits.items():
                            if d.is_dma:
                                eng.wait_ge(sems[k], 16 * d.slotval)
                            else:
                                eng.wait_ge(sems[k], d.sigval)
                        if op.is_dma:
                            pe_ = eng
                        elif e == "pe":
                            pe_ = _PEProxy(eng)
                        elif USE_F32R:
                            pe_ = _OutProxy(eng, self.r32)
                        else:
                            pe_ = eng
                        ins = op.fn(pe_)
                        if op.is_dma:
                            ins.then_inc(sems[op.slot], 16)
                        elif op.signal:
                            ins.then_inc(sems[e], 1)
                    lasts = {}
                    for op in self.per_eng[e]:
                        if op.is_dma:
                            lasts[op.slot] = op
                    for slot, op in lasts.items():
                        eng.wait_ge(sems[slot], 16 * op.slotval)
                return body

            for e in ENGS:
                if self.per_eng[e]:
                    getattr(block, engmap[e])(mk(e))


def _consts():
    s = np.arange(128)[:, None]
    t = np.arange(128)[None, :]
    c = {}
    c["ident"] = np.eye(128, dtype=np.float32)
    c["identb"] = np.eye(128).astype(ml_dtypes.bfloat16)
    c["msi"] = np.concatenate([(s < t), (s <= t)], axis=1).astype(np.float32)
    c["mst"] = (t < s).astype(np.float32)
    c["tmk"] = (-KAP * ((s <= t).astype(np.float64) - (s <= 63).astype(np.float64))).astype(np.float32)
    c["coefs"] = np.stack([KAP * (np.arange(128) <= 63), -KAP * (np.arange(128) > 63)], axis=1).astype(np.float32)
    pm = np.zeros((3, 4, 128, 128), np.float64)
    for g, w in enumerate((2, 4, 8, 16)):
        pm[0, g] = ((s > t - w) & (s <= t)) / w - (s == t)
        cnt = np.minimum(t + 1, w)
        pm[1, g] = ((s > t - w) & (s <= t)) / cnt - (s == t)
        pm[2, g] = (s > 128 + t - w) / w
    c["poolm"] = np.ascontiguousarray(pm.transpose(2, 0, 1, 3)).astype(np.float32)
    c["amask"] = np.stack([(s <= t), (s > t)], axis=1).astype(ml_dtypes.bfloat16)
    freqs = (np.float32(10000.0) ** (-np.arange(32, dtype=np.float32) / np.float32(32))).astype(np.float32)
    pos = (np.arange(NCH)[None, :] * 128 + np.arange(128)[:, None]).astype(np.float32)
    ang = (pos[:, :, None] * freqs[None, None, :]).astype(np.float32)
    c["rope"] = np.stack([np.cos(ang), np.sin(ang)], axis=1).astype(np.float32)
    angs = (np.float32(PAST) * freqs).astype(np.float32)
    c["ropes"] = np.tile(np.stack([np.cos(angs), np.sin(angs)])[None], (128, 1, 1)).astype(np.float32)
    p = np.arange(128)
    smc = np.zeros((128, 64), np.float32)
    g_ = p // 64
    smc[p, g_ * 4 + p % 4] = 1.0
    smc[p, 8 + (p % 64) // 4] = 1.0
    smc[:, 24] = 1 - g_
    smc[:, 25] = g_
    smc[p, 26 + (p // 2) % 4] = 1.0
    smc[:, 30] = 1 - (p % 2)
    smc[:, 31] = p % 2
    smc[p, 32 + p % 8] = 1.0
    smc[p, 40 + p // 8] = 1.0
    c["smc"] = smc
    c["indBT"] = np.ascontiguousarray(smc[:, 40:56].T)
    return c


PARAM_NAMES = ["norm_g", "w_in", "w_out", "pool_w", "pool_scale", "q_norm_g", "k_norm_g", "attn_sinks",
               "rwkv_mu", "rwkv_w0", "rwkv_w_up", "rwkv_a0", "rwkv_a_up", "rwkv_k_k", "rwkv_k_a",
               "rwkv_r_k", "rwkv_ln_g", "rwkv_ln_b"]
PARAM_SHAPES = {"norm_g": [2, 1024], "w_in": [2, 1024, 2880], "w_out": [2, 1024, 1024], "pool_w": [2, 4, 64, 64],
                "pool_scale": [2, 256], "q_norm_g": [2, 64], "k_norm_g": [2, 64], "attn_sinks": [2, 8],
                "rwkv_mu": [2, 832], "rwkv_w0": [2, 256], "rwkv_w_up": [2, 32, 256], "rwkv_a0": [2, 256],
                "rwkv_a_up": [2, 32, 256], "rwkv_k_k": [2, 256], "rwkv_k_a": [2, 256], "rwkv_r_k": [2, 256],
                "rwkv_ln_g": [2, 256], "rwkv_ln_b": [2, 256]}
CONST_SHAPES = {"ident": ([128, 128], F32), "identb": ([128, 128], BF16), "msi": ([128, 256], F32),
                "mst": ([128, 128], F32), "tmk": ([128, 128], F32), "coefs": ([128, 2], F32),
                "poolm": ([128, 3, 4, 128], F32), "amask": ([128, 2, 128], BF16),
                "rope": ([128, 2, NCH, 32], F32), "ropes": ([128, 2, 32], F32),
                "smc": ([128, 64], F32), "indBT": ([16, 128], F32)}


def build_program(do_sample=True, nchunks=NCH, nseq=SP, stage=99, sstage=99):
    nc = bass.Bass("TRN2", target_bir_lowering=False)
    DI = lambda n, s, d=F32: nc.dram_tensor(n, s, d, kind="ExternalInput").ap()
    DO = lambda n, s, d=F32: nc.dram_tensor(n, s, d, kind="ExternalOutput").ap()
    xp = DI("xp", [SP, SEQ, D])
    xs_in = DI("xs", [SB, D])
    spool = DI("spool", [2, SB, 15, 256])
    ck = DI("ck", [2, SB, 128, 128])
    cv = DI("cv", [2, SB, 128, 128])
    sshift = DI("sshift", [2, SB, 832])
    swkv = DI("swkv", [2, SB, 4, 64, 64])
    P = {n: DI(n, PARAM_SHAPES[n]) for n in PARAM_NAMES}
    C = {n: DI("c_" + n, sh, dt) for n, (sh, dt) in CONST_SHAPES.items()}
    yp = DO("yp", [SP, SEQ, D])
    ys = DO("ys", [SB, D])
    poolp = DO("poolp", [2, SP, 15, 256])
    pools = DO("pools", [2, SB, 15, 256])
    kp = DO("kp", [2, SP, 128, 128])
    ksm = DO("ksm", [2, SB, 128, 128])
    vp = DO("vp", [2, SP, 128, 128])
    vsm = DO("vsm", [2, SB, 128, 128])
    shp = DO("shp", [2, SP, 832])
    shs = DO("shs", [2, SB, 832])
    wkp = DO("wkp", [2, SP, 4, 64, 64])
    wks = DO("wks", [2, SB, 4, 64, 64])
    wsc_in = nc.dram_tensor("wsc_in", [2, 128, 8, DIN], BF16, kind="Internal").ap()
    wsc_out = nc.dram_tensor("wsc_out", [2, 128, 8, D], BF16, kind="Internal").ap()

    S = Sched()

    class _Probe:
        rec = None

        def matmul(self, out, lhsT=None, rhs=None, **kw):
            self.rec = ("mm", out, lhsT, rhs)

        def transpose(self, **kw):
            self.rec = ("tr",)

    def A(eng, fn, r=(), w=(), dma=False):
        mp = None
        if eng == "pe" and not dma:
            pr = _Probe()
            fn(pr)
            if pr.rec[0] == "mm" and pr.rec[2].dtype == F32:
                S.r32.add(pr.rec[2].tensor.name)
                S.r32.add(pr.rec[3].tensor.name)
            if USE_FENCE and pr.rec[0] == "mm" and pr.rec[2].dtype == F32:
                out = pr.rec[1]
                assert len(out.shape) == 2, out.shape
                mp = (out[:, 0:2], out.shape[0])
        return S.add(eng, fn, r, w, dma=dma, mp=mp)
    with contextlib.ExitStack() as st:
        T = lambda n, s, d=F32: st.enter_context(nc.sbuf_tensor(n, s, d))
        PS = lambda n, s, d=F32: st.enter_context(nc.psum_tensor(n, s, d))
        win = T("win", [128, 8, DIN], BF16)
        wout = T("wout", [128, 8, D], BF16)
        gT = T("gT", [128, D])
        muT = T("muT", [128, 832])
        rows = T("rows", [128, 5, 256])
        qkgain = T("qkgain", [128, 2, 10, 64])
        esink = T("esink", [128, 2, 8])
        poolw = T("poolw", [64, 2, 4, 64])
        pscl = T("pscl", [64, 2, 256])
        lora = T("lora", [33, 2, 2, 256])
        ident = T("ident", [128, 128])
        identb = T("identb", [128, 128], BF16)
        zb = T("zb", [128, 128], BF16)
        S.zb = zb
        msi = T("msi", [128, 256])
        mst = T("mst", [128, 128])
        tmk = T("tmk", [128, 128])
        coefs = T("coefs", [128, 2])
        poolm = T("poolm", [128, 3, 4, 128])
        amask = T("amask", [128, 2, 128], BF16)
        rope = T("rope", [128, 2, NCH, 32])
        xg = T("xg", [128, GRP, D])
        hm = T("hm", [128, D], BF16)
        hx = T("hx", [128, D], BF16)
        gates2 = T("gates2", [128, 2, 256])
        hTm = T("hTm", [128, 8, 128], BF16)
        gates = T("gates", [128, D])
        stat = T("stat", [128, 32])
        ubuf = T("ubuf", [128, 2, 2, 256])
        pTsb = T("pTsb", [64, 512])
        qk = T("qk", [128, 10, 64])
        qk2 = T("qk2", [128, 10, 64])
        qk3 = T("qk3", [128, 10, 64])
        qT = T("qT", [128, 4, 128], BF16)
        kTb = T("kTb", [128, 2, 2, 128], BF16)
        vaug = T("vaug", [128, 2, 2, 2, 65], BF16)
        PT = T("PT", [128, 4, 512], BF16)
        atmp = T("atmp", [128, 8, 64])
        rin = T("rin", [128, 832])
        prevL = T("prevL", [128, 2, 832])
        xsb = T("xsb", [128, 832])
        lwin = T("lwin", [33, 2, 128])
        sg = T("sg", [128, 512])
        ra = T("ra", [128, 10, 256])
        XT = T("XT", [64, 4, 4, 128])
        AM = T("AM", [128, 6, 4, 128])
        Stt = T("Stt", [64, 4, 64])
        Gs = T("Gs", [128, 4, 64])
        Us = T("Us", [128, 4, 64])
        STs = T("STs", [64, 2, 4, 64])
        edge = T("edge", [64, 4, 2])
        pz = [PS("pz0", [128, 512]), PS("pz1", [128, 512])]
        ptb = PS("ptb", [128, 8, 128], BF16)
        pA = PS("pA", [128, 512])
        pB = PS("pB", [128, 512])
        pCD = PS("pCD", [128, 4, 256])
        pE = PS("pE", [128, 4, 128])
        S.fence_aps = (ptb[0:32, 0, 0:32], identb[0:32, 0:32])

        dq = ["sp"]

        def dma(out, in_, r=(), w=(), q="sp"):
            A(q, lambda e: e.dma_start(out=out, in_=in_), r, w, dma=True)

        xgf = xg[:, :, :].rearrange("p a c -> p (a c)")
        for n, tl in (("ident", ident), ("identb", identb), ("msi", msi), ("mst", mst),
                      ("amask", amask), ("rope", rope)):
            full = tuple(slice(None) for _ in CONST_SHAPES[n][0])
            dma(tl[full], C[n][full], w=[n])
        dma(xgf[:, 0:1536], C["poolm"].rearrange("p a g t -> p (a g t)"), w=["xg"])
        A("act", lambda e: e.activation(out=poolm[:, :, :, :].rearrange("p a g t -> p (a g t)"), in_=xgf[:, 0:1536], func=AF.Copy),
          ["xg"], ["poolm"])
        dma(xgf[:, 1536:1664], C["tmk"][:, :], w=["xg"])
        dma(xgf[:, 1664:1666], C["coefs"][:, :], w=["xg"])
        A("act", lambda e: e.activation(out=tmk[:, :], in_=xgf[:, 1536:1664], func=AF.Copy), ["xg"], ["tmk"])
        A("act", lambda e: e.activation(out=coefs[:, :], in_=xgf[:, 1664:1666], func=AF.Copy), ["xg"], ["coefs"])
        for kc in range(8):
            for h2 in range(2):
                c0 = h2 * 1440
                dma(win[:, kc, c0:c0 + 1440], P["w_in"][0, kc * 128:(kc + 1) * 128, c0:c0 + 1440],
                    w=["win/%d/%d" % (kc, h2)], q="pool")
        for kc in range(8):
            dma(wout[:, kc, :], P["w_out"][0, kc * 128:(kc + 1) * 128, :], w=["wout/%d" % kc], q="pool")
        for kc in range(8):
            for h2 in range(2):
                c0 = h2 * 1440
                dma(wsc_in[1, :, kc, c0:c0 + 1440], P["w_in"][1, kc * 128:(kc + 1) * 128, c0:c0 + 1440],
                    w=["wsc_in1/%d/%d" % (kc, h2)], q="pool")
            dma(wsc_out[1, :, kc, :], P["w_out"][1, kc * 128:(kc + 1) * 128, :], w=["wsc_out1/%d" % kc], q="pool")
        for kc in range(8):
            dma(wsc_in[0, :, kc, :], win[:, kc, :], r=["win/%d" % kc], w=["wsc_in0/%d" % kc])
            dma(wsc_out[0, :, kc, :], wout[:, kc, :], r=["wout/%d" % kc], w=["wsc_out0/%d" % kc])
        for l in range(2):
            dma(qkgain[:, l, 0:8, :], P["q_norm_g"][l].partition_broadcast(128).unsqueeze(1).to_broadcast([128, 8, 64]),
                w=["qkgain"])
            dma(qkgain[:, l, 8:10, :], P["k_norm_g"][l].partition_broadcast(128).unsqueeze(1).to_broadcast([128, 2, 64]),
                w=["qkgain"])
            dma(esink[:, l, :], P["attn_sinks"][l].partition_broadcast(128), w=["esink"])
            dma(xgf[0:64, 2048 + l * 256:2048 + (l + 1) * 256].rearrange("c (g d) -> c g d", g=4), P["pool_w"][l].rearrange("g c d -> c g d"), w=["xg"])
            dma(pscl[:, l, :], P["pool_scale"][l].partition_broadcast(64), w=["pscl"])
            lst = xgf[0:33, 2560 + l * 512:2560 + (l + 1) * 512].rearrange("r (j n) -> r j n", j=2)
            dma(lst[0:32, 0, :], P["rwkv_w_up"][l], w=["xg"])
            dma(lst[0:32, 1, :], P["rwkv_a_up"][l], w=["xg"])
            dma(lst[32:33, 0, :], P["rwkv_w0"][l:l + 1, :], w=["xg"])
            dma(lst[32:33, 1, :], P["rwkv_a0"][l:l + 1, :], w=["xg"])
        A("dve", lambda e: e.tensor_scalar(out=qkgain[:, :, 0:8, :], in0=qkgain[:, :, 0:8, :], scalar1=0.125,
                                           scalar2=None, op0=ALU.mult), ["qkgain"], ["qkgain"])
        A("act", lambda e: e.activation(out=esink[:, :, :], in_=esink[:, :, :], func=AF.Exp), ["esink"], ["esink"])
        A("dve", lambda e: e.tensor_tensor(out=poolw[:, :, :, :], in0=xgf[0:64, 2048:2560].rearrange("c (l g d) -> c l g d", l=2, g=4),
                                           in1=pscl[:, :, :].rearrange("c l (g d) -> c l g d", g=4),
                                           op=ALU.mult), ["xg", "pscl"], ["poolw"])
        A("act", lambda e: e.activation(out=lora[:, :, :, :].rearrange("r l j n -> r (l j n)"), in_=xgf[0:33, 2560:3584], func=AF.Copy),
          ["xg"], ["lora"])
        A("pool", lambda e: e.memset(lwin[32:33, :, :], 1.0), [], ["lwin/ones"])
        A("pool", lambda e: e.memset(zb[:, :], 0.0), [], ["zb"])
        A("pool", lambda e: e.memset(vaug[:, :, :, :, :].rearrange("p l a g c -> p (l a g) c")[:, :, 64:65], 1.0), [], ["vaug"])

        cur_layer = [None]

        win_loaded = [0]

        def reload_win(l):
            for kc in range(8):
                dma(win[:, kc, :], wsc_in[l, :, kc, :], r=["wsc_in%d/%d" % (l, kc)], w=["win/%d" % kc], q="pool")
            dma(gT[:, :], P["norm_g"][l].partition_broadcast(128), w=["gT"], q="pool")
            win_loaded[0] = l

        def load_layer(l, first):
            if not first:
                if win_loaded[0] != l:
                    reload_win(l)
                for kc in range(8):
                    dma(wout[:, kc, :], wsc_out[l, :, kc, :], r=["wsc_out%d/%d" % (l, kc)], w=["wout/%d" % kc])
            else:
                dma(gT[:, :], P["norm_g"][l].partition_broadcast(128), w=["gT"])
            dma(muT[:, :], P["rwkv_mu"][l].partition_broadcast(128), w=["muT"])
            for i, n in enumerate(("rwkv_k_k", "rwkv_k_a", "rwkv_r_k", "rwkv_ln_g", "rwkv_ln_b")):
                dma(rows[:, i, :], P[n][l].partition_broadcast(128), w=["rows/%d" % i])
            cur_layer[0] = l

        def rsqrt_small(dst, src, n, scale, eps, rtok, wtok):
            A("dve", lambda e: e.tensor_scalar(out=dst, in0=src, scalar1=scale, scalar2=eps, op0=ALU.mult,
                                               op1=ALU.add), rtok, wtok)
            A("act", lambda e: e.activation(out=dst, in_=dst, func=AF.Ln), wtok, wtok)
            A("act", lambda e: e.activation(out=dst, in_=dst, func=AF.Exp, scale=-0.5), wtok, wtok)

        def sigmoid_act(tmp, src, rtok, tmptok):
            A("act", lambda e: e.activation(out=tmp, in_=src, func=AF.Exp, scale=-1.0), rtok, tmptok)
            A("act", lambda e: e.activation(out=tmp, in_=tmp, func=AF.Ln, bias=1.0), tmptok, tmptok)
            A("act", lambda e: e.activation(out=tmp, in_=tmp, func=AF.Exp, scale=-1.0), tmptok, tmptok)

        def silu_to(dst, src_ps, ncols, tmp, rtok, wtok, tmptok):
            sigmoid_act(tmp, src_ps, rtok, tmptok)
            A("dve", lambda e: e.tensor_tensor(out=dst, in0=src_ps, in1=tmp, op=ALU.mult), rtok + tmptok, wtok)

        def rmsnorm_T(xt, np_, xtok):
            A("act", lambda e: e.activation(out=hx[0:np_, :], in_=xt, func=AF.Square,
                                            accum_out=stat[0:np_, 0:1]), [xtok], ["hx", "stat/0"])
            rsqrt_small(stat[0:np_, 0:1], stat[0:np_, 0:1], 1, 1.0 / D, NORM_EPS, ["stat/0"], ["stat/0"])
            A("dve", lambda e: e.scalar_tensor_tensor(out=hx[0:np_, :], in0=xt, scalar=stat[0:np_, 0:1],
                                                      in1=gT[0:np_, :], op0=ALU.mult, op1=ALU.mult),
              [xtok, "stat/0", "gT"], ["hx"])
            for kc in range(8):
                A("pe", lambda e, kc=kc: e.transpose(out=ptb[:, kc, 0:np_], in_=hx[0:np_, kc * 128:(kc + 1) * 128],
                                                     identity=identb[0:np_, 0:np_]), ["hx", "identb"], ["ptb"])
            A("act", lambda e: e.activation(out=hTm[:, :, 0:np_], in_=ptb[:, :, 0:np_], func=AF.Copy),
              ["ptb"], ["hTm"])

        def win_group(pzt, ptok, c0, ncols, np_, perm=False):
            for kc in range(8):
                if perm:
                    rhs = win[:, kc, c0:c0 + ncols].rearrange("p (g t d) -> p t g d", g=2, t=4, d=64)
                    out = pzt[0:np_, 0:ncols].rearrange("p (t g d) -> p t g d", g=2, t=4, d=64)
                else:
                    rhs = win[:, kc, c0:c0 + ncols]
                    out = pzt[0:np_, 0:ncols]
                A("pe", lambda e, kc=kc, rhs=rhs, out=out: e.matmul(out, lhsT=hTm[:, kc, 0:np_], rhs=rhs,
                                                                    start=(kc == 0), stop=(kc == 7)),
                  ["hTm", "win/%d" % kc], [ptok])

        def wout_apply(xt, np_, xtok, between=None):
            for kc in range(8):
                A("pe", lambda e, kc=kc: e.transpose(out=ptb[:, kc, 0:np_], in_=hm[0:np_, kc * 128:(kc + 1) * 128],
                                                     identity=identb[0:np_, 0:np_]), ["hm", "identb"], ["ptb"])
            A("act", lambda e: e.activation(out=hTm[:, :, 0:np_], in_=ptb[:, :, 0:np_], func=AF.Copy),
              ["ptb"], ["hTm"])
            for half in range(2):
                for kc in range(8):
                    A("pe", lambda e, kc=kc, half=half: e.matmul(pz[half][0:np_, :], lhsT=hTm[:, kc, 0:np_],
                                                                 rhs=wout[:, kc, half * 512:(half + 1) * 512],
                                                                 start=(kc == 0), stop=(kc == 7)),
                      ["hTm", "wout/%d" % kc], ["pz%d" % half])
            if between is not None:
                between()
            for half in range(2):
                A("dve", lambda e, half=half: e.tensor_tensor(out=xt[:, half * 512:(half + 1) * 512],
                                                              in0=pz[half][0:np_, :],
                                                              in1=xt[:, half * 512:(half + 1) * 512], op=ALU.add),
                  ["pz%d" % half, xtok], [xtok])

        def qk_norm_rope(np_, l, cosT, sinT, rtok="rope"):
            A("dve", lambda e: e.tensor_tensor(out=qk2[0:np_], in0=qk[0:np_], in1=qk[0:np_], op=ALU.mult),
              ["qk"], ["qk2"])
            A("dve", lambda e: e.tensor_reduce(out=stat[0:np_, 1:11], in_=qk2[0:np_], axis=AX.X, op=ALU.add),
              ["qk2"], ["stat/1"])
            rsqrt_small(stat[0:np_, 1:11], stat[0:np_, 1:11], 10, 1.0 / 64, QK_EPS, ["stat/1"], ["stat/1"])
            A("dve", lambda e: e.tensor_tensor(out=qk[0:np_], in0=qk[0:np_],
                                               in1=stat[0:np_, 1:11].unsqueeze(2).to_broadcast([np_, 10, 64]),
                                               op=ALU.mult), ["qk", "stat/1"], ["qk"])
            A("dve", lambda e: e.tensor_tensor(out=qk[0:np_], in0=qk[0:np_], in1=qkgain[0:np_, l], op=ALU.mult),
              ["qk", "qkgain"], ["qk"])
            cb = cosT.unsqueeze(1).to_broadcast([np_, 10, 32])
            sb = sinT.unsqueeze(1).to_broadcast([np_, 10, 32])
            x1 = qk[0:np_, :, 0:32]
            x2 = qk[0:np_, :, 32:64]
            A("dve", lambda e: e.tensor_tensor(out=qk2[0:np_, :, 0:32], in0=x1, in1=cb, op=ALU.mult),
              ["qk", rtok], ["qk2/a"])
            A("pool", lambda e: e.tensor_tensor(out=qk2[0:np_, :, 32:64], in0=x2, in1=sb, op=ALU.mult),
              ["qk", rtok], ["qk2/b"])
            A("dve", lambda e: e.tensor_tensor(out=qk3[0:np_, :, 0:32], in0=x1, in1=sb, op=ALU.mult),
              ["qk", rtok], ["qk3/a"])
            A("pool", lambda e: e.tensor_tensor(out=qk3[0:np_, :, 32:64], in0=x2, in1=cb, op=ALU.mult),
              ["qk", rtok], ["qk3/b"])
            A("dve", lambda e: e.tensor_tensor(out=x1, in0=qk2[0:np_, :, 0:32], in1=qk2[0:np_, :, 32:64],
                                               op=ALU.subtract), ["qk2"], ["qk"])
            A("pool", lambda e: e.tensor_tensor(out=x2, in0=qk3[0:np_, :, 0:32], in1=qk3[0:np_, :, 32:64],
                                                op=ALU.add), ["qk3", "qk"], ["qk"])

        def chunk_front(s, c, l, gi):
            first = (c == 0)
            last = (c == nchunks - 1)
            par = c % 2
            xt = xg[:, gi, :]
            xtok = "xg/%d" % gi
            if l == 0:
                dma(xt, xp[s, c * 128:(c + 1) * 128, :], w=[xtok])
            rmsnorm_T(xt, 128, xtok)
            yield
            win_group(pz[0], "pz0", OU, 512, 128)
            ucur = ubuf[:, l, par, :]
            uprev = ubuf[:, l, 1 - par, :]
            A("act", lambda e: e.activation(out=ucur, in_=pz[0][:, 0:256], func=AF.Copy), ["pz0"], ["ubuf/%d/%d" % (l, par)])
            silu_to(gates[:, 0:256], pz[0][:, 256:512], 256, qk3[:, 0:4, :].rearrange("p h d -> p (h d)"), ["pz0"], ["gates/0"], ["qk3"])
            yield
            if last:
                dma(poolp[l, s, :, :], ubuf[113:128, l, par, :], r=["ubuf/%d/%d" % (l, par)])
            win_group(pz[1], "pz1", OQ, 512, 128, perm=True)
            A("act", lambda e: e.activation(out=qk[:, 0:8, :], in_=pz[1][:, :].rearrange("p (h d) -> p h d", d=64),
                                            func=AF.Copy), ["pz1"], ["qk"])
            yield
            for kc in range(8):
                b_ = win[:, kc, OKK:OKK + 256]
                rhs2 = bass.AP(b_.tensor, b_.offset, [list(b_.ap[0]), [OGR - OKK, 2], [1, 256]])
                A("pe", lambda e, kc=kc, rhs2=rhs2: e.matmul(pz[0][:, :].rearrange("p (a c) -> p a c", a=2), lhsT=hTm[:, kc, :], rhs=rhs2,
                                                             start=(kc == 0), stop=(kc == 7)), ["hTm", "win/%d" % kc], ["pz0"])
            A("act", lambda e: e.activation(out=qk[:, 8:10, :], in_=pz[0][:, 0:128].rearrange("p (h d) -> p h d", d=64),
                                            func=AF.Copy), ["pz0"], ["qk"])
            vcur = vaug[:, l, par]
            vprev = vaug[:, l, 1 - par]
            vtok = "vaug/%d/%d" % (l, par)
            A("act", lambda e: e.activation(out=vcur[:, :, 0:64], in_=pz[0][:, 128:256].rearrange("p (h d) -> p h d", d=64),
                                            func=AF.Copy), ["pz0"], [vtok])
            if last:
                A("act", lambda e: e.activation(out=atmp[:, 0:2, :], in_=pz[0][:, 128:256].rearrange("p (h d) -> p h d", d=64),
                                                func=AF.Copy), ["pz0"], ["atmp"])
                dma(vp[l, s, :, :], atmp[:, 0:2, :].rearrange("p h d -> p (h d)"), r=["atmp"])
            silu_to(gates2[:, par, :], pz[0][:, 256:512], 256, qk3[:, 0:4, :].rearrange("p h d -> p (h d)"), ["pz0"], ["gates2/%d" % par], ["qk3"])
            yield
            win_group(pz[1], "pz1", OGA, 512, 128)
            silu_to(gates[:, 256:768], pz[1][:, :], 512, atmp[:, :, :].rearrange("p h d -> p (h d)"), ["pz1"], ["gates/1"], ["atmp"])
            yield
            win_group(pz[0], "pz0", ORIN, 512, 128)
            A("act", lambda e: e.activation(out=rin[:, 0:512], in_=pz[0][:, :], func=AF.Copy), ["pz0"], ["rin/a"])
            yield
            win_group(pz[1], "pz1", ORIN + 512, 320, 128)
            A("dve", lambda e: e.tensor_copy(out=rin[:, 512:832], in_=pz[1][:, 0:320]), ["pz1"], ["rin/b"])
            yield

        def chunk_qk(s, c, l):
            qk_norm_rope(128, l, rope[:, 0, c, :], rope[:, 1, c, :])
            if c == nchunks - 1:
                dma(kp[l, s, :, :], qk[:, 8:10, :].rearrange("p h d -> p (h d)"), r=["qk"])

        def chunk_mid(s, c, l, gi):
            first = (c == 0)
            last = (c == nchunks - 1)
            par = c % 2
            ucur = ubuf[:, l, par, :]
            uprev = ubuf[:, l, 1 - par, :]
            vcur = vaug[:, l, par]
            vprev = vaug[:, l, 1 - par]
            vtok = "vaug/%d/%d" % (l, par)
            for g in range(4):
                A("pe", lambda e, g=g: e.matmul(pA[0:64, g * 128:(g + 1) * 128], lhsT=ucur[:, g * 64:(g + 1) * 64],
                                                rhs=poolm[:, 1 if first else 0, g, :], start=True, stop=first),
                  ["ubuf/%d/%d" % (l, par), "poolm"], ["pA"])
                if not first:
                    A("pe", lambda e, g=g: e.matmul(pA[0:64, g * 128:(g + 1) * 128], lhsT=uprev[:, g * 64:(g + 1) * 64],
                                                    rhs=poolm[:, 2, g, :], start=False, stop=True),
                      ["ubuf/%d/%d" % (l, 1 - par), "poolm"], ["pA"])
            A("act", lambda e: e.activation(out=pTsb[:, :], in_=pA[0:64, :], func=AF.Copy), ["pA"], ["pTsb"])
            for g in range(4):
                A("pe", lambda e, g=g: e.matmul(pB[:, g * 64:(g + 1) * 64], lhsT=pTsb[:, g * 128:(g + 1) * 128],
                                                rhs=poolw[:, l, g, :], start=True, stop=True),
                  ["pTsb", "poolw"], ["pB"])
            A("dve", lambda e: e.tensor_tensor(out=hm[:, 0:256], in0=pB[:, 0:256], in1=gates[:, 0:256], op=ALU.mult),
              ["pB", "gates/0"], ["hm/0"])

            for tpair in range(2):
                for tt in range(2):
                    t_ = tpair * 2 + tt
                    A("pe", lambda e, t_=t_, tt=tt: e.transpose(
                        out=pA[:, tt * 128:(tt + 1) * 128],
                        in_=qk[:, 2 * t_:2 * t_ + 2, :].rearrange("p h d -> p (h d)"), identity=ident[:, :]),
                      ["qk", "ident"], ["pA"])
                A("act", lambda e, tpair=tpair: e.activation(
                    out=qT[:, 2 * tpair:2 * tpair + 2, :],
                    in_=pA[:, 0:256].rearrange("p (t q) -> p t q", t=2), func=AF.Copy), ["pA"], ["qT"])
            kcur = kTb[:, l, par, :]
            kprev = kTb[:, l, 1 - par, :]
            ktok = "kTb/%d/%d" % (l, par)
            A("pe", lambda e: e.transpose(out=pB[:, 0:128], in_=qk[:, 8:10, :].rearrange("p h d -> p (h d)"),
                                          identity=ident[:, :]), ["qk", "ident"], ["pB"])
            A("act", lambda e: e.activation(out=kcur, in_=pB[:, 0:128], func=AF.Copy), ["pB"], [ktok])
            scp = [pA, pB]
            for blk in range(1 if first else 2):
                kt_ = kcur if blk == 0 else kprev
                ktk = ktok if blk == 0 else "kTb/%d/%d" % (l, 1 - par)
                for g in range(2):
                    A("pe", lambda e, g=g, kt_=kt_: e.matmul(scp[g][:, :], lhsT=kt_[g * 64:(g + 1) * 64, :],
                                                             rhs=qT[g * 64:(g + 1) * 64, :, :],
                                                             start=True, stop=True), [ktk, "qT"], ["pA" if g == 0 else "pB"])
                    pt_ = PT[:, blk * 2 + g, :]
                    ptk = "PT/%d" % (blk * 2 + g)
                    A("act", lambda e, g=g, pt_=pt_: e.activation(out=pt_, in_=scp[g][:, :], func=AF.Exp),
                      ["pA" if g == 0 else "pB"], [ptk])
                    A("dve", lambda e, pt_=pt_, blk=blk: e.tensor_tensor(
                        out=pt_.rearrange("p (t q) -> p t q", t=4), in0=pt_.rearrange("p (t q) -> p t q", t=4),
                        in1=amask[:, blk, :].unsqueeze(1).to_broadcast([128, 4, 128]), op=ALU.mult),
                      [ptk, "amask"], [ptk])
            ov = pCD[:, :, :].rearrange("p a (b c) -> p (a b) c", b=2)
            for g in range(2):
                for t_ in range(4):
                    h = g * 4 + t_
                    A("pe", lambda e, g=g, t_=t_, h=h: e.matmul(ov[:, h, 0:65], lhsT=PT[:, g, t_ * 128:(t_ + 1) * 128],
                                                                 rhs=vcur[:, g, :], start=True, stop=first),
                      ["PT/%d" % g, vtok], ["pCD"])
                    if not first:
                        A("pe", lambda e, g=g, t_=t_, h=h: e.matmul(ov[:, h, 0:65], lhsT=PT[:, 2 + g, t_ * 128:(t_ + 1) * 128],
                                                                     rhs=vprev[:, g, :], start=False, stop=True),
                          ["PT/%d" % (2 + g), "vaug/%d/%d" % (l, 1 - par)], ["pCD"])
            A("dve", lambda e: e.tensor_tensor(out=sg[:, 0:8], in0=ov[:, :, 64], in1=esink[:, l, :], op=ALU.add),
              ["pCD", "esink"], ["sg"])
            A("dve", lambda e: e.reciprocal(out=sg[:, 0:8], in_=sg[:, 0:8]), ["sg"], ["sg"])
            A("dve", lambda e: e.tensor_tensor(out=atmp[:, :, :], in0=ov[:, :, 0:64],
                                               in1=sg[:, 0:8].unsqueeze(2).to_broadcast([128, 8, 64]), op=ALU.mult),
              ["pCD", "sg"], ["atmp"])
            A("dve", lambda e: e.tensor_tensor(out=hm[:, 256:768], in0=atmp[:, :, :].rearrange("p h d -> p (h d)"),
                                                in1=gates[:, 256:768], op=ALU.mult), ["atmp", "gates/1"], ["hm/1"])


        def chunk_tail(s, c, l, gi, between=None):
            xt = xg[:, gi, :]
            xtok = "xg/%d" % gi
            wout_apply(xt, 128, xtok, between)
            if l == 1:
                dma(yp[s, c * 128:(c + 1) * 128, :], xt, r=[xtok])

        xs_done = set()

        def rwkv_xs(s, c, l):
            first = (c == 0)
            last = (c == nchunks - 1)
            pl = prevL[:, l, :]
            ptok = "prevL/%d" % l
            if first:
                A("pool", lambda e: e.memset(prevL[0:1, l, :], 0.0), [], [ptok + "/r0"])
                A("pool", lambda e: e.memset(STs[:, l], 0.0), [], ["STs/%d" % l])
            dma(prevL[1:128, l, :], rin[0:127, :], r=["rin"], w=[ptok + "/rest"])
            if last:
                dma(shp[l, s:s + 1, :], rin[127:128, :], r=["rin"])
            A("dve", lambda e: e.tensor_tensor(out=xsb[:, :], in0=pl, in1=rin[:, :], op=ALU.subtract),
              [ptok, "rin"], ["xsb"])
            A("dve", lambda e: e.tensor_tensor(out=xsb[:, :], in0=xsb[:, :], in1=muT[:, :], op=ALU.mult),
              ["xsb", "muT"], ["xsb"])
            A("dve", lambda e: e.tensor_tensor(out=xsb[:, :], in0=xsb[:, :], in1=rin[:, :], op=ALU.add),
              ["xsb", "rin"], ["xsb"])
            k_ = xsb[:, 256:512]
            KK = ra[:, 3, :]
            TMP = ra[:, 9, :]
            tk = lambda i: "ra/%d" % i
            A("dve", lambda e: e.tensor_tensor(out=KK, in0=k_, in1=rows[:, 0, :], op=ALU.mult), ["xsb", "rows/0"], [tk(3)])
            A("dve", lambda e: e.tensor_tensor(out=TMP, in0=KK, in1=KK, op=ALU.mult), [tk(3)], [tk(9)])
            A("dve", lambda e: e.tensor_reduce(out=stat[:, 11:15], in_=TMP.rearrange("p (h d) -> p h d", d=64), axis=AX.X,
                                               op=ALU.add), [tk(9)], ["stat/3"])
            A("dve", lambda e: e.tensor_scalar(out=stat[:, 11:15], in0=stat[:, 11:15], scalar1=1e-18, scalar2=None,
                                               op0=ALU.max), ["stat/3"], ["stat/3"])
            A("act", lambda e: e.activation(out=stat[:, 11:15], in_=stat[:, 11:15], func=AF.Ln), ["stat/3"], ["stat/3"])
            A("act", lambda e: e.activation(out=stat[:, 11:15], in_=stat[:, 11:15], func=AF.Exp, scale=-0.5),
              ["stat/3"], ["stat/3"])
            A("dve", lambda e: e.tensor_tensor(out=KK.rearrange("p (h d) -> p h d", d=64),
                                               in0=KK.rearrange("p (h d) -> p h d", d=64),
                                               in1=stat[:, 11:15].unsqueeze(2).to_broadcast([128, 4, 64]), op=ALU.mult),
              [tk(3), "stat/3"], [tk(3)])
            xs_done.add((s, c, l))

        def rwkv_chunk(s, c, l, first, last):
            par = c % 2
            pl = prevL[:, l, :]
            ptok = "prevL/%d" % l
            if (s, c, l) not in xs_done:
                rwkv_xs(s, c, l)
            yield
            dma(prevL[0:1, l, :], rin[127:128, :], r=["rin", ptok], w=[ptok + "/r0"])
            r_ = xsb[:, 0:256]
            k_ = xsb[:, 256:512]
            v_ = xsb[:, 512:768]
            E1, E2, E3, KK, KM, AT, BT, KT, RT, TMP = [ra[:, i, :] for i in range(10)]
            tk = lambda i: "ra/%d" % i
            A("pe", lambda e: e.transpose(out=pA[0:32, 0:128], in_=xsb[:, 768:800], identity=ident[:, :]),
              ["xsb", "ident"], ["pA"])
            A("pe", lambda e: e.transpose(out=pA[0:32, 128:256], in_=xsb[:, 800:832], identity=ident[:, :]),
              ["xsb", "ident"], ["pA"])
            yield
            A("act", lambda e: e.activation(out=lwin[0:32, 0, :], in_=pA[0:32, 0:128], func=AF.Exp, scale=2.0),
              ["pA"], ["lwin/w"])
            A("dve", lambda e: e.tensor_scalar(out=lwin[0:32, 0, :], in0=lwin[0:32, 0, :], scalar1=1.0, scalar2=None,
                                               op0=ALU.add), ["lwin/w"], ["lwin/w"])
            A("dve", lambda e: e.reciprocal(out=lwin[0:32, 0, :], in_=lwin[0:32, 0, :]), ["lwin/w"], ["lwin/w"])
            A("dve", lambda e: e.tensor_scalar(out=lwin[0:32, 0, :], in0=lwin[0:32, 0, :], scalar1=-2.0, scalar2=1.0,
                                               op0=ALU.mult, op1=ALU.add), ["lwin/w"], ["lwin/w"])
            A("act", lambda e: e.activation(out=lwin[0:32, 1, :], in_=pA[0:32, 128:256], func=AF.Copy),
              ["pA"], ["lwin/a"])
            for j in range(2):
                A("pe", lambda e, j=j: e.matmul(pB[:, j * 256:(j + 1) * 256], lhsT=lwin[0:33, j, :], rhs=lora[0:33, l, j, :],
                                                start=True, stop=True), ["lwin", "lora"], ["pB"])
            yield
            sigmoid_act(sg[:, :], pB[:, :], ["pB"], ["sg"])
            lw = sg[:, 0:256]
            a_ = sg[:, 256:512]
            yield
            A("pe", lambda e: e.matmul(pA[:, 0:256], lhsT=tmk[:, :], rhs=lw, start=True, stop=True),
              ["tmk", "sg"], ["pA"])
            for h in range(4):
                A("pe", lambda e, h=h: e.matmul(pB[0:64, 2 * h:2 * h + 2], lhsT=sg[:, h * 64:(h + 1) * 64],
                                                rhs=coefs[:, :], start=True, stop=True), ["sg", "coefs"], ["pB"])
            A("act", lambda e: e.activation(out=edge[:, :, 0], in_=pB[0:64, 0:8].rearrange("p (h x) -> p h x", x=2)[:, :, 0],
                                            func=AF.Exp, scale=-1.0), ["pB"], ["edge/0"])
            A("act", lambda e: e.activation(out=edge[:, :, 1], in_=pB[0:64, 0:8].rearrange("p (h x) -> p h x", x=2)[:, :, 1],
                                            func=AF.Exp), ["pB"], ["edge/1"])
            A("act", lambda e: e.activation(out=E1, in_=pA[:, 0:256], func=AF.Exp), ["pA"], [tk(0)])
            A("act", lambda e: e.activation(out=E2, in_=pA[:, 0:256], func=AF.Exp, scale=-1.0), ["pA"], [tk(1)])
            A("dve", lambda e: e.scalar_tensor_tensor(out=E3, in0=lw, scalar=KAP, in1=pA[:, 0:256], op0=ALU.mult,
                                                      op1=ALU.add), ["sg", "pA"], [tk(2)])
            A("act", lambda e: e.activation(out=E3, in_=E3, func=AF.Exp), [tk(2)], [tk(2)])
            yield
            yield
            A("dve", lambda e: e.scalar_tensor_tensor(out=KM, in0=a_, scalar=-1.0, in1=rows[:, 1, :], op0=ALU.add,
                                                      op1=ALU.mult), ["sg", "rows/1"], [tk(4)])
            A("dve", lambda e: e.scalar_tensor_tensor(out=KM, in0=KM, scalar=1.0, in1=k_, op0=ALU.add, op1=ALU.mult),
              [tk(4), "xsb"], [tk(4)])
            yield
            A("dve", lambda e: e.scalar_tensor_tensor(out=AT, in0=KK, scalar=-1.0, in1=E3, op0=ALU.mult, op1=ALU.mult),
              [tk(3), tk(2)], [tk(5)])
            A("pool", lambda e: e.tensor_tensor(out=BT, in0=KK, in1=a_, op=ALU.mult), [tk(3), "sg"], [tk(6)])
            A("pool", lambda e: e.tensor_tensor(out=BT, in0=BT, in1=E2, op=ALU.mult), [tk(6), tk(1)], [tk(6)])
            A("pool", lambda e: e.tensor_tensor(out=KT, in0=KM, in1=E2, op=ALU.mult), [tk(4), tk(1)], [tk(7)])
            A("dve", lambda e: e.tensor_tensor(out=RT, in0=r_, in1=E1, op=ALU.mult), ["xsb", tk(0)], [tk(8)])
            yield
            for ai, (src, stok) in enumerate(((AT, tk(5)), (RT, tk(8)), (BT, tk(6)), (KT, tk(7)))):
                for h in range(4):
                    A("pe", lambda e, h=h, src=src: e.transpose(out=pA[0:64, h * 128:(h + 1) * 128],
                                                                 in_=src[:, h * 64:(h + 1) * 64], identity=ident[:, :]),
                      [stok, "ident"], ["pA"])
                eng = "act" if ai % 2 == 0 else "dve"
                if eng == "act":
                    A("act", lambda e, ai=ai: e.activation(out=XT[:, :, ai, :], in_=pA[0:64, :].rearrange("p (h t) -> p h t", h=4),
                                                           func=AF.Copy), ["pA"], ["XT/%d" % ai])
                else:
                    A("dve", lambda e, ai=ai: e.tensor_copy(out=XT[:, :, ai, :], in_=pA[0:64, :].rearrange("p (h t) -> p h t", h=4)),
                      ["pA"], ["XT/%d" % ai])
            yield
            R, RTm, TM_, AKA, ABR, AKR = [AM[:, i] for i in range(6)]
            for h in range(4):
                A("pe", lambda e, h=h: e.matmul(pCD[:, h, :], lhsT=XT[:, h, 2, :], rhs=XT[:, h, 0:2, :], start=True, stop=True),
                  ["XT"], ["pCD"])
            A("dve", lambda e: e.tensor_tensor(out=R, in0=pCD[:, :, 0:128], in1=msi[:, 0:128].unsqueeze(1).to_broadcast([128, 4, 128]),
                                               op=ALU.mult), ["pCD", "msi"], ["AM/0"])
            A("dve", lambda e: e.tensor_tensor(out=ABR, in0=pCD[:, :, 128:256], in1=msi[:, 128:256].unsqueeze(1).to_broadcast([128, 4, 128]),
                                               op=ALU.mult), ["pCD", "msi"], ["AM/4"])
            for h in range(4):
                A("pe", lambda e, h=h: e.matmul(pCD[:, h, :], lhsT=XT[:, h, 3, :], rhs=XT[:, h, 0:2, :], start=True, stop=True),
                  ["XT"], ["pCD"])
            A("dve", lambda e: e.tensor_tensor(out=AKA, in0=pCD[:, :, 0:128], in1=msi[:, 0:128].unsqueeze(1).to_broadcast([128, 4, 128]),
                                               op=ALU.mult), ["pCD", "msi"], ["AM/3"])
            A("dve", lambda e: e.tensor_tensor(out=AKR, in0=pCD[:, :, 128:256], in1=msi[:, 128:256].unsqueeze(1).to_broadcast([128, 4, 128]),
                                               op=ALU.mult), ["pCD", "msi"], ["AM/5"])
            for h in range(4):
                A("pe", lambda e, h=h: e.matmul(pE[:, h, :], lhsT=XT[:, h, 0, :], rhs=XT[:, h, 2, :], start=True, stop=True),
                  ["XT"], ["pE"])
            A("dve", lambda e: e.tensor_tensor(out=RTm, in0=pE[:, :, :], in1=mst[:, :].unsqueeze(1).to_broadcast([128, 4, 128]),
                                               op=ALU.mult), ["pE", "mst"], ["AM/1"])
            A("pool", lambda e: e.tensor_tensor(out=TM_, in0=R, in1=ident[:, :].unsqueeze(1).to_broadcast([128, 4, 128]),
                                                op=ALU.add), ["AM/0", "ident"], ["AM/2"])
            yield
            for j in range(1, 8):
                yield
                for p in range(2):
                    hs = (2 * p, 2 * p + 1)
                    a0, a1, a2 = "AM/0/%d" % p, "AM/1/%d" % p, "AM/2/%d" % p
                    pbk = (pA, pB)[p]
                    pbt = "pA" if p == 0 else "pB"
                    pbv_ = pbk[:, 0:256].rearrange("p (h t) -> p h t", h=2)
                    for h in hs:
                        if j == 1:
                            A("pe", lambda e, h=h: e.matmul(pCD[:, h, 0:128], lhsT=RTm[:, h, :], rhs=R[:, h, :], start=True, stop=True),
                              [a1, a0], ["pCD/%d" % p])
                        else:
                            A("pe", lambda e, h=h: e.matmul(pCD[:, h, :], lhsT=RTm[:, h, :], rhs=AM[:, 0:3:2, h, :], start=True, stop=True),
                              [a1, a0, a2], ["pCD/%d" % p])
                    if j < 7:
                        for h in hs:
                            A("pe", lambda e, h=h, pbk=pbk: e.matmul(pbk[:, (h % 2) * 128:(h % 2 + 1) * 128], lhsT=R[:, h, :], rhs=RTm[:, h, :],
                                                                    start=True, stop=True), [a1, a0], [pbt])
                    if j > 1:
                        A("dve", lambda e, p=p: e.tensor_tensor(out=TM_[:, 2 * p:2 * p + 2, :], in0=pCD[:, 2 * p:2 * p + 2, 128:256],
                                                                in1=TM_[:, 2 * p:2 * p + 2, :], op=ALU.add), ["pCD/%d" % p, a2], [a2])
                    if j < 7:
                        A("act", lambda e, p=p: e.activation(out=R[:, 2 * p:2 * p + 2, :], in_=pCD[:, 2 * p:2 * p + 2, 0:128], func=AF.Copy),
                          ["pCD/%d" % p], [a0])
                        A("dve", lambda e, p=p, pbv_=pbv_: e.tensor_copy(out=RTm[:, 2 * p:2 * p + 2, :], in_=pbv_), [pbt], [a1])
            yield
            stl = STs[:, l]
            sttok = "STs/%d" % l
            A("dve", lambda e: e.tensor_tensor(out=Stt[:, :, :], in0=stl, in1=edge[:, :, 0:1].to_broadcast([64, 4, 64]), op=ALU.mult),
              [sttok, "edge/0"], ["Stt"])
            gv = pA[:, 0:256].rearrange("p (h v) -> p h v", h=4)
            for h in range(4):
                A("pe", lambda e, h=h: e.matmul(gv[:, h, :], lhsT=XT[:, h, 0, :], rhs=Stt[:, h, :], start=True, stop=False),
                  ["XT/0", "Stt"], ["pA"])
                A("pe", lambda e, h=h: e.matmul(gv[:, h, :], lhsT=AKA[:, h, :], rhs=v_[:, h * 64:(h + 1) * 64], start=False, stop=True),
                  ["AM/3", "xsb"], ["pA"])
            A("act", lambda e: e.activation(out=Gs[:, :, :], in_=gv, func=AF.Copy), ["pA"], ["Gs"])
            uv = pB[:, 0:256].rearrange("p (h v) -> p h v", h=4)
            for h in range(4):
                A("pe", lambda e, h=h: e.matmul(uv[:, h, :], lhsT=TM_[:, h, :], rhs=Gs[:, h, :], start=True, stop=True),
                  ["AM/2", "Gs"], ["pB"])
            A("act", lambda e: e.activation(out=Us[:, :, :], in_=uv, func=AF.Copy), ["pB"], ["Us"])
            ovr = pA[:, 256:512].rearrange("p (h v) -> p h v", h=4)
            for h in range(4):
                A("pe", lambda e, h=h: e.matmul(ovr[:, h, :], lhsT=XT[:, h, 1, :], rhs=Stt[:, h, :], start=True, stop=False),
                  ["XT/1", "Stt"], ["pA/o"])
                A("pe", lambda e, h=h: e.matmul(ovr[:, h, :], lhsT=ABR[:, h, :], rhs=Us[:, h, :], start=False, stop=False),
                  ["AM/4", "Us"], ["pA/o"])
                A("pe", lambda e, h=h: e.matmul(ovr[:, h, :], lhsT=AKR[:, h, :], rhs=v_[:, h * 64:(h + 1) * 64], start=False, stop=True),
                  ["AM/5", "xsb"], ["pA/o"])
            scv = pB[0:64, 256:512].rearrange("p (h v) -> p h v", h=4)
            for h in range(4):
                A("pe", lambda e, h=h: e.matmul(scv[:, h, :], lhsT=BT[:, h * 64:(h + 1) * 64], rhs=Us[:, h, :], start=True, stop=False),
                  [tk(6), "Us"], ["pB/s"])
                A("pe", lambda e, h=h: e.matmul(scv[:, h, :], lhsT=KT[:, h * 64:(h + 1) * 64], rhs=v_[:, h * 64:(h + 1) * 64], start=False, stop=True),
                  [tk(7), "xsb"], ["pB/s"])
            A("dve", lambda e: e.tensor_tensor(out=Stt[:, :, :], in0=scv, in1=Stt[:, :, :], op=ALU.add), ["pB/s", "Stt"], ["Stt"])
            A("dve", lambda e: e.tensor_tensor(out=stl, in0=Stt[:, :, :], in1=edge[:, :, 1:2].to_broadcast([64, 4, 64]), op=ALU.mult),
              ["Stt", "edge/1"], [sttok])
            yield
            if last:
                for h in range(4):
                    A("pe", lambda e, h=h: e.transpose(out=pE[0:64, h, 0:64], in_=STs[:, l, h, :], identity=ident[0:64, 0:64]),
                      [sttok, "ident"], ["pE"])
                A("act", lambda e: e.activation(out=Gs[0:64, :, :], in_=pE[0:64, :, 0:64], func=AF.Copy), ["pE"], ["Gs"])
                dma(wkp[l, s].rearrange("h v k -> v h k"), Gs[0:64, :, :], r=["Gs"])
            yield
            rwkv_post(128, pA[:, 256:512], "pA/o", r_, KM, v_, ["xsb"], [tk(4)], gates2[:, par, :], "gates2/%d" % par)

        def rwkv_post(np_, ops, opstok, r_, KM, v_, xtoks, kmtoks, g2ap, g2tok):
            tk = lambda i: "ra/%d" % i
            O = ra[0:np_, 0, :]
            E2 = ra[0:np_, 1, :]
            st_ = lambda a_, b_: stat[0:np_, a_:b_]
            bc = lambda ap_: ap_.unsqueeze(2).to_broadcast([np_, 4, 64])
            A("act", lambda e: e.activation(out=O, in_=ops, func=AF.Copy), [opstok], [tk(0)])
            O3 = O.rearrange("p (h d) -> p h d", d=64)
            A("dve", lambda e: e.tensor_reduce(out=st_(16, 20), in_=O3, axis=AX.X, op=ALU.add), [tk(0)], ["stat/gm"])
            A("pool", lambda e: e.tensor_tensor(out=E2, in0=O, in1=O, op=ALU.mult), [tk(0)], [tk(1)])
            A("dve", lambda e: e.tensor_reduce(out=st_(20, 24), in_=E2.rearrange("p (h d) -> p h d", d=64), axis=AX.X, op=ALU.add),
              [tk(1)], ["stat/gv"])
            A("dve", lambda e: e.tensor_scalar(out=st_(16, 20), in0=st_(16, 20), scalar1=1.0 / 64, scalar2=None, op0=ALU.mult),
              ["stat/gm"], ["stat/gm"])
            A("dve", lambda e: e.tensor_tensor(out=st_(24, 28), in0=st_(16, 20), in1=st_(16, 20), op=ALU.mult),
              ["stat/gm"], ["stat/gt"])
            A("dve", lambda e: e.scalar_tensor_tensor(out=st_(20, 24), in0=st_(20, 24), scalar=1.0 / 64, in1=st_(24, 28),
                                                      op0=ALU.mult, op1=ALU.subtract), ["stat/gv", "stat/gt"], ["stat/gv"])
            rsqrt_small(st_(20, 24), st_(20, 24), 4, 1.0, GN_EPS, ["stat/gv"], ["stat/gv"])
            A("dve", lambda e: e.tensor_tensor(out=O3, in0=O3, in1=bc(st_(16, 20)), op=ALU.subtract), [tk(0), "stat/gm"], [tk(0)])
            A("dve", lambda e: e.tensor_tensor(out=O3, in0=O3, in1=bc(st_(20, 24)), op=ALU.mult), [tk(0), "stat/gv"], [tk(0)])
            A("dve", lambda e: e.tensor_tensor(out=O, in0=O, in1=rows[0:np_, 3, :], op=ALU.mult), [tk(0), "rows/3"], [tk(0)])
            A("dve", lambda e: e.tensor_tensor(out=O, in0=O, in1=rows[0:np_, 4, :], op=ALU.add), [tk(0), "rows/4"], [tk(0)])
            A("pool", lambda e: e.tensor_tensor(out=E2, in0=r_, in1=KM, op=ALU.mult), xtoks + kmtoks, [tk(1)])
            A("pool", lambda e: e.tensor_tensor(out=E2, in0=E2, in1=rows[0:np_, 2, :], op=ALU.mult), [tk(1), "rows/2"], [tk(1)])
            A("dve", lambda e: e.tensor_reduce(out=st_(24, 28), in_=E2.rearrange("p (h d) -> p h d", d=64), axis=AX.X, op=ALU.add),
              [tk(1)], ["stat/gt"])
            A("dve", lambda e: e.tensor_tensor(out=E2.rearrange("p (h d) -> p h d", d=64), in0=v_.rearrange("p (h d) -> p h d", d=64),
                                               in1=bc(st_(24, 28)), op=ALU.mult), xtoks + ["stat/gt"], [tk(1)])
            A("dve", lambda e: e.tensor_tensor(out=O, in0=O, in1=E2, op=ALU.add), [tk(0), tk(1)], [tk(0)])
            A("dve", lambda e: e.tensor_tensor(out=hm[0:np_, 768:1024], in0=O, in1=g2ap, op=ALU.mult),
              [tk(0), g2tok], ["hm/2"])

        xsmp = T("xsmp", [SB, D])
        smc = T("smc", [128, 64])
        indBT = T("indBT", [SB, 128])
        ropes = T("ropes", [128, 2, 32])
        esinkP = T("esinkP", [128, 2, 2])
        dma(xgf[:, 3584:3648], C["smc"][:, :], w=["xg"])
        dma(xgf[0:SB, 3712:3840], C["indBT"][:, :], w=["xg"])
        dma(ropes[:, :, :], C["ropes"][:, :, :], w=["ropes"])
        for l in range(2):
            for dd in range(2):
                for g in range(2):
                    dma(esinkP[g * 64:(g + 1) * 64, l, dd:dd + 1], P["attn_sinks"][l, g * 4:(g + 1) * 4].partition_broadcast(16), w=["esinkP"])
        A("act", lambda e: e.activation(out=esinkP[:, :, :], in_=esinkP[:, :, :], func=AF.Exp), ["esinkP"], ["esinkP"])
        A("act", lambda e: e.activation(out=smc[:, :], in_=xgf[:, 3584:3648], func=AF.Copy), ["xg"], ["smc"])
        A("act", lambda e: e.activation(out=indBT[:, :], in_=xgf[0:SB, 3712:3840], func=AF.Copy), ["xg"], ["indBT"])
        hmask8 = smc[:, 0:8]
        indB = smc[:, 8:24]
        gsel = smc[:, 24:26]
        hm4 = smc[:, 26:30]
        vhm = smc[:, 30:32]
        hmask8r = smc[:, 32:40]
        indBr = smc[:, 40:56]

        def sample_layer(l):
            NP = SB
            xt = xsmp[0:NP, :]
            xtok = "xsmp"
            tk = lambda i: "ra/%d" % i
            if l == 0:
                dma(xt, xs_in[:, :], w=[xtok])
            rmsnorm_T(xt, NP, xtok)
            cp = lambda o, i, r, w: A("act", lambda e: e.activation(out=o, in_=i, func=AF.Copy), r, w)
            hd = lambda ap_: ap_.rearrange("p (h d) -> p h d", d=64)
            us = Gs[0:NP, :, :].rearrange("p h d -> p (h d)")
            win_group(pz[0], "pz0", OU, 512, NP)
            cp(us, pz[0][0:NP, 0:256], ["pz0"], ["Gs"])
            silu_to(gates[0:NP, 0:256], pz[0][0:NP, 256:512], 256, ra[0:NP, 9, :], ["pz0"], ["gates/0"], ["ra/9"])
            win_group(pz[1], "pz1", OQ, 512, NP, perm=True)
            cp(qk[0:NP, 0:8, :], hd(pz[1][0:NP, :]), ["pz1"], ["qk"])
            win_group(pz[0], "pz0", OKK, 256, NP)
            cp(qk[0:NP, 8:10, :], hd(pz[0][0:NP, 0:128]), ["pz0"], ["qk"])
            cp(atmp[0:NP, 0:2, :], hd(pz[0][0:NP, 128:256]), ["pz0"], ["atmp"])
            win_group(pz[1], "pz1", OGA, 512, NP)
            silu_to(gates[0:NP, 256:768], pz[1][0:NP, :], 512, sg[0:NP, :], ["pz1"], ["gates/1"], ["sg"])
            win_group(pz[0], "pz0", ORIN, 512, NP)
            cp(rin[0:NP, 0:512], pz[0][0:NP, :], ["pz0"], ["rin/a"])
            win_group(pz[1], "pz1", ORIN + 512, 320, NP)
            A("dve", lambda e: e.tensor_copy(out=rin[0:NP, 512:832], in_=pz[1][0:NP, 0:320]), ["pz1"], ["rin/b"])
            win_group(pz[0], "pz0", OGR, 256, NP)
            silu_to(gates[0:NP, 768:1024], pz[0][0:NP, 0:256], 256, ra[0:NP, 9, :], ["pz0"], ["gates/2"], ["ra/9"])
            if sstage <= 1:
                return
            AMf = xgf[:, 2048:4096]
            pooled = sg[0:NP, 0:256]
            off = 0
            for g, w in enumerate((2, 4, 8, 16)):
                n = (w - 1) * 64
                bufv = AMf[0:NP, off:off + n].rearrange("p (r c) -> p r c", c=64)
                dma(bufv, spool[l, :, 15 - (w - 1):15, g * 64:(g + 1) * 64], w=["xg"])
                A("dve", lambda e, g=g, bufv=bufv: e.tensor_reduce(out=pooled[:, g * 64:(g + 1) * 64],
                                                                   in_=bufv.rearrange("p r c -> p c r"), axis=AX.X, op=ALU.add),
                  ["xg"], ["sg"])
                A("dve", lambda e, g=g, w=w: e.tensor_scalar(out=pooled[:, g * 64:(g + 1) * 64], in0=pooled[:, g * 64:(g + 1) * 64],
                                                             scalar1=1.0 / w, scalar2=None, op0=ALU.mult), ["sg"], ["sg"])
                A("dve", lambda e, g=g, w=w: e.scalar_tensor_tensor(out=pooled[:, g * 64:(g + 1) * 64], in0=us[:, g * 64:(g + 1) * 64],
                                                                    scalar=(1.0 / w - 1.0), in1=pooled[:, g * 64:(g + 1) * 64],
                                                                    op0=ALU.mult, op1=ALU.add), ["sg", "Gs"], ["sg"])
                off += n
            dma(pools[l, :, 0:14, :], spool[l, :, 1:15, :])
            dma(pools[l, :, 14, :], us, r=["Gs"])
            for g in range(4):
                A("pe", lambda e, g=g: e.transpose(out=pA[0:64, g * 128:g * 128 + NP], in_=pooled[:, g * 64:(g + 1) * 64],
                                                   identity=ident[0:NP, 0:NP]), ["sg", "ident"], ["pA"])
            cp(pTsb[:, :].rearrange("p (g t) -> p g t", g=4)[:, :, 0:NP], pA[0:64, :].rearrange("p (g t) -> p g t", g=4)[:, :, 0:NP],
               ["pA"], ["pTsb"])
            for g in range(4):
                A("pe", lambda e, g=g: e.matmul(pB[0:NP, g * 64:(g + 1) * 64], lhsT=pTsb[:, g * 128:g * 128 + NP],
                                                rhs=poolw[:, l, g, :], start=True, stop=True), ["pTsb", "poolw"], ["pB"])
            A("dve", lambda e: e.tensor_tensor(out=hm[0:NP, 0:256], in0=pB[0:NP, 0:256], in1=gates[0:NP, 0:256], op=ALU.mult),
              ["pB", "gates/0"], ["hm/0"])
            if sstage <= 2:
                return
            qk_norm_rope(NP, l, ropes[0:NP, 0, :], ropes[0:NP, 1, :], "ropes")
            Kc = xgf[:, 0:2048].rearrange("p (b c) -> p b c", c=128)
            Vst = xgf[:, 2048:4096].rearrange("p (b c) -> p b c", c=128)
            Vc = ra[:, 0:8, :].rearrange("p a c -> p (a c)").rearrange("p (b c) -> p b c", c=128)
            dma(Kc[0:127, :, :], ck[l].rearrange("b j c -> j b c")[1:128], w=["xg"])
            dma(Kc[127:128, :, :], qk[0:NP, 8:10, :], r=["qk"], w=["xg"])
            dma(ksm[l].rearrange("b j c -> j b c"), Kc, r=["xg"])
            dma(Vst[0:127, :, :], cv[l].rearrange("b j c -> j b c")[1:128], w=["xg"])
            dma(Vst[127:128, :, :], atmp[0:NP, 0:2, :], r=["atmp"], w=["xg"])
            dma(vsm[l].rearrange("b j c -> j b c"), Vst, r=["xg"])
            A("act", lambda e: e.activation(out=Vc[:, 0:8, :], in_=Vst[:, 0:8, :], func=AF.Copy), ["xg"], ["ra"])
            A("dve", lambda e: e.tensor_copy(out=Vc[:, 8:16, :], in_=Vst[:, 8:16, :]), ["xg"], ["ra"])
            if sstage <= 2.2:
                return
            KTv = PT[:, :, :].rearrange("p a (b t) -> p (a b) t", t=128)
            pcdv = pCD[:, :, :].rearrange("p a (b c) -> p (a b) c", b=2)
            pbv = pB[:, :].rearrange("p (a c) -> p a c", a=4)
            for b in range(SB):
                dst = pcdv[:, b, :] if b < 8 else (pE[:, b - 8, :] if b < 12 else pbv[:, b - 12, :])
                dtok = "pCD" if b < 8 else ("pE" if b < 12 else "pB")
                A("pe", lambda e, b=b, dst=dst: e.transpose(out=dst, in_=Kc[:, b, :], identity=ident[:, :]), ["xg", "ident"], [dtok])
            cp(KTv[:, 0:4, :], pcdv[:, 0:4, :], ["pCD"], ["PT"])
            cp(KTv[:, 4:8, :], pcdv[:, 4:8, :], ["pCD"], ["PT"])
            A("dve", lambda e: e.tensor_copy(out=KTv[:, 8:12, :], in_=pE[:, :, :]), ["pE"], ["PT"])
            cp(KTv[:, 12:16, :], pbv, ["pB"], ["PT"])
            if sstage <= 2.4:
                return
            for t_ in range(4):
                A("pe", lambda e, t_=t_: e.transpose(out=pA[:, t_ * 128:t_ * 128 + NP],
                                                     in_=qk[0:NP, 2 * t_:2 * t_ + 2, :].rearrange("p h d -> p (h d)"),
                                                     identity=ident[0:NP, 0:NP]), ["qk", "ident"], ["pA"])
            cp(qT[:, :, 0:NP], pA[:, :].rearrange("p (t q) -> p t q", t=4)[:, :, 0:NP], ["pA"], ["qT"])
            for b in range(SB):
                for g in range(2):
                    A("pe", lambda e, b=b, g=g: e.matmul((pA, pB)[g][:, b * 4:b * 4 + 4], lhsT=KTv[g * 64:(g + 1) * 64, b, :],
                                                         rhs=qT[g * 64:(g + 1) * 64, :, b], start=True, stop=True),
                      ["PT", "qT"], ["pA" if g == 0 else "pB"])
            PTs = qk2[:, :, :].rearrange("p h d -> p (h d)")[:, 0:128]
            A("act", lambda e: e.activation(out=PTs[:, 0:64], in_=pA[:, 0:64], func=AF.Exp), ["pA"], ["qk2"])
            A("act", lambda e: e.activation(out=PTs[:, 64:128], in_=pB[:, 0:64], func=AF.Exp), ["pB"], ["qk2"])
            if sstage <= 2.6:
                return
            sgv = sg[:, :].rearrange("p (i c) -> p i c", i=4)
            ones2 = Us[:, 2, 2:4]
            A("pool", lambda e: e.memset(ones2, 1.0), [], ["Us/2"])
            A("pe", lambda e: e.matmul(pz[1][:, 128:130], lhsT=PTs, rhs=ones2, start=True, stop=True), ["qk2", "Us/2"], ["pz1"])
            for r_ in range(4):
                A("pool", lambda e: e.memset(sg[:, :], 0.0), [], ["sg"])
                base = sg[:, 16 * r_:16 * r_ + 4]
                diag = bass.AP(base.tensor, base.offset, [list(base.ap[0]), [132, 4], [64, 2], [1, 4]])
                pb_ = PTs[:, 16 * r_:16 * r_ + 4]
                src = bass.AP(pb_.tensor, pb_.offset, [list(pb_.ap[0]), [4, 4], [64, 2], [1, 4]])
                A("dve", lambda e, diag=diag, src=src: e.tensor_copy(out=diag, in_=src), ["qk2"], ["sg"])
                for i in range(4):
                    b = 4 * r_ + i
                    if sstage <= 2.7:
                        continue
                    A("pe", lambda e, b=b, i=i: e.matmul(pz[1][:, 0:128], lhsT=sgv[:, i, :], rhs=Vc[:, b, :], start=(b == 0), stop=(b == SB - 1)),
                      ["sg", "ra"], ["pz1"])
            den2 = Us[:, 2, 0:2]
            den = Us[:, 2, 0:1]
            if sstage <= 2.8:
                return
            t64 = Us[:, 3, :]
            A("dve", lambda e: e.tensor_tensor(out=den2, in0=pz[1][:, 128:130], in1=esinkP[:, l, :], op=ALU.add), ["pz1", "esinkP"], ["Us/2"])
            A("dve", lambda e: e.reciprocal(out=den2, in_=den2), ["Us/2"], ["Us/2"])
            if sstage <= 2.85:
                return
            A("dve", lambda e: e.tensor_scalar(out=t64, in0=pz[1][:, 0:64], scalar1=gsel[:, 0:1], scalar2=None, op0=ALU.mult),
              ["pz1", "smc"], ["Us/3"])
            A("dve", lambda e: e.scalar_tensor_tensor(out=t64, in0=pz[1][:, 64:128], scalar=gsel[:, 1:2], in1=t64, op0=ALU.mult, op1=ALU.add),
              ["pz1", "smc", "Us/3"], ["Us/3"])
            A("dve", lambda e: e.tensor_scalar(out=t64, in0=t64, scalar1=den, scalar2=None, op0=ALU.mult), ["Us/3", "Us/2"], ["Us/3"])
            if sstage <= 2.9:
                return
            for h8 in range(8):
                A("dve", lambda e, h8=h8: e.tensor_scalar(out=atmp[:, h8, :], in0=t64, scalar1=hmask8[:, h8:h8 + 1], scalar2=None, op0=ALU.mult),
                  ["Us/3", "smc"], ["atmp"])
            if sstage <= 2.95:
                return
            A("pe", lambda e: e.matmul(pz[0][0:NP, :], lhsT=indB, rhs=atmp[:, :, :].rearrange("p h d -> p (h d)"), start=True, stop=True),
              ["smc", "atmp"], ["pz0"])
            A("dve", lambda e: e.tensor_tensor(out=hm[0:NP, 256:768], in0=pz[0][0:NP, :], in1=gates[0:NP, 256:768], op=ALU.mult),
              ["pz0", "gates/1"], ["hm/1"])
            if sstage <= 3:
                return
            dma(xgf[0:NP, 2048:2880], sshift[l], w=["xg"])
            dma(shs[l], rin[0:NP, :], r=["rin"])
            A("pool", lambda e: e.tensor_tensor(out=xsb[0:NP, :], in0=xgf[0:NP, 2048:2880], in1=rin[0:NP, :], op=ALU.subtract), ["xg", "rin"], ["xsb"])
            A("pool", lambda e: e.tensor_tensor(out=xsb[0:NP, :], in0=xsb[0:NP, :], in1=muT[0:NP, :], op=ALU.mult), ["xsb", "muT"], ["xsb"])
            A("dve", lambda e: e.tensor_tensor(out=xsb[0:NP, :], in0=xsb[0:NP, :], in1=rin[0:NP, :], op=ALU.add), ["xsb", "rin"], ["xsb"])
            r_s = xsb[0:NP, 0:256]
            k_s = xsb[0:NP, 256:512]
            v_s = xsb[0:NP, 512:768]
            A("pe", lambda e: e.transpose(out=pA[0:32, 0:NP], in_=xsb[0:NP, 768:800], identity=ident[0:NP, 0:NP]), ["xsb", "ident"], ["pA"])
            A("pe", lambda e: e.transpose(out=pA[0:32, 128:128 + NP], in_=xsb[0:NP, 800:832], identity=ident[0:NP, 0:NP]), ["xsb", "ident"], ["pA"])
            lw0 = lwin[0:32, 0, 0:NP]
            A("act", lambda e: e.activation(out=lw0, in_=pA[0:32, 0:NP], func=AF.Exp, scale=2.0), ["pA"], ["lwin/w"])
            A("dve", lambda e: e.tensor_scalar(out=lw0, in0=lw0, scalar1=1.0, scalar2=None, op0=ALU.add), ["lwin/w"], ["lwin/w"])
            A("dve", lambda e: e.reciprocal(out=lw0, in_=lw0), ["lwin/w"], ["lwin/w"])
            A("dve", lambda e: e.tensor_scalar(out=lw0, in0=lw0, scalar1=-2.0, scalar2=1.0, op0=ALU.mult, op1=ALU.add), ["lwin/w"], ["lwin/w"])
            cp(lwin[0:32, 1, 0:NP], pA[0:32, 128:128 + NP], ["pA"], ["lwin/a"])
            for j in range(2):
                A("pe", lambda e, j=j: e.matmul(pB[0:NP, j * 256:(j + 1) * 256], lhsT=lwin[0:33, j, 0:NP], rhs=lora[0:33, l, j, :],
                                                start=True, stop=True), ["lwin", "lora"], ["pB"])
            sgs = sg[0:NP, :]
            sigmoid_act(sgs, pB[0:NP, :], ["pB"], ["sg"])
            lw = sg[0:NP, 0:256]
            a_ = sg[0:NP, 256:512]
            vec6 = AM[0:NP, 0:3, :, :].rearrange("p a h t -> p (a h t)").rearrange("p (q c) -> p q c", c=256)
            VKK, VW, VKA, VKM, VR, VV = [vec6[:, i, :] for i in range(6)]
            A("act", lambda e: e.activation(out=VW, in_=lw, func=AF.Exp, scale=-KAP), ["sg"], ["AM/0"])
            A("pool", lambda e: e.tensor_tensor(out=VKK, in0=k_s, in1=rows[0:NP, 0, :], op=ALU.mult), ["xsb", "rows/0"], ["AM/0"])
            TMPs = qk3[0:NP, :, :].rearrange("p h d -> p (h d)")[:, 0:256]
            A("pool", lambda e: e.tensor_tensor(out=TMPs, in0=VKK, in1=VKK, op=ALU.mult), ["AM/0"], ["qk3"])
            skk = stat[0:NP, 11:15]
            A("dve", lambda e: e.tensor_reduce(out=skk, in_=hd(TMPs), axis=AX.X, op=ALU.add), ["qk3"], ["stat/3"])
            A("dve", lambda e: e.tensor_scalar(out=skk, in0=skk, scalar1=1e-18, scalar2=None, op0=ALU.max), ["stat/3"], ["stat/3"])
            A("act", lambda e: e.activation(out=skk, in_=skk, func=AF.Ln), ["stat/3"], ["stat/3"])
            A("act", lambda e: e.activation(out=skk, in_=skk, func=AF.Exp, scale=-0.5), ["stat/3"], ["stat/3"])
            A("dve", lambda e: e.tensor_tensor(out=hd(VKK), in0=hd(VKK), in1=skk.unsqueeze(2).to_broadcast([NP, 4, 64]), op=ALU.mult),
              ["AM/0", "stat/3"], ["AM/0"])
            A("pool", lambda e: e.tensor_tensor(out=VKA, in0=VKK, in1=a_, op=ALU.mult), ["AM/0", "sg"], ["AM/1"])
            A("dve", lambda e: e.scalar_tensor_tensor(out=VKM, in0=a_, scalar=-1.0, in1=rows[0:NP, 1, :], op0=ALU.add, op1=ALU.mult),
              ["sg", "rows/1"], ["AM/1"])
            A("dve", lambda e: e.scalar_tensor_tensor(out=VKM, in0=VKM, scalar=1.0, in1=k_s, op0=ALU.add, op1=ALU.mult),
              ["AM/1", "xsb"], ["AM/1"])
            KMs = qk3[0:NP, :, :].rearrange("p h d -> p (h d)")[:, 256:512]
            A("pool", lambda e: e.tensor_copy(out=KMs, in_=VKM), ["AM/1"], ["qk3"])
            A("pool", lambda e: e.tensor_copy(out=VR, in_=r_s), ["xsb"], ["AM/2"])
            A("pool", lambda e: e.tensor_copy(out=VV, in_=v_s), ["xsb"], ["AM/2"])
            v6f = vec6.rearrange("p q c -> p (q c)")
            pcdf = pCD[:, :, :].rearrange("p a c -> p (a c)")
            for i in range(3):
                dst = pcdf[:, i * 512:(i + 1) * 512] if i < 2 else pE[:, :, :].rearrange("p a c -> p (a c)")
                A("pe", lambda e, i=i, dst=dst: e.matmul(dst, lhsT=indBT[:, :], rhs=v6f[:, i * 512:(i + 1) * 512], start=True, stop=True),
                  ["indBT", "AM/0", "AM/1", "AM/2"], ["pCD" if i < 2 else "pE"])
            seltmp = AM[:, 3:6, :, :].rearrange("p a h t -> p (a h t)").rearrange("p (q h c) -> p q h c", h=4, c=64)
            hb = lambda nq: hm4.unsqueeze(1).unsqueeze(3).to_broadcast([128, nq, 4, 64])
            for bk in range(2):
                A("dve", lambda e, bk=bk: e.tensor_tensor(out=seltmp[:, 2 * bk:2 * bk + 2],
                                                          in0=pcdf[:, bk * 512:(bk + 1) * 512].rearrange("p (q h c) -> p q h c", h=4, c=64),
                                                          in1=hb(2), op=ALU.mult), ["pCD", "smc"], ["AM/%d" % (3 + bk)])
            A("dve", lambda e: e.tensor_tensor(out=seltmp[:, 4:6], in0=pE[:, :, :].rearrange("p a c -> p (a c)").rearrange("p (q h c) -> p q h c", h=4, c=64),
                                               in1=hb(2), op=ALU.mult), ["pE", "smc"], ["AM/5"])
            vsel = qk[:, 0:6, :]
            A("dve", lambda e: e.tensor_reduce(out=vsel, in_=seltmp.rearrange("p q h c -> p q c h"), axis=AX.X, op=ALU.add),
              ["AM/3", "AM/4", "AM/5"], ["qk"])
            v32 = Us[:, 2, 0:32]
            A("dve", lambda e: e.tensor_scalar(out=v32, in0=vsel[:, 5, 0:32], scalar1=vhm[:, 0:1], scalar2=None, op0=ALU.mult), ["qk", "smc"], ["Us/2"])
            A("dve", lambda e: e.scalar_tensor_tensor(out=v32, in0=vsel[:, 5, 32:64], scalar=vhm[:, 1:2], in1=v32, op0=ALU.mult, op1=ALU.add),
              ["qk", "smc", "Us/2"], ["Us/2"])
            Sst = xgf[:, 0:2048].rearrange("p (v k) -> p v k", k=64)
            tmpS = AM[:, 0:4, :, :].rearrange("p a h t -> p (a h t)").rearrange("p (v k) -> p v k", k=64)
            stok = ["xg"]
            ttok = ["AM/0", "AM/1", "AM/2", "AM/3"]
            dma(Sst, swkv[l].rearrange("b h (vh v) k -> (b h vh) v k", vh=2), w=stok)
            bv = lambda q: vsel[:, q, :].unsqueeze(1).to_broadcast([128, 32, 64])
            sa = Us[:, 0, 0:32]
            osm = Us[:, 1, 0:32]
            A("dve", lambda e: e.tensor_tensor(out=tmpS, in0=Sst, in1=bv(0), op=ALU.mult), stok + ["qk"], ttok)
            A("dve", lambda e: e.tensor_reduce(out=sa, in_=tmpS, axis=AX.X, op=ALU.add), ttok, ["Us/0"])
            A("pool", lambda e: e.tensor_tensor(out=Sst, in0=Sst, in1=bv(1), op=ALU.mult), stok + ["qk"], stok)
            A("dve", lambda e: e.scalar_tensor_tensor(out=tmpS, in0=tmpS, scalar=0.0, in1=bv(2), op0=ALU.mult, op1=ALU.add), ttok + ["qk"], ttok)
            A("dve", lambda e: e.tensor_tensor(out=tmpS, in0=tmpS, in1=sa.unsqueeze(2).to_broadcast([128, 32, 64]), op=ALU.mult),
              ttok + ["Us/0"], ttok)
            A("pool", lambda e: e.tensor_tensor(out=Sst, in0=Sst, in1=tmpS, op=ALU.subtract), stok + ttok, stok)
            A("dve", lambda e: e.scalar_tensor_tensor(out=tmpS, in0=tmpS, scalar=0.0, in1=bv(3), op0=ALU.mult, op1=ALU.add), ttok + ["qk"], ttok)
            A("dve", lambda e: e.tensor_tensor(out=tmpS, in0=tmpS, in1=v32.unsqueeze(2).to_broadcast([128, 32, 64]), op=ALU.mult),
              ttok + ["Us/2"], ttok)
            A("pool", lambda e: e.tensor_tensor(out=Sst, in0=Sst, in1=tmpS, op=ALU.add), stok + ttok, stok)
            dma(wks[l].rearrange("b h (vh v) k -> (b h vh) v k", vh=2), Sst, r=stok)
            A("dve", lambda e: e.tensor_tensor(out=tmpS, in0=Sst, in1=bv(4), op=ALU.mult), stok + ["qk"], ttok)
            A("dve", lambda e: e.tensor_reduce(out=osm, in_=tmpS, axis=AX.X, op=ALU.add), ttok, ["Us/1"])
            for h8 in range(8):
                A("dve", lambda e, h8=h8: e.tensor_scalar(out=atmp[:, h8, 0:32], in0=osm, scalar1=hmask8r[:, h8:h8 + 1], scalar2=None, op0=ALU.mult),
                  ["Us/1", "smc"], ["atmp"])
            A("pe", lambda e: e.matmul(pz[1][0:NP, 0:256], lhsT=indBr, rhs=atmp[:, :, 0:32], start=True, stop=True), ["smc", "atmp"], ["pz1"])
            rwkv_post(NP, pz[1][0:NP, 0:256], "pz1", r_s, KMs, v_s, ["xsb"], ["qk3"], gates[0:NP, 768:1024], "gates/2")
            wout_apply(xt, NP, xtok)
            if l == 1:
                dma(ys[:, :], xt, r=[xtok])

        load_layer(0, True)
        sample_done = [False, False]
        if do_sample:
            sample_layer(0)
            sample_done[0] = True
        def run_rr(gens):
            gens = [g for g in gens if g is not None]
            while gens:
                for g in list(gens):
                    try:
                        next(g)
                    except StopIteration:
                        gens.remove(g)

        ngrp = (nchunks + GRP - 1) // GRP
        units = [(s, grp, l) for s in range(nseq) for grp in range(ngrp) for l in range(2)]
        front_done = False
        for ui, (s, grp, l) in enumerate(units):
            if cur_layer[0] != l:
                load_layer(l, False)
            cs = [grp * GRP + gi for gi in range(GRP) if grp * GRP + gi < nchunks]
            if not front_done:
                run_rr([chunk_front(s, cs[0], l, 0)])
                chunk_qk(s, cs[0], l)
            front_done = False
            nu = units[ui + 1] if ui + 1 < len(units) else None
            for gi, c in enumerate(cs):
                chunk_mid(s, c, l, gi)
                if gi + 1 < len(cs):
                    nxt = chunk_front(s, cs[gi + 1], l, gi + 1)
                elif nu is not None and len(cs) > 1:
                    if win_loaded[0] != nu[2]:
                        reload_win(nu[2])
                    nxt = chunk_front(nu[0], nu[1] * GRP, nu[2], 0)
                    front_done = True
                else:
                    nxt = None
                    if nu is not None and win_loaded[0] != nu[2]:
                        reload_win(nu[2])
                run_rr([rwkv_chunk(s, c, l, c == 0, c == nchunks - 1), nxt])
                if gi + 2 == len(cs) and nu is not None:
                    reload_win(nu[2])
                if gi + 1 < len(cs):
                    btw = (lambda s=s, c2=cs[gi + 1], l=l: (chunk_qk(s, c2, l), rwkv_xs(s, c2, l)))
                elif front_done:
                    btw = (lambda nu=nu: chunk_qk(nu[0], nu[1] * GRP, nu[2]))
                else:
                    btw = None
                chunk_tail(s, c, l, gi, btw)
        if do_sample:
            if cur_layer[0] != 1:
                load_layer(1, False)
            sample_layer(1)
        S.build()
        with nc.allow_low_precision("float32r (fp32 container, 11-bit mantissa) matmul operands"):
            S.emit(nc)
    return nc


_CACHE = {}


def kernel(**inputs):
    do_sample = True
    if "prog" not in _CACHE:
        _CACHE["prog"] = build_program(do_sample)
    nc = _CACHE["prog"]
    consts = _consts()
    f = lambda a: np.ascontiguousarray(np.asarray(a, dtype=np.float32))
    in_maps = []
    for i in range(NCORES):
        m = {}
        m["xp"] = f(inputs["x_prompt"][i * SP:(i + 1) * SP])
        m["xs"] = f(inputs["x_sample"][i * SB:(i + 1) * SB, 0])
        m["spool"] = f(inputs["state_pool"][:, i * SB:(i + 1) * SB])
        m["ck"] = f(np.asarray(inputs["cache_swa_k"])[:, i * SB:(i + 1) * SB].reshape(2, SB, 128, 128))
        m["cv"] = f(np.asarray(inputs["cache_swa_v"])[:, i * SB:(i + 1) * SB].reshape(2, SB, 128, 128))
        m["sshift"] = f(inputs["state_rwkv_shift"][:, i * SB:(i + 1) * SB])
        m["swkv"] = f(inputs["state_rwkv_wkv"][:, i * SB:(i + 1) * SB])
        for n in PARAM_NAMES:
            m[n] = f(inputs[n])
        for n, v in consts.items():
            m["c_" + n] = v
        in_maps.append(m)
    res = run_bass_kernel_spmd(nc, in_maps, core_ids=list(range(NCORES)))
    R = res.results
    cat = lambda n, ax: np.concatenate([np.asarray(r[n], dtype=np.float32) for r in R], axis=ax)
    y_p = cat("yp", 0)
    y_s = cat("ys", 0).reshape(NCORES * SB, 1, D)
    return (y_p, y_s,
            cat("poolp", 1), cat("pools", 1),
            cat("kp", 1).reshape(2, NCORES * SP, 128, 2, 64), cat("ksm", 1).reshape(2, NCORES * SB, 128, 2, 64),
            cat("vp", 1).reshape(2, NCORES * SP, 128, 2, 64), cat("vsm", 1).reshape(2, NCORES * SB, 128, 2, 64),
            cat("shp", 1), cat("shs", 1), cat("wkp", 1), cat("wks", 1))
```

```python
import contextlib
import numpy as np
import ml_dtypes
import concourse.bass as bass
import concourse.mybir as mybir
from concourse.bass_utils import run_bass_kernel_spmd

F32 = mybir.dt.float32
BF16 = mybir.dt.bfloat16
AF = mybir.ActivationFunctionType
ALU = mybir.AluOpType
AX = mybir.AxisListType

NCORES = 8
D = 1024
DIN = 2880
SEQ = 2048
NCH = SEQ // 128
GRP = 4
SP = 2
SB = 16
OU, OGP, OQ, OKK, OV, OGA, ORIN, OGR = 0, 256, 512, 1024, 1152, 1280, 1792, 2624
KAP = float(np.exp(-0.5))
NORM_EPS = 1e-6
QK_EPS = 1e-6
GN_EPS = 64e-5
PAST = 16384

ENGS = ("pe", "act", "dve", "pool", "sp")
USE_FENCE = False
USE_F32R = False
F32R = mybir.dt.float32r


class _PEProxy:
    def __init__(self, eng):
        self._e = eng

    def matmul(self, out, lhsT=None, rhs=None, **kw):
        if USE_F32R and lhsT.dtype == F32:
            lhsT = lhsT.bitcast(F32R)
            rhs = rhs.bitcast(F32R)
        return self._e.matmul(out, lhsT=lhsT, rhs=rhs, **kw)

    def __getattr__(self, n):
        return getattr(self._e, n)


class _OutProxy:
    def __init__(self, eng, names):
        self._e = eng
        self._n = names

    def _fix(self, ap):
        if ap.dtype == F32 and ap.tensor.name in self._n:
            return ap.bitcast(F32R)
        return ap

    def __getattr__(self, n):
        f = getattr(self._e, n)
        if not callable(f):
            return f

        def w(*a, **kw):
            if "out" in kw:
                kw["out"] = self._fix(kw["out"])
            return f(*a, **kw)
        return w


class Op:
    def __init__(self, eng, fn, reads, writes, is_dma):
        self.eng = eng
        self.fn = fn
        self.reads = reads
        self.writes = writes
        self.is_dma = is_dma
        self.waits = {}
        self.signal = False
        self.slot = None
        self.slotval = 0
        self.known = None


def _related(a, b):
    return a == b or a.startswith(b + "/") or b.startswith(a + "/")


class Sched:
    NSLOT = 8

    def __init__(self):
        self.ops = []
        self.per_eng = {e: [] for e in ENGS}
        self.res = {}
        self.pending_mp = []
        self.zb = None
        self.r32 = set()

    PSUM_ROOTS = ("pz0", "pz1", "ptb", "pA", "pB", "pCD", "pE")

    def _deps(self, op):
        deps = []
        for tok in op.reads:
            root = tok.split("/", 1)[0]
            for t2, rec in self.res.get(root, {}).items():
                if _related(tok, t2) and rec[0] is not None:
                    deps.append(rec[0])
                if root in self.PSUM_ROOTS and _related(tok, t2):
                    deps.extend(r for r in rec[1] if r.eng != op.eng)
        for tok in op.writes:
            root = tok.split("/", 1)[0]
            for t2, rec in self.res.get(root, {}).items():
                if _related(tok, t2):
                    if rec[0] is not None:
                        deps.append(rec[0])
                    deps.extend(rec[1])
        return deps

    def _fence(self):
        pend = self.pending_mp
        self.pending_mp = []
        last = pend[-1]
        for o in pend:
            o.mp_pending = False
        ap, m = last.mp
        zb = self.zb
        rd = tuple(dict.fromkeys(t for o in pend for t in o.reads)) + ("zb",)
        wr = tuple(dict.fromkeys(t for o in pend for t in o.writes))
        fap, fin = self.fence_aps
        self.add("pe", lambda e: e.transpose(out=fap, in_=fin, identity=fin), rd + ("identb",), wr + ("ptb",))

    def add(self, eng, fn, reads=(), writes=(), dma=False, mp=None):
        def n1(t):
            parts = t.split("/")
            if parts[0] == "pCD" and len(parts) > 1:
                return "pCD/" + parts[1]
            return parts[0] if parts[0] in self.PSUM_ROOTS else t
        norm = lambda toks: tuple(dict.fromkeys(n1(t) for t in toks))
        op = Op(eng, fn, norm(reads), norm(writes), dma)
        op.mp = mp
        op.mp_pending = mp is not None
        deps = self._deps(op)
        if (eng != "pe" or dma) and any(getattr(d, "mp_pending", False) for d in deps):
            self._fence()
            deps = self._deps(op)
        self.ops.append(op)
        self.per_eng[eng].append(op)
        if mp is not None:
            self.pending_mp.append(op)
        op.deps = deps
        for tok in op.reads:
            root = tok.split("/", 1)[0]
            rec = self.res.setdefault(root, {}).setdefault(tok, [None, []])
            rec[1].append(op)
        for tok in op.writes:
            root = tok.split("/", 1)[0]
            d = self.res.setdefault(root, {})
            for t2 in [t for t in d if t == tok or t.startswith(tok + "/")]:
                del d[t2]
            d[tok] = [op, []]
        return op

    def build(self):
        if self.pending_mp:
            self._fence()
        for e in ENGS:
            i = 0
            n = 0
            for op in self.per_eng[e]:
                if op.is_dma:
                    op.slot = (e, n % self.NSLOT)
                    op.slotval = n // self.NSLOT + 1
                    n += 1
                else:
                    i += 1
                    op.ord = i
        known_eng = {e: {} for e in ENGS}
        last_on_slot = {}
        for op in self.ops:
            kn = known_eng[op.eng]
            needs = {}
            deps = list(op.deps)
            if op.is_dma and op.slot in last_on_slot:
                deps.append(last_on_slot[op.slot])
            for d in deps:
                if d is op:
                    continue
                if (not d.is_dma) and d.eng == "pe" and op.eng == "pe" and not op.is_dma:
                    continue
                k = d.slot if d.is_dma else d.eng
                val = d.slotval if d.is_dma else d.ord
                if kn.get(k, 0) >= val:
                    continue
                if needs.get(k, (0, None))[0] < val:
                    needs[k] = (val, d)
            for k, (val, d) in needs.items():
                d.signal = True
                kn[k] = max(kn.get(k, 0), val)
                for k2, v2 in d.known.items():
                    if kn.get(k2, 0) < v2:
                        kn[k2] = v2
            op.waits = {k: d for k, (val, d) in needs.items()}
            op.known = dict(kn)
            if op.is_dma:
                last_on_slot[op.slot] = op
            op.deps = None
        for e in ENGS:
            c = 0
            for op in self.per_eng[e]:
                if not op.is_dma:
                    if op.signal:
                        c += 1
                    op.sigval = c
        return self

    def emit(self, nc):
        engmap = {"pe": "tensor", "act": "scalar", "dve": "vector", "pool": "gpsimd", "sp": "sync"}
        with contextlib.ExitStack() as st:
            sems = {}
            for e in ENGS:
                sems[e] = st.enter_context(nc.semaphore("s_" + e))
            for e in ENGS:
                nslots = min(self.NSLOT, sum(1 for o in self.per_eng[e] if o.is_dma))
                for j in range(nslots):
                    sems[(e, j)] = st.enter_context(nc.semaphore("d_%s%d" % (e, j)))
            block = st.enter_context(nc.Block())

            def mk(e):
                def body(eng):
                    for op in self.per_eng[e]:
                        for k, d in op.waits.items():
                            if d.is_dma:
                                eng.wait_ge(sems[k], 16 * d.slotval)
                            else:
                                eng.wait_ge(sems[k], d.sigval)
                        if op.is_dma:
                            pe_ = eng
                        elif e == "pe":
                            pe_ = _PEProxy(eng)
                        elif USE_F32R:
                            pe_ = _OutProxy(eng, self.r32)
                        else:
                            pe_ = eng
                        ins = op.fn(pe_)
                        if op.is_dma:
                            ins.then_inc(sems[op.slot], 16)
                        elif op.signal:
                            ins.then_inc(sems[e], 1)
                    lasts = {}
                    for op in self.per_eng[e]:
                        if op.is_dma:
                            lasts[op.slot] = op
                    for slot, op in lasts.items():
                        eng.wait_ge(sems[slot], 16 * op.slotval)
                return body

            for e in ENGS:
                if self.per_eng[e]:
                    getattr(block, engmap[e])(mk(e))


def _consts():
    s = np.arange(128)[:, None]
    t = np.arange(128)[None, :]
    c = {}
    c["ident"] = np.eye(128, dtype=np.float32)
    c["identb"] = np.eye(128).astype(ml_dtypes.bfloat16)
    c["msi"] = np.concatenate([(s < t), (s <= t)], axis=1).astype(np.float32)
    c["mst"] = (t < s).astype(np.float32)
    c["tmk"] = (-KAP * ((s <= t).astype(np.float64) - (s <= 63).astype(np.float64))).astype(np.float32)
    c["coefs"] = np.stack([KAP * (np.arange(128) <= 63), -KAP * (np.arange(128) > 63)], axis=1).astype(np.float32)
    pm = np.zeros((3, 4, 128, 128), np.float64)
    for g, w in enumerate((2, 4, 8, 16)):
        pm[0, g] = ((s > t - w) & (s <= t)) / w - (s == t)
        cnt = np.minimum(t + 1, w)
        pm[1, g] = ((s > t - w) & (s <= t)) / cnt - (s == t)
        pm[2, g] = (s > 128 + t - w) / w
    c["poolm"] = np.ascontiguousarray(pm.transpose(2, 0, 1, 3)).astype(np.float32)
    c["amask"] = np.stack([(s <= t), (s > t)], axis=1).astype(ml_dtypes.bfloat16)
    freqs = (np.float32(10000.0) ** (-np.arange(32, dtype=np.float32) / np.float32(32))).astype(np.float32)
    pos = (np.arange(NCH)[None, :] * 128 + np.arange(128)[:, None]).astype(np.float32)
    ang = (pos[:, :, None] * freqs[None, None, :]).astype(np.float32)
    c["rope"] = np.stack([np.cos(ang), np.sin(ang)], axis=1).astype(np.float32)
    angs = (np.float32(PAST) * freqs).astype(np.float32)
    c["ropes"] = np.tile(np.stack([np.cos(angs), np.sin(angs)])[None], (128, 1, 1)).astype(np.float32)
    p = np.arange(128)
    smc = np.zeros((128, 64), np.float32)
    g_ = p // 64
    smc[p, g_ * 4 + p % 4] = 1.0
    smc[p, 8 + (p % 64) // 4] = 1.0
    smc[:, 24] = 1 - g_
    smc[:, 25] = g_
    smc[p, 26 + (p // 2) % 4] = 1.0
    smc[:, 30] = 1 - (p % 2)
    smc[:, 31] = p % 2
    smc[p, 32 + p % 8] = 1.0
    smc[p, 40 + p // 8] = 1.0
    c["smc"] = smc
    c["indBT"] = np.ascontiguousarray(smc[:, 40:56].T)
    return c


PARAM_NAMES = ["norm_g", "w_in", "w_out", "pool_w", "pool_scale", "q_norm_g", "k_norm_g", "attn_sinks",
               "rwkv_mu", "rwkv_w0", "rwkv_w_up", "rwkv_a0", "rwkv_a_up", "rwkv_k_k", "rwkv_k_a",
               "rwkv_r_k", "rwkv_ln_g", "rwkv_ln_b"]
PARAM_SHAPES = {"norm_g": [2, 1024], "w_in": [2, 1024, 2880], "w_out": [2, 1024, 1024], "pool_w": [2, 4, 64, 64],
                "pool_scale": [2, 256], "q_norm_g": [2, 64], "k_norm_g": [2, 64], "attn_sinks": [2, 8],
                "rwkv_mu": [2, 832], "rwkv_w0": [2, 256], "rwkv_w_up": [2, 32, 256], "rwkv_a0": [2, 256],
                "rwkv_a_up": [2, 32, 256], "rwkv_k_k": [2, 256], "rwkv_k_a": [2, 256], "rwkv_r_k": [2, 256],
                "rwkv_ln_g": [2, 256], "rwkv_ln_b": [2, 256]}
CONST_SHAPES = {"ident": ([128, 128], F32), "identb": ([128, 128], BF16), "msi": ([128, 256], F32),
                "mst": ([128, 128], F32), "tmk": ([128, 128], F32), "coefs": ([128, 2], F32),
                "poolm": ([128, 3, 4, 128], F32), "amask": ([128, 2, 128], BF16),
                "rope": ([128, 2, NCH, 32], F32), "ropes": ([128, 2, 32], F32),
                "smc": ([128, 64], F32), "indBT": ([16, 128], F32)}


def build_program(do_sample=True, nchunks=NCH, nseq=SP, stage=99, sstage=99):
    nc = bass.Bass("TRN2", target_bir_lowering=False)
    DI = lambda n, s, d=F32: nc.dram_tensor(n, s, d, kind="ExternalInput").ap()
    DO = lambda n, s, d=F32: nc.dram_tensor(n, s, d, kind="ExternalOutput").ap()
    xp = DI("xp", [SP, SEQ, D])
    xs_in = DI("xs", [SB, D])
    spool = DI("spool", [2, SB, 15, 256])
    ck = DI("ck", [2, SB, 128, 128])
    cv = DI("cv", [2, SB, 128, 128])
    sshift = DI("sshift", [2, SB, 832])
    swkv = DI("swkv", [2, SB, 4, 64, 64])
    P = {n: DI(n, PARAM_SHAPES[n]) for n in PARAM_NAMES}
    C = {n: DI("c_" + n, sh, dt) for n, (sh, dt) in CONST_SHAPES.items()}
    yp = DO("yp", [SP, SEQ, D])
    ys = DO("ys", [SB, D])
    poolp = DO("poolp", [2, SP, 15, 256])
    pools = DO("pools", [2, SB, 15, 256])
    kp = DO("kp", [2, SP, 128, 128])
    ksm = DO("ksm", [2, SB, 128, 128])
    vp = DO("vp", [2, SP, 128, 128])
    vsm = DO("vsm", [2, SB, 128, 128])
    shp = DO("shp", [2, SP, 832])
    shs = DO("shs", [2, SB, 832])
    wkp = DO("wkp", [2, SP, 4, 64, 64])
    wks = DO("wks", [2, SB, 4, 64, 64])
    wsc_in = nc.dram_tensor("wsc_in", [2, 128, 8, DIN], BF16, kind="Internal").ap()
    wsc_out = nc.dram_tensor("wsc_out", [2, 128, 8, D], BF16, kind="Internal").ap()

    S = Sched()

    class _Probe:
        rec = None

        def matmul(self, out, lhsT=None, rhs=None, **kw):
            self.rec = ("mm", out, lhsT, rhs)

        def transpose(self, **kw):
            self.rec = ("tr",)

    def A(eng, fn, r=(), w=(), dma=False):
        mp = None
        if eng == "pe" and not dma:
            pr = _Probe()
            fn(pr)
            if pr.rec[0] == "mm" and pr.rec[2].dtype == F32:
                S.r32.add(pr.rec[2].tensor.name)
                S.r32.add(pr.rec[3].tensor.name)
            if USE_FENCE and pr.rec[0] == "mm" and pr.rec[2].dtype == F32:
                out = pr.rec[1]
                assert len(out.shape) == 2, out.shape
                mp = (out[:, 0:2], out.shape[0])
        return S.add(eng, fn, r, w, dma=dma, mp=mp)
    with contextlib.ExitStack() as st:
        T = lambda n, s, d=F32: st.enter_context(nc.sbuf_tensor(n, s, d))
        PS = lambda n, s, d=F32: st.enter_context(nc.psum_tensor(n, s, d))
        win = T("win", [128, 8, DIN], BF16)
        wout = T("wout", [128, 8, D], BF16)
        gT = T("gT", [128, D])
        muT = T("muT", [128, 832])
        rows = T("rows", [128, 5, 256])
        qkgain = T("qkgain", [128, 2, 10, 64])
        esink = T("esink", [128, 2, 8])
        poolw = T("poolw", [64, 2, 4, 64])
        pscl = T("pscl", [64, 2, 256])
        lora = T("lora", [33, 2, 2, 256])
        ident = T("ident", [128, 128])
        identb = T("identb", [128, 128], BF16)
        zb = T("zb", [128, 128], BF16)
        S.zb = zb
        msi = T("msi", [128, 256])
        mst = T("mst", [128, 128])
        tmk = T("tmk", [128, 128])
        coefs = T("coefs", [128, 2])
        poolm = T("poolm", [128, 3, 4, 128])
        amask = T("amask", [128, 2, 128], BF16)
        rope = T("rope", [128, 2, NCH, 32])
        xg = T("xg", [128, GRP, D])
        hm = T("hm", [128, D], BF16)
        hx = T("hx", [128, D], BF16)
        gates2 = T("gates2", [128, 2, 256])
        hTm = T("hTm", [128, 8, 128], BF16)
        gates = T("gates", [128, D])
        stat = T("stat", [128, 32])
        ubuf = T("ubuf", [128, 2, 2, 256])
        pTsb = T("pTsb", [64, 512])
        qk = T("qk", [128, 10, 64])
        qk2 = T("qk2", [128, 10, 64])
        qk3 = T("qk3", [128, 10, 64])
        qT = T("qT", [128, 4, 128], BF16)
        kTb = T("kTb", [128, 2, 2, 128], BF16)
        vaug = T("vaug", [128, 2, 2, 2, 65], BF16)
        PT = T("PT", [128, 4, 512], BF16)
        atmp = T("atmp", [128, 8, 64])
        rin = T("rin", [128, 832])
        prevL = T("prevL", [128, 2, 832])
        xsb = T("xsb", [128, 832])
        lwin = T("lwin", [33, 2, 128])
        sg = T("sg", [128, 512])
        ra = T("ra", [128, 10, 256])
        XT = T("XT", [64, 4, 4, 128])
        AM = T("AM", [128, 6, 4, 128])
        Stt = T("Stt", [64, 4, 64])
        Gs = T("Gs", [128, 4, 64])
        Us = T("Us", [128, 4, 64])
        STs = T("STs", [64, 2, 4, 64])
        edge = T("edge", [64, 4, 2])
        pz = [PS("pz0", [128, 512]), PS("pz1", [128, 512])]
        ptb = PS("ptb", [128, 8, 128], BF16)
        pA = PS("pA", [128, 512])
        pB = PS("pB", [128, 512])
        pCD = PS("pCD", [128, 4, 256])
        pE = PS("pE", [128, 4, 128])
        S.fence_aps = (ptb[0:32, 0, 0:32], identb[0:32, 0:32])

        dq = ["sp"]

        def dma(out, in_, r=(), w=(), q="sp"):
            A(q, lambda e: e.dma_start(out=out, in_=in_), r, w, dma=True)

        xgf = xg[:, :, :].rearrange("p a c -> p (a c)")
        for n, tl in (("ident", ident), ("identb", identb), ("msi", msi), ("mst", mst),
                      ("amask", amask), ("rope", rope)):
            full = tuple(slice(None) for _ in CONST_SHAPES[n][0])
            dma(tl[full], C[n][full], w=[n])
        dma(xgf[:, 0:1536], C["poolm"].rearrange("p a g t -> p (a g t)"), w=["xg"])
        A("act", lambda e: e.activation(out=poolm[:, :, :, :].rearrange("p a g t -> p (a g t)"), in_=xgf[:, 0:1536], func=AF.Copy),
          ["xg"], ["poolm"])
        dma(xgf[:, 1536:1664], C["tmk"][:, :], w=["xg"])
        dma(xgf[:, 1664:1666], C["coefs"][:, :], w=["xg"])
        A("act", lambda e: e.activation(out=tmk[:, :], in_=xgf[:, 1536:1664], func=AF.Copy), ["xg"], ["tmk"])
        A("act", lambda e: e.activation(out=coefs[:, :], in_=xgf[:, 1664:1666], func=AF.Copy), ["xg"], ["coefs"])
        for kc in range(8):
            for h2 in range(2):
                c0 = h2 * 1440
                dma(win[:, kc, c0:c0 + 1440], P["w_in"][0, kc * 128:(kc + 1) * 128, c0:c0 + 1440],
                    w=["win/%d/%d" % (kc, h2)], q="pool")
        for kc in range(8):
            dma(wout[:, kc, :], P["w_out"][0, kc * 128:(kc + 1) * 128, :], w=["wout/%d" % kc], q="pool")
        for kc in range(8):
            for h2 in range(2):
                c0 = h2 * 1440
                dma(wsc_in[1, :, kc, c0:c0 + 1440], P["w_in"][1, kc * 128:(kc + 1) * 128, c0:c0 + 1440],
                    w=["wsc_in1/%d/%d" % (kc, h2)], q="pool")
            dma(wsc_out[1, :, kc, :], P["w_out"][1, kc * 128:(kc + 1) * 128, :], w=["wsc_out1/%d" % kc], q="pool")
        for kc in range(8):
            dma(wsc_in[0, :, kc, :], win[:, kc, :], r=["win/%d" % kc], w=["wsc_in0/%d" % kc])
            dma(wsc_out[0, :, kc, :], wout[:, kc, :], r=["wout/%d" % kc], w=["wsc_out0/%d" % kc])
        for l in range(2):
            dma(qkgain[:, l, 0:8, :], P["q_norm_g"][l].partition_broadcast(128).unsqueeze(1).to_broadcast([128, 8, 64]),
                w=["qkgain"])
            dma(qkgain[:, l, 8:10, :], P["k_norm_g"][l].partition_broadcast(128).unsqueeze(1).to_broadcast([128, 2, 64]),
                w=["qkgain"])
            dma(esink[:, l, :], P["attn_sinks"][l].partition_broadcast(128), w=["esink"])
            dma(xgf[0:64, 2048 + l * 256:2048 + (l + 1) * 256].rearrange("c (g d) -> c g d", g=4), P["pool_w"][l].rearrange("g c d -> c g d"), w=["xg"])
            dma(pscl[:, l, :], P["pool_scale"][l].partition_broadcast(64), w=["pscl"])
            lst = xgf[0:33, 2560 + l * 512:2560 + (l + 1) * 512].rearrange("r (j n) -> r j n", j=2)
            dma(lst[0:32, 0, :], P["rwkv_w_up"][l], w=["xg"])
            dma(lst[0:32, 1, :], P["rwkv_a_up"][l], w=["xg"])
            dma(lst[32:33, 0, :], P["rwkv_w0"][l:l + 1, :], w=["xg"])
            dma(lst[32:33, 1, :], P["rwkv_a0"][l:l + 1, :], w=["xg"])
        A("dve", lambda e: e.tensor_scalar(out=qkgain[:, :, 0:8, :], in0=qkgain[:, :, 0:8, :], scalar1=0.125,
                                           scalar2=None, op0=ALU.mult), ["qkgain"], ["qkgain"])
        A("act", lambda e: e.activation(out=esink[:, :, :], in_=esink[:, :, :], func=AF.Exp), ["esink"], ["esink"])
        A("dve", lambda e: e.tensor_tensor(out=poolw[:, :, :, :], in0=xgf[0:64, 2048:2560].rearrange("c (l g d) -> c l g d", l=2, g=4),
                                           in1=pscl[:, :, :].rearrange("c l (g d) -> c l g d", g=4),
                                           op=ALU.mult), ["xg", "pscl"], ["poolw"])
        A("act", lambda e: e.activation(out=lora[:, :, :, :].rearrange("r l j n -> r (l j n)"), in_=xgf[0:33, 2560:3584], func=AF.Copy),
          ["xg"], ["lora"])
        A("pool", lambda e: e.memset(lwin[32:33, :, :], 1.0), [], ["lwin/ones"])
        A("pool", lambda e: e.memset(zb[:, :], 0.0), [], ["zb"])
        A("pool", lambda e: e.memset(vaug[:, :, :, :, :].rearrange("p l a g c -> p (l a g) c")[:, :, 64:65], 1.0), [], ["vaug"])

        cur_layer = [None]

        win_loaded = [0]

        def reload_win(l):
            for kc in range(8):
                dma(win[:, kc, :], wsc_in[l, :, kc, :], r=["wsc_in%d/%d" % (l, kc)], w=["win/%d" % kc], q="pool")
            dma(gT[:, :], P["norm_g"][l].partition_broadcast(128), w=["gT"], q="pool")
            win_loaded[0] = l

        def load_layer(l, first):
            if not first:
                if win_loaded[0] != l:
                    reload_win(l)
                for kc in range(8):
                    dma(wout[:, kc, :], wsc_out[l, :, kc, :], r=["wsc_out%d/%d" % (l, kc)], w=["wout/%d" % kc])
            else:
                dma(gT[:, :], P["norm_g"][l].partition_broadcast(128), w=["gT"])
            dma(muT[:, :], P["rwkv_mu"][l].partition_broadcast(128), w=["muT"])
            for i, n in enumerate(("rwkv_k_k", "rwkv_k_a", "rwkv_r_k", "rwkv_ln_g", "rwkv_ln_b")):
                dma(rows[:, i, :], P[n][l].partition_broadcast(128), w=["rows/%d" % i])
            cur_layer[0] = l

        def rsqrt_small(dst, src, n, scale, eps, rtok, wtok):
            A("dve", lambda e: e.tensor_scalar(out=dst, in0=src, scalar1=scale, scalar2=eps, op0=ALU.mult,
                                               op1=ALU.add), rtok, wtok)
            A("act", lambda e: e.activation(out=dst, in_=dst, func=AF.Ln), wtok, wtok)
            A("act", lambda e: e.activation(out=dst, in_=dst, func=AF.Exp, scale=-0.5), wtok, wtok)

        def sigmoid_act(tmp, src, rtok, tmptok):
            A("act", lambda e: e.activation(out=tmp, in_=src, func=AF.Exp, scale=-1.0), rtok, tmptok)
            A("act", lambda e: e.activation(out=tmp, in_=tmp, func=AF.Ln, bias=1.0), tmptok, tmptok)
            A("act", lambda e: e.activation(out=tmp, in_=tmp, func=AF.Exp, scale=-1.0), tmptok, tmptok)

        def silu_to(dst, src_ps, ncols, tmp, rtok, wtok, tmptok):
            sigmoid_act(tmp, src_ps, rtok, tmptok)
            A("dve", lambda e: e.tensor_tensor(out=dst, in0=src_ps, in1=tmp, op=ALU.mult), rtok + tmptok, wtok)

        def rmsnorm_T(xt, np_, xtok):
            A("act", lambda e: e.activation(out=hx[0:np_, :], in_=xt, func=AF.Square,
                                            accum_out=stat[0:np_, 0:1]), [xtok], ["hx", "stat/0"])
            rsqrt_small(stat[0:np_, 0:1], stat[0:np_, 0:1], 1, 1.0 / D, NORM_EPS, ["stat/0"], ["stat/0"])
            A("dve", lambda e: e.scalar_tensor_tensor(out=hx[0:np_, :], in0=xt, scalar=stat[0:np_, 0:1],
                                                      in1=gT[0:np_, :], op0=ALU.mult, op1=ALU.mult),
              [xtok, "stat/0", "gT"], ["hx"])
            for kc in range(8):
                A("pe", lambda e, kc=kc: e.transpose(out=ptb[:, kc, 0:np_], in_=hx[0:np_, kc * 128:(kc + 1) * 128],
                                                     identity=identb[0:np_, 0:np_]), ["hx", "identb"], ["ptb"])
            A("act", lambda e: e.activation(out=hTm[:, :, 0:np_], in_=ptb[:, :, 0:np_], func=AF.Copy),
              ["ptb"], ["hTm"])

        def win_group(pzt, ptok, c0, ncols, np_, perm=False):
            for kc in range(8):
                if perm:
                    rhs = win[:, kc, c0:c0 + ncols].rearrange("p (g t d) -> p t g d", g=2, t=4, d=64)
                    out = pzt[0:np_, 0:ncols].rearrange("p (t g d) -> p t g d", g=2, t=4, d=64)
                else:
                    rhs = win[:, kc, c0:c0 + ncols]
                    out = pzt[0:np_, 0:ncols]
                A("pe", lambda e, kc=kc, rhs=rhs, out=out: e.matmul(out, lhsT=hTm[:, kc, 0:np_], rhs=rhs,
                                                                    start=(kc == 0), stop=(kc == 7)),
                  ["hTm", "win/%d" % kc], [ptok])

        def wout_apply(xt, np_, xtok, between=None):
            for kc in range(8):
                A("pe", lambda e, kc=kc: e.transpose(out=ptb[:, kc, 0:np_], in_=hm[0:np_, kc * 128:(kc + 1) * 128],
                                                     identity=identb[0:np_, 0:np_]), ["hm", "identb"], ["ptb"])
            A("act", lambda e: e.activation(out=hTm[:, :, 0:np_], in_=ptb[:, :, 0:np_], func=AF.Copy),
              ["ptb"], ["hTm"])
            for half in range(2):
                for kc in range(8):
                    A("pe", lambda e, kc=kc, half=half: e.matmul(pz[half][0:np_, :], lhsT=hTm[:, kc, 0:np_],
                                                                 rhs=wout[:, kc, half * 512:(half + 1) * 512],
                                                                 start=(kc == 0), stop=(kc == 7)),
                      ["hTm", "wout/%d" % kc], ["pz%d" % half])
            if between is not None:
                between()
            for half in range(2):
                A("dve", lambda e, half=half: e.tensor_tensor(out=xt[:, half * 512:(half + 1) * 512],
                                                              in0=pz[half][0:np_, :],
                                                              in1=xt[:, half * 512:(half + 1) * 512], op=ALU.add),
                  ["pz%d" % half, xtok], [xtok])

        def qk_norm_rope(np_, l, cosT, sinT, rtok="rope"):
            A("dve", lambda e: e.tensor_tensor(out=qk2[0:np_], in0=qk[0:np_], in1=qk[0:np_], op=ALU.mult),
              ["qk"], ["qk2"])
            A("dve", lambda e: e.tensor_reduce(out=stat[0:np_, 1:11], in_=qk2[0:np_], axis=AX.X, op=ALU.add),
              ["qk2"], ["stat/1"])
            rsqrt_small(stat[0:np_, 1:11], stat[0:np_, 1:11], 10, 1.0 / 64, QK_EPS, ["stat/1"], ["stat/1"])
            A("dve", lambda e: e.tensor_tensor(out=qk[0:np_], in0=qk[0:np_],
                                               in1=stat[0:np_, 1:11].unsqueeze(2).to_broadcast([np_, 10, 64]),
                                               op=ALU.mult), ["qk", "stat/1"], ["qk"])
            A("dve", lambda e: e.tensor_tensor(out=qk[0:np_], in0=qk[0:np_], in1=qkgain[0:np_, l], op=ALU.mult),
              ["qk", "qkgain"], ["qk"])
            cb = cosT.unsqueeze(1).to_broadcast([np_, 10, 32])
            sb = sinT.unsqueeze(1).to_broadcast([np_, 10, 32])
            x1 = qk[0:np_, :, 0:32]
            x2 = qk[0:np_, :, 32:64]
            A("dve", lambda e: e.tensor_tensor(out=qk2[0:np_, :, 0:32], in0=x1, in1=cb, op=ALU.mult),
              ["qk", rtok], ["qk2/a"])
            A("pool", lambda e: e.tensor_tensor(out=qk2[0:np_, :, 32:64], in0=x2, in1=sb, op=ALU.mult),
              ["qk", rtok], ["qk2/b"])
            A("dve", lambda e: e.tensor_tensor(out=qk3[0:np_, :, 0:32], in0=x1, in1=sb, op=ALU.mult),
              ["qk", rtok], ["qk3/a"])
            A("pool", lambda e: e.tensor_tensor(out=qk3[0:np_, :, 32:64], in0=x2, in1=cb, op=ALU.mult),
              ["qk", rtok], ["qk3/b"])
            A("dve", lambda e: e.tensor_tensor(out=x1, in0=qk2[0:np_, :, 0:32], in1=qk2[0:np_, :, 32:64],
                                               op=ALU.subtract), ["qk2"], ["qk"])
            A("pool", lambda e: e.tensor_tensor(out=x2, in0=qk3[0:np_, :, 0:32], in1=qk3[0:np_, :, 32:64],
                                                op=ALU.add), ["qk3", "qk"], ["qk"])

        def chunk_front(s, c, l, gi):
            first = (c == 0)
            last = (c == nchunks - 1)
            par = c % 2
            xt = xg[:, gi, :]
            xtok = "xg/%d" % gi
            if l == 0:
                dma(xt, xp[s, c * 128:(c + 1) * 128, :], w=[xtok])
            rmsnorm_T(xt, 128, xtok)
            yield
            win_group(pz[0], "pz0", OU, 512, 128)
            ucur = ubuf[:, l, par, :]
            uprev = ubuf[:, l, 1 - par, :]
            A("act", lambda e: e.activation(out=ucur, in_=pz[0][:, 0:256], func=AF.Copy), ["pz0"], ["ubuf/%d/%d" % (l, par)])
            silu_to(gates[:, 0:256], pz[0][:, 256:512], 256, qk3[:, 0:4, :].rearrange("p h d -> p (h d)"), ["pz0"], ["gates/0"], ["qk3"])
            yield
            if last:
                dma(poolp[l, s, :, :], ubuf[113:128, l, par, :], r=["ubuf/%d/%d" % (l, par)])
            win_group(pz[1], "pz1", OQ, 512, 128, perm=True)
            A("act", lambda e: e.activation(out=qk[:, 0:8, :], in_=pz[1][:, :].rearrange("p (h d) -> p h d", d=64),
                                            func=AF.Copy), ["pz1"], ["qk"])
            yield
            for kc in range(8):
                b_ = win[:, kc, OKK:OKK + 256]
                rhs2 = bass.AP(b_.tensor, b_.offset, [list(b_.ap[0]), [OGR - OKK, 2], [1, 256]])
                A("pe", lambda e, kc=kc, rhs2=rhs2: e.matmul(pz[0][:, :].rearrange("p (a c) -> p a c", a=2), lhsT=hTm[:, kc, :], rhs=rhs2,
                                                             start=(kc == 0), stop=(kc == 7)), ["hTm", "win/%d" % kc], ["pz0"])
            A("act", lambda e: e.activation(out=qk[:, 8:10, :], in_=pz[0][:, 0:128].rearrange("p (h d) -> p h d", d=64),
                                            func=AF.Copy), ["pz0"], ["qk"])
            vcur = vaug[:, l, par]
            vprev = vaug[:, l, 1 - par]
            vtok = "vaug/%d/%d" % (l, par)
            A("act", lambda e: e.activation(out=vcur[:, :, 0:64], in_=pz[0][:, 128:256].rearrange("p (h d) -> p h d", d=64),
                                            func=AF.Copy), ["pz0"], [vtok])
            if last:
                A("act", lambda e: e.activation(out=atmp[:, 0:2, :], in_=pz[0][:, 128:256].rearrange("p (h d) -> p h d", d=64),
                                                func=AF.Copy), ["pz0"], ["atmp"])
                dma(vp[l, s, :, :], atmp[:, 0:2, :].rearrange("p h d -> p (h d)"), r=["atmp"])
            silu_to(gates2[:, par, :], pz[0][:, 256:512], 256, qk3[:, 0:4, :].rearrange("p h d -> p (h d)"), ["pz0"], ["gates2/%d" % par], ["qk3"])
            yield
            win_group(pz[1], "pz1", OGA, 512, 128)
            silu_to(gates[:, 256:768], pz[1][:, :], 512, atmp[:, :, :].rearrange("p h d -> p (h d)"), ["pz1"], ["gates/1"], ["atmp"])
            yield
            win_group(pz[0], "pz0", ORIN, 512, 128)
            A("act", lambda e: e.activation(out=rin[:, 0:512], in_=pz[0][:, :], func=AF.Copy), ["pz0"], ["rin/a"])
            yield
            win_group(pz[1], "pz1", ORIN + 512, 320, 128)
            A("dve", lambda e: e.tensor_copy(out=rin[:, 512:832], in_=pz[1][:, 0:320]), ["pz1"], ["rin/b"])
            yield

        def chunk_qk(s, c, l):
            qk_norm_rope(128, l, rope[:, 0, c, :], rope[:, 1, c, :])
            if c == nchunks - 1:
                dma(kp[l, s, :, :], qk[:, 8:10, :].rearrange("p h d -> p (h d)"), r=["qk"])

        def chunk_mid(s, c, l, gi):
            first = (c == 0)
            last = (c == nchunks - 1)
            par = c % 2
            ucur = ubuf[:, l, par, :]
            uprev = ubuf[:, l, 1 - par, :]
            vcur = vaug[:, l, par]
            vprev = vaug[:, l, 1 - par]
            vtok = "vaug/%d/%d" % (l, par)
            for g in range(4):
                A("pe", lambda e, g=g: e.matmul(pA[0:64, g * 128:(g + 1) * 128], lhsT=ucur[:, g * 64:(g + 1) * 64],
                                                rhs=poolm[:, 1 if first else 0, g, :], start=True, stop=first),
                  ["ubuf/%d/%d" % (l, par), "poolm"], ["pA"])
                if not first:
                    A("pe", lambda e, g=g: e.matmul(pA[0:64, g * 128:(g + 1) * 128], lhsT=uprev[:, g * 64:(g + 1) * 64],
                                                    rhs=poolm[:, 2, g, :], start=False, stop=True),
                      ["ubuf/%d/%d" % (l, 1 - par), "poolm"], ["pA"])
            A("act", lambda e: e.activation(out=pTsb[:, :], in_=pA[0:64, :], func=AF.Copy), ["pA"], ["pTsb"])
            for g in range(4):
                A("pe", lambda e, g=g: e.matmul(pB[:, g * 64:(g + 1) * 64], lhsT=pTsb[:, g * 128:(g + 1) * 128],
                                                rhs=poolw[:, l, g, :], start=True, stop=True),
                  ["pTsb", "poolw"], ["pB"])
            A("dve", lambda e: e.tensor_tensor(out=hm[:, 0:256], in0=pB[:, 0:256], in1=gates[:, 0:256], op=ALU.mult),
              ["pB", "gates/0"], ["hm/0"])

            for tpair in range(2):
                for tt in range(2):
                    t_ = tpair * 2 + tt
                    A("pe", lambda e, t_=t_, tt=tt: e.transpose(
                        out=pA[:, tt * 128:(tt + 1) * 128],
                        in_=qk[:, 2 * t_:2 * t_ + 2, :].rearrange("p h d -> p (h d)"), identity=ident[:, :]),
                      ["qk", "ident"], ["pA"])
                A("act", lambda e, tpair=tpair: e.activation(
                    out=qT[:, 2 * tpair:2 * tpair + 2, :],
                    in_=pA[:, 0:256].rearrange("p (t q) -> p t q", t=2), func=AF.Copy), ["pA"], ["qT"])
            kcur = kTb[:, l, par, :]
            kprev = kTb[:, l, 1 - par, :]
            ktok = "kTb/%d/%d" % (l, par)
            A("pe", lambda e: e.transpose(out=pB[:, 0:128], in_=qk[:, 8:10, :].rearrange("p h d -> p (h d)"),
                                          identity=ident[:, :]), ["qk", "ident"], ["pB"])
            A("act", lambda e: e.activation(out=kcur, in_=pB[:, 0:128], func=AF.Copy), ["pB"], [ktok])
            scp = [pA, pB]
            for blk in range(1 if first else 2):
                kt_ = kcur if blk == 0 else kprev
                ktk = ktok if blk == 0 else "kTb/%d/%d" % (l, 1 - par)
                for g in range(2):
                    A("pe", lambda e, g=g, kt_=kt_: e.matmul(scp[g][:, :], lhsT=kt_[g * 64:(g + 1) * 64, :],
                                                             rhs=qT[g * 64:(g + 1) * 64, :, :],
                                                             start=True, stop=True), [ktk, "qT"], ["pA" if g == 0 else "pB"])
                    pt_ = PT[:, blk * 2 + g, :]
                    ptk = "PT/%d" % (blk * 2 + g)
                    A("act", lambda e, g=g, pt_=pt_: e.activation(out=pt_, in_=scp[g][:, :], func=AF.Exp),
                      ["pA" if g == 0 else "pB"], [ptk])
                    A("dve", lambda e, pt_=pt_, blk=blk: e.tensor_tensor(
                        out=pt_.rearrange("p (t q) -> p t q", t=4), in0=pt_.rearrange("p (t q) -> p t q", t=4),
                        in1=amask[:, blk, :].unsqueeze(1).to_broadcast([128, 4, 128]), op=ALU.mult),
                      [ptk, "amask"], [ptk])
            ov = pCD[:, :, :].rearrange("p a (b c) -> p (a b) c", b=2)
            for g in range(2):
                for t_ in range(4):
                    h = g * 4 + t_
                    A("pe", lambda e, g=g, t_=t_, h=h: e.matmul(ov[:, h, 0:65], lhsT=PT[:, g, t_ * 128:(t_ + 1) * 128],
                                                                 rhs=vcur[:, g, :], start=True, stop=first),
                      ["PT/%d" % g, vtok], ["pCD"])
                    if not first:
                        A("pe", lambda e, g=g, t_=t_, h=h: e.matmul(ov[:, h, 0:65], lhsT=PT[:, 2 + g, t_ * 128:(t_ + 1) * 128],
                                                                     rhs=vprev[:, g, :], start=False, stop=True),
                          ["PT/%d" % (2 + g), "vaug/%d/%d" % (l, 1 - par)], ["pCD"])
            A("dve", lambda e: e.tensor_tensor(out=sg[:, 0:8], in0=ov[:, :, 64], in1=esink[:, l, :], op=ALU.add),
              ["pCD", "esink"], ["sg"])
            A("dve", lambda e: e.reciprocal(out=sg[:, 0:8], in_=sg[:, 0:8]), ["sg"], ["sg"])
            A("dve", lambda e: e.tensor_tensor(out=atmp[:, :, :], in0=ov[:, :, 0:64],
                                               in1=sg[:, 0:8].unsqueeze(2).to_broadcast([128, 8, 64]), op=ALU.mult),
              ["pCD", "sg"], ["atmp"])
            A("dve", lambda e: e.tensor_tensor(out=hm[:, 256:768], in0=atmp[:, :, :].rearrange("p h d -> p (h d)"),
                                                in1=gates[:, 256:768], op=ALU.mult), ["atmp", "gates/1"], ["hm/1"])


        def chunk_tail(s, c, l, gi, between=None):
            xt = xg[:, gi, :]
            xtok = "xg/%d" % gi
            wout_apply(xt, 128, xtok, between)
            if l == 1:
                dma(yp[s, c * 128:(c + 1) * 128, :], xt, r=[xtok])

        xs_done = set()

        def rwkv_xs(s, c, l, phase=2):
            first = (c == 0)
            last = (c == nchunks - 1)
            pl = prevL[:, l, :]
            ptok = "prevL/%d" % l
            if phase != 1:
                if first:
                    A("pool", lambda e: e.memset(prevL[0:1, l, :], 0.0), [], [ptok + "/r0"])
                    A("pool", lambda e: e.memset(STs[:, l], 0.0), [], ["STs/%d" % l])
                dma(prevL[1:128, l, :], rin[0:127, :], r=["rin"], w=[ptok + "/rest"])
                if last:
                    dma(shp[l, s:s + 1, :], rin[127:128, :], r=["rin"])
            if phase == 0:
                return
            A("dve", lambda e: e.tensor_tensor(out=xsb[:, :], in0=pl, in1=rin[:, :], op=ALU.subtract),
              [ptok, "rin"], ["xsb"])
            A("dve", lambda e: e.tensor_tensor(out=xsb[:, :], in0=xsb[:, :], in1=muT[:, :], op=ALU.mult),
              ["xsb", "muT"], ["xsb"])
            A("dve", lambda e: e.tensor_tensor(out=xsb[:, :], in0=xsb[:, :], in1=rin[:, :], op=ALU.add),
              ["xsb", "rin"], ["xsb"])
            xs_done.add((s, c, l))

        def rwkv_chunk(s, c, l, first, last):
            par = c % 2
            pl = prevL[:, l, :]
            ptok = "prevL/%d" % l
            if (s, c, l) not in xs_done:
                rwkv_xs(s, c, l)
            yield
            dma(prevL[0:1, l, :], rin[127:128, :], r=["rin", ptok], w=[ptok + "/r0"])
            r_ = xsb[:, 0:256]
            k_ = xsb[:, 256:512]
            v_ = xsb[:, 512:768]
            E1, E2, E3, KK, KM, AT, BT, KT, RT, TMP = [ra[:, i, :] for i in range(10)]
            tk = lambda i: "ra/%d" % i
            A("pe", lambda e: e.transpose(out=pA[0:32, 0:128], in_=xsb[:, 768:800], identity=ident[:, :]),
              ["xsb", "ident"], ["pA"])
            A("pe", lambda e: e.transpose(out=pA[0:32, 128:256], in_=xsb[:, 800:832], identity=ident[:, :]),
              ["xsb", "ident"], ["pA"])
            yield
            A("act", lambda e: e.activation(out=lwin[0:32, 0, :], in_=pA[0:32, 0:128], func=AF.Exp, scale=2.0),
              ["pA"], ["lwin/w"])
            A("dve", lambda e: e.tensor_scalar(out=lwin[0:32, 0, :], in0=lwin[0:32, 0, :], scalar1=1.0, scalar2=None,
                                               op0=ALU.add), ["lwin/w"], ["lwin/w"])
            A("dve", lambda e: e.reciprocal(out=lwin[0:32, 0, :], in_=lwin[0:32, 0, :]), ["lwin/w"], ["lwin/w"])
            A("dve", lambda e: e.tensor_scalar(out=lwin[0:32, 0, :], in0=lwin[0:32, 0, :], scalar1=-2.0, scalar2=1.0,
                                               op0=ALU.mult, op1=ALU.add), ["lwin/w"], ["lwin/w"])
            A("act", lambda e: e.activation(out=lwin[0:32, 1, :], in_=pA[0:32, 128:256], func=AF.Copy),
              ["pA"], ["lwin/a"])
            for j in range(2):
                A("pe", lambda e, j=j: e.matmul(pB[:, j * 256:(j + 1) * 256], lhsT=lwin[0:33, j, :], rhs=lora[0:33, l, j, :],
                                                start=True, stop=True), ["lwin", "lora"], ["pB"])
            yield
            sigmoid_act(sg[:, :], pB[:, :], ["pB"], ["sg"])
            lw = sg[:, 0:256]
            a_ = sg[:, 256:512]
            yield
            A("pe", lambda e: e.matmul(pA[:, 0:256], lhsT=tmk[:, :], rhs=lw, start=True, stop=True),
              ["tmk", "sg"], ["pA"])
            for h in range(4):
                A("pe", lambda e, h=h: e.matmul(pB[0:64, 2 * h:2 * h + 2], lhsT=sg[:, h * 64:(h + 1) * 64],
                                                rhs=coefs[:, :], start=True, stop=True), ["sg", "coefs"], ["pB"])
            A("act", lambda e: e.activation(out=edge[:, :, 0], in_=pB[0:64, 0:8].rearrange("p (h x) -> p h x", x=2)[:, :, 0],
                                            func=AF.Exp, scale=-1.0), ["pB"], ["edge/0"])
            A("act", lambda e: e.activation(out=edge[:, :, 1], in_=pB[0:64, 0:8].rearrange("p (h x) -> p h x", x=2)[:, :, 1],
                                            func=AF.Exp), ["pB"], ["edge/1"])
            A("act", lambda e: e.activation(out=E1, in_=pA[:, 0:256], func=AF.Exp), ["pA"], [tk(0)])
            A("act", lambda e: e.activation(out=E2, in_=pA[:, 0:256], func=AF.Exp, scale=-1.0), ["pA"], [tk(1)])
            A("dve", lambda e: e.scalar_tensor_tensor(out=E3, in0=lw, scalar=KAP, in1=pA[:, 0:256], op0=ALU.mult,
                                                      op1=ALU.add), ["sg", "pA"], [tk(2)])
            A("act", lambda e: e.activation(out=E3, in_=E3, func=AF.Exp), [tk(2)], [tk(2)])
            yield
            A("dve", lambda e: e.tensor_tensor(out=KK, in0=k_, in1=rows[:, 0, :], op=ALU.mult), ["xsb", "rows/0"], [tk(3)])
            A("dve", lambda e: e.tensor_tensor(out=TMP, in0=KK, in1=KK, op=ALU.mult), [tk(3)], [tk(9)])
            A("dve", lambda e: e.tensor_reduce(out=stat[:, 11:15], in_=TMP.rearrange("p (h d) -> p h d", d=64), axis=AX.X,
                                               op=ALU.add), [tk(9)], ["stat/3"])
            A("dve", lambda e: e.tensor_scalar(out=stat[:, 11:15], in0=stat[:, 11:15], scalar1=1e-18, scalar2=None,
                                               op0=ALU.max), ["stat/3"], ["stat/3"])
            A("act", lambda e: e.activation(out=stat[:, 11:15], in_=stat[:, 11:15], func=AF.Ln), ["stat/3"], ["stat/3"])
            A("act", lambda e: e.activation(out=stat[:, 11:15], in_=stat[:, 11:15], func=AF.Exp, scale=-0.5),
              ["stat/3"], ["stat/3"])
            A("dve", lambda e: e.tensor_tensor(out=KK.rearrange("p (h d) -> p h d", d=64),
                                               in0=KK.rearrange("p (h d) -> p h d", d=64),
                                               in1=stat[:, 11:15].unsqueeze(2).to_broadcast([128, 4, 64]), op=ALU.mult),
              [tk(3), "stat/3"], [tk(3)])
            yield
            A("dve", lambda e: e.scalar_tensor_tensor(out=KM, in0=a_, scalar=-1.0, in1=rows[:, 1, :], op0=ALU.add,
                                                      op1=ALU.mult), ["sg", "rows/1"], [tk(4)])
            A("dve", lambda e: e.scalar_tensor_tensor(out=KM, in0=KM, scalar=1.0, in1=k_, op0=ALU.add, op1=ALU.mult),
              [tk(4), "xsb"], [tk(4)])
            yield
            A("dve", lambda e: e.scalar_tensor_tensor(out=AT, in0=KK, scalar=-1.0, in1=E3, op0=ALU.mult, op1=ALU.mult),
              [tk(3), tk(2)], [tk(5)])
            A("pool", lambda e: e.tensor_tensor(out=BT, in0=KK, in1=a_, op=ALU.mult), [tk(3), "sg"], [tk(6)])
            A("pool", lambda e: e.tensor_tensor(out=BT, in0=BT, in1=E2, op=ALU.mult), [tk(6), tk(1)], [tk(6)])
            A("pool", lambda e: e.tensor_tensor(out=KT, in0=KM, in1=E2, op=ALU.mult), [tk(4), tk(1)], [tk(7)])
            A("dve", lambda e: e.tensor_tensor(out=RT, in0=r_, in1=E1, op=ALU.mult), ["xsb", tk(0)], [tk(8)])
            yield
            for ai, (src, stok) in enumerate(((AT, tk(5)), (RT, tk(8)), (BT, tk(6)), (KT, tk(7)))):
                for h in range(4):
                    A("pe", lambda e, h=h, src=src: e.transpose(out=pA[0:64, h * 128:(h + 1) * 128],
                                                                 in_=src[:, h * 64:(h + 1) * 64], identity=ident[:, :]),
                      [stok, "ident"], ["pA"])
                eng = "act" if ai % 2 == 0 else "dve"
                if eng == "act":
                    A("act", lambda e, ai=ai: e.activation(out=XT[:, :, ai, :], in_=pA[0:64, :].rearrange("p (h t) -> p h t", h=4),
                                                           func=AF.Copy), ["pA"], ["XT/%d" % ai])
                else:
                    A("dve", lambda e, ai=ai: e.tensor_copy(out=XT[:, :, ai, :], in_=pA[0:64, :].rearrange("p (h t) -> p h t", h=4)),
                      ["pA"], ["XT/%d" % ai])
            yield
            R, RTm, TM_, AKA, ABR, AKR = [AM[:, i] for i in range(6)]
            for h in range(4):
                A("pe", lambda e, h=h: e.matmul(pCD[:, h, :], lhsT=XT[:, h, 2, :], rhs=XT[:, h, 0:2, :], start=True, stop=True),
                  ["XT"], ["pCD"])
            A("dve", lambda e: e.tensor_tensor(out=R, in0=pCD[:, :, 0:128], in1=msi[:, 0:128].unsqueeze(1).to_broadcast([128, 4, 128]),
                                               op=ALU.mult), ["pCD", "msi"], ["AM/0"])
            A("dve", lambda e: e.tensor_tensor(out=ABR, in0=pCD[:, :, 128:256], in1=msi[:, 128:256].unsqueeze(1).to_broadcast([128, 4, 128]),
                                               op=ALU.mult), ["pCD", "msi"], ["AM/4"])
            for h in range(4):
                A("pe", lambda e, h=h: e.matmul(pCD[:, h, :], lhsT=XT[:, h, 3, :], rhs=XT[:, h, 0:2, :], start=True, stop=True),
                  ["XT"], ["pCD"])
            A("dve", lambda e: e.tensor_tensor(out=AKA, in0=pCD[:, :, 0:128], in1=msi[:, 0:128].unsqueeze(1).to_broadcast([128, 4, 128]),
                                               op=ALU.mult), ["pCD", "msi"], ["AM/3"])
            A("dve", lambda e: e.tensor_tensor(out=AKR, in0=pCD[:, :, 128:256], in1=msi[:, 128:256].unsqueeze(1).to_broadcast([128, 4, 128]),
                                               op=ALU.mult), ["pCD", "msi"], ["AM/5"])
            for h in range(4):
                A("pe", lambda e, h=h: e.matmul(pE[:, h, :], lhsT=XT[:, h, 0, :], rhs=XT[:, h, 2, :], start=True, stop=True),
                  ["XT"], ["pE"])
            A("dve", lambda e: e.tensor_tensor(out=RTm, in0=pE[:, :, :], in1=mst[:, :].unsqueeze(1).to_broadcast([128, 4, 128]),
                                               op=ALU.mult), ["pE", "mst"], ["AM/1"])
            A("pool", lambda e: e.tensor_tensor(out=TM_, in0=R, in1=ident[:, :].unsqueeze(1).to_broadcast([128, 4, 128]),
                                                op=ALU.add), ["AM/0", "ident"], ["AM/2"])
            yield
            for j in range(1, 8):
                yield
                for p in range(2):
                    hs = (2 * p, 2 * p + 1)
                    a0, a1, a2 = "AM/0/%d" % p, "AM/1/%d" % p, "AM/2/%d" % p
                    pbk = (pA, pB)[p]
                    pbt = "pA" if p == 0 else "pB"
                    pbv_ = pbk[:, 0:256].rearrange("p (h t) -> p h t", h=2)
                    for h in hs:
                        if j == 1:
                            A("pe", lambda e, h=h: e.matmul(pCD[:, h, 0:128], lhsT=RTm[:, h, :], rhs=R[:, h, :], start=True, stop=True),
                              [a1, a0], ["pCD/%d" % p])
                        else:
                            A("pe", lambda e, h=h: e.matmul(pCD[:, h, :], lhsT=RTm[:, h, :], rhs=AM[:, 0:3:2, h, :], start=True, stop=True),
                              [a1, a0, a2], ["pCD/%d" % p])
                    if j < 7:
                        for h in hs:
                            A("pe", lambda e, h=h, pbk=pbk: e.matmul(pbk[:, (h % 2) * 128:(h % 2 + 1) * 128], lhsT=R[:, h, :], rhs=RTm[:, h, :],
                                                                    start=True, stop=True), [a1, a0], [pbt])
                    if j > 1:
                        A("dve", lambda e, p=p: e.tensor_tensor(out=TM_[:, 2 * p:2 * p + 2, :], in0=pCD[:, 2 * p:2 * p + 2, 128:256],
                                                                in1=TM_[:, 2 * p:2 * p + 2, :], op=ALU.add), ["pCD/%d" % p, a2], [a2])
                    if j < 7:
                        A("act", lambda e, p=p: e.activation(out=R[:, 2 * p:2 * p + 2, :], in_=pCD[:, 2 * p:2 * p + 2, 0:128], func=AF.Copy),
                          ["pCD/%d" % p], [a0])
                        A("dve", lambda e, p=p, pbv_=pbv_: e.tensor_copy(out=RTm[:, 2 * p:2 * p + 2, :], in_=pbv_), [pbt], [a1])
            yield
            stl = STs[:, l]
            sttok = "STs/%d" % l
            A("dve", lambda e: e.tensor_tensor(out=Stt[:, :, :], in0=stl, in1=edge[:, :, 0:1].to_broadcast([64, 4, 64]), op=ALU.mult),
              [sttok, "edge/0"], ["Stt"])
            gv = pA[:, 0:256].rearrange("p (h v) -> p h v", h=4)
            for h in range(4):
                A("pe", lambda e, h=h: e.matmul(gv[:, h, :], lhsT=XT[:, h, 0, :], rhs=Stt[:, h, :], start=True, stop=False),
                  ["XT/0", "Stt"], ["pA"])
                A("pe", lambda e, h=h: e.matmul(gv[:, h, :], lhsT=AKA[:, h, :], rhs=v_[:, h * 64:(h + 1) * 64], start=False, stop=True),
                  ["AM/3", "xsb"], ["pA"])
            A("act", lambda e: e.activation(out=Gs[:, :, :], in_=gv, func=AF.Copy), ["pA"], ["Gs"])
            uv = pB[:, 0:256].rearrange("p (h v) -> p h v", h=4)
            for h in range(4):
                A("pe", lambda e, h=h: e.matmul(uv[:, h, :], lhsT=TM_[:, h, :], rhs=Gs[:, h, :], start=True, stop=True),
                  ["AM/2", "Gs"], ["pB"])
            A("act", lambda e: e.activation(out=Us[:, :, :], in_=uv, func=AF.Copy), ["pB"], ["Us"])
            ovr = pA[:, 256:512].rearrange("p (h v) -> p h v", h=4)
            for h in range(4):
                A("pe", lambda e, h=h: e.matmul(ovr[:, h, :], lhsT=XT[:, h, 1, :], rhs=Stt[:, h, :], start=True, stop=False),
                  ["XT/1", "Stt"], ["pA/o"])
                A("pe", lambda e, h=h: e.matmul(ovr[:, h, :], lhsT=ABR[:, h, :], rhs=Us[:, h, :], start=False, stop=False),
                  ["AM/4", "Us"], ["pA/o"])
                A("pe", lambda e, h=h: e.matmul(ovr[:, h, :], lhsT=AKR[:, h, :], rhs=v_[:, h * 64:(h + 1) * 64], start=False, stop=True),
                  ["AM/5", "xsb"], ["pA/o"])
            scv = pB[0:64, 256:512].rearrange("p (h v) -> p h v", h=4)
            for h in range(4):
                A("pe", lambda e, h=h: e.matmul(scv[:, h, :], lhsT=BT[:, h * 64:(h + 1) * 64], rhs=Us[:, h, :], start=True, stop=False),
                  [tk(6), "Us"], ["pB/s"])
                A("pe", lambda e, h=h: e.matmul(scv[:, h, :], lhsT=KT[:, h * 64:(h + 1) * 64], rhs=v_[:, h * 64:(h + 1) * 64], start=False, stop=True),
                  [tk(7), "xsb"], ["pB/s"])
            A("dve", lambda e: e.tensor_tensor(out=Stt[:, :, :], in0=scv, in1=Stt[:, :, :], op=ALU.add), ["pB/s", "Stt"], ["Stt"])
            A("dve", lambda e: e.tensor_tensor(out=stl, in0=Stt[:, :, :], in1=edge[:, :, 1:2].to_broadcast([64, 4, 64]), op=ALU.mult),
              ["Stt", "edge/1"], [sttok])
            yield
            if last:
                for h in range(4):
                    A("pe", lambda e, h=h: e.transpose(out=pE[0:64, h, 0:64], in_=STs[:, l, h, :], identity=ident[0:64, 0:64]),
                      [sttok, "ident"], ["pE"])
                A("act", lambda e: e.activation(out=Gs[0:64, :, :], in_=pE[0:64, :, 0:64], func=AF.Copy), ["pE"], ["Gs"])
                dma(wkp[l, s].rearrange("h v k -> v h k"), Gs[0:64, :, :], r=["Gs"])
            yield
            rwkv_post(128, pA[:, 256:512], "pA/o", r_, KM, v_, ["xsb"], [tk(4)], gates2[:, par, :], "gates2/%d" % par)

        def rwkv_post(np_, ops, opstok, r_, KM, v_, xtoks, kmtoks, g2ap, g2tok):
            tk = lambda i: "ra/%d" % i
            O = ra[0:np_, 0, :]
            E2 = ra[0:np_, 1, :]
            st_ = lambda a_, b_: stat[0:np_, a_:b_]
            bc = lambda ap_: ap_.unsqueeze(2).to_broadcast([np_, 4, 64])
            A("act", lambda e: e.activation(out=O, in_=ops, func=AF.Copy), [opstok], [tk(0)])
            O3 = O.rearrange("p (h d) -> p h d", d=64)
            A("dve", lambda e: e.tensor_reduce(out=st_(16, 20), in_=O3, axis=AX.X, op=ALU.add), [tk(0)], ["stat/gm"])
            A("pool", lambda e: e.tensor_tensor(out=E2, in0=O, in1=O, op=ALU.mult), [tk(0)], [tk(1)])
            A("dve", lambda e: e.tensor_reduce(out=st_(20, 24), in_=E2.rearrange("p (h d) -> p h d", d=64), axis=AX.X, op=ALU.add),
              [tk(1)], ["stat/gv"])
            A("dve", lambda e: e.tensor_scalar(out=st_(16, 20), in0=st_(16, 20), scalar1=1.0 / 64, scalar2=None, op0=ALU.mult),
              ["stat/gm"], ["stat/gm"])
            A("dve", lambda e: e.tensor_tensor(out=st_(24, 28), in0=st_(16, 20), in1=st_(16, 20), op=ALU.mult),
              ["stat/gm"], ["stat/gt"])
            A("dve", lambda e: e.scalar_tensor_tensor(out=st_(20, 24), in0=st_(20, 24), scalar=1.0 / 64, in1=st_(24, 28),
                                                      op0=ALU.mult, op1=ALU.subtract), ["stat/gv", "stat/gt"], ["stat/gv"])
            rsqrt_small(st_(20, 24), st_(20, 24), 4, 1.0, GN_EPS, ["stat/gv"], ["stat/gv"])
            A("dve", lambda e: e.tensor_tensor(out=O3, in0=O3, in1=bc(st_(16, 20)), op=ALU.subtract), [tk(0), "stat/gm"], [tk(0)])
            A("dve", lambda e: e.tensor_tensor(out=O3, in0=O3, in1=bc(st_(20, 24)), op=ALU.mult), [tk(0), "stat/gv"], [tk(0)])
            A("dve", lambda e: e.tensor_tensor(out=O, in0=O, in1=rows[0:np_, 3, :], op=ALU.mult), [tk(0), "rows/3"], [tk(0)])
            A("dve", lambda e: e.tensor_tensor(out=O, in0=O, in1=rows[0:np_, 4, :], op=ALU.add), [tk(0), "rows/4"], [tk(0)])
            A("pool", lambda e: e.tensor_tensor(out=E2, in0=r_, in1=KM, op=ALU.mult), xtoks + kmtoks, [tk(1)])
            A("pool", lambda e: e.tensor_tensor(out=E2, in0=E2, in1=rows[0:np_, 2, :], op=ALU.mult), [tk(1), "rows/2"], [tk(1)])
            A("dve", lambda e: e.tensor_reduce(out=st_(24, 28), in_=E2.rearrange("p (h d) -> p h d", d=64), axis=AX.X, op=ALU.add),
              [tk(1)], ["stat/gt"])
            A("dve", lambda e: e.tensor_tensor(out=E2.rearrange("p (h d) -> p h d", d=64), in0=v_.rearrange("p (h d) -> p h d", d=64),
                                               in1=bc(st_(24, 28)), op=ALU.mult), xtoks + ["stat/gt"], [tk(1)])
            A("dve", lambda e: e.tensor_tensor(out=O, in0=O, in1=E2, op=ALU.add), [tk(0), tk(1)], [tk(0)])
            A("dve", lambda e: e.tensor_tensor(out=hm[0:np_, 768:1024], in0=O, in1=g2ap, op=ALU.mult),
              [tk(0), g2tok], ["hm/2"])

        xsmp = T("xsmp", [SB, D])
        smc = T("smc", [128, 64])
        indBT = T("indBT", [SB, 128])
        ropes = T("ropes", [128, 2, 32])
        esinkP = T("esinkP", [128, 2, 2])
        dma(xgf[:, 3584:3648], C["smc"][:, :], w=["xg"])
        dma(xgf[0:SB, 3712:3840], C["indBT"][:, :], w=["xg"])
        dma(ropes[:, :, :], C["ropes"][:, :, :], w=["ropes"])
        for l in range(2):
            for dd in range(2):
                for g in range(2):
                    dma(esinkP[g * 64:(g + 1) * 64, l, dd:dd + 1], P["attn_sinks"][l, g * 4:(g + 1) * 4].partition_broadcast(16), w=["esinkP"])
        A("act", lambda e: e.activation(out=esinkP[:, :, :], in_=esinkP[:, :, :], func=AF.Exp), ["esinkP"], ["esinkP"])
        A("act", lambda e: e.activation(out=smc[:, :], in_=xgf[:, 3584:3648], func=AF.Copy), ["xg"], ["smc"])
        A("act", lambda e: e.activation(out=indBT[:, :], in_=xgf[0:SB, 3712:3840], func=AF.Copy), ["xg"], ["indBT"])
        hmask8 = smc[:, 0:8]
        indB = smc[:, 8:24]
        gsel = smc[:, 24:26]
        hm4 = smc[:, 26:30]
        vhm = smc[:, 30:32]
        hmask8r = smc[:, 32:40]
        indBr = smc[:, 40:56]

        def sample_layer(l):
            NP = SB
            xt = xsmp[0:NP, :]
            xtok = "xsmp"
            tk = lambda i: "ra/%d" % i
            if l == 0:
                dma(xt, xs_in[:, :], w=[xtok])
            rmsnorm_T(xt, NP, xtok)
            cp = lambda o, i, r, w: A("act", lambda e: e.activation(out=o, in_=i, func=AF.Copy), r, w)
            hd = lambda ap_: ap_.rearrange("p (h d) -> p h d", d=64)
            us = Gs[0:NP, :, :].rearrange("p h d -> p (h d)")
            win_group(pz[0], "pz0", OU, 512, NP)
            cp(us, pz[0][0:NP, 0:256], ["pz0"], ["Gs"])
            silu_to(gates[0:NP, 0:256], pz[0][0:NP, 256:512], 256, ra[0:NP, 9, :], ["pz0"], ["gates/0"], ["ra/9"])
            win_group(pz[1], "pz1", OQ, 512, NP, perm=True)
            cp(qk[0:NP, 0:8, :], hd(pz[1][0:NP, :]), ["pz1"], ["qk"])
            win_group(pz[0], "pz0", OKK, 256, NP)
            cp(qk[0:NP, 8:10, :], hd(pz[0][0:NP, 0:128]), ["pz0"], ["qk"])
            cp(atmp[0:NP, 0:2, :], hd(pz[0][0:NP, 128:256]), ["pz0"], ["atmp"])
            win_group(pz[1], "pz1", OGA, 512, NP)
            silu_to(gates[0:NP, 256:768], pz[1][0:NP, :], 512, sg[0:NP, :], ["pz1"], ["gates/1"], ["sg"])
            win_group(pz[0], "pz0", ORIN, 512, NP)
            cp(rin[0:NP, 0:512], pz[0][0:NP, :], ["pz0"], ["rin/a"])
            win_group(pz[1], "pz1", ORIN + 512, 320, NP)
            A("dve", lambda e: e.tensor_copy(out=rin[0:NP, 512:832], in_=pz[1][0:NP, 0:320]), ["pz1"], ["rin/b"])
            win_group(pz[0], "pz0", OGR, 256, NP)
            silu_to(gates[0:NP, 768:1024], pz[0][0:NP, 0:256], 256, ra[0:NP, 9, :], ["pz0"], ["gates/2"], ["ra/9"])
            if sstage <= 1:
                return
            AMf = xgf[:, 2048:4096]
            pooled = sg[0:NP, 0:256]
            off = 0
            for g, w in enumerate((2, 4, 8, 16)):
                n = (w - 1) * 64
                bufv = AMf[0:NP, off:off + n].rearrange("p (r c) -> p r c", c=64)
                dma(bufv, spool[l, :, 15 - (w - 1):15, g * 64:(g + 1) * 64], w=["xg"])
                A("dve", lambda e, g=g, bufv=bufv: e.tensor_reduce(out=pooled[:, g * 64:(g + 1) * 64],
                                                                   in_=bufv.rearrange("p r c -> p c r"), axis=AX.X, op=ALU.add),
                  ["xg"], ["sg"])
                A("dve", lambda e, g=g, w=w: e.tensor_scalar(out=pooled[:, g * 64:(g + 1) * 64], in0=pooled[:, g * 64:(g + 1) * 64],
                                                             scalar1=1.0 / w, scalar2=None, op0=ALU.mult), ["sg"], ["sg"])
                A("dve", lambda e, g=g, w=w: e.scalar_tensor_tensor(out=pooled[:, g * 64:(g + 1) * 64], in0=us[:, g * 64:(g + 1) * 64],
                                                                    scalar=(1.0 / w - 1.0), in1=pooled[:, g * 64:(g + 1) * 64],
                                                                    op0=ALU.mult, op1=ALU.add), ["sg", "Gs"], ["sg"])
                off += n
            dma(pools[l, :, 0:14, :], spool[l, :, 1:15, :])
            dma(pools[l, :, 14, :], us, r=["Gs"])
            for g in range(4):
                A("pe", lambda e, g=g: e.transpose(out=pA[0:64, g * 128:g * 128 + NP], in_=pooled[:, g * 64:(g + 1) * 64],
                                                   identity=ident[0:NP, 0:NP]), ["sg", "ident"], ["pA"])
            cp(pTsb[:, :].rearrange("p (g t) -> p g t", g=4)[:, :, 0:NP], pA[0:64, :].rearrange("p (g t) -> p g t", g=4)[:, :, 0:NP],
               ["pA"], ["pTsb"])
            for g in range(4):
                A("pe", lambda e, g=g: e.matmul(pB[0:NP, g * 64:(g + 1) * 64], lhsT=pTsb[:, g * 128:g * 128 + NP],
                                                rhs=poolw[:, l, g, :], start=True, stop=True), ["pTsb", "poolw"], ["pB"])
            A("dve", lambda e: e.tensor_tensor(out=hm[0:NP, 0:256], in0=pB[0:NP, 0:256], in1=gates[0:NP, 0:256], op=ALU.mult),
              ["pB", "gates/0"], ["hm/0"])
            if sstage <= 2:
                return
            qk_norm_rope(NP, l, ropes[0:NP, 0, :], ropes[0:NP, 1, :], "ropes")
            Kc = xgf[:, 0:2048].rearrange("p (b c) -> p b c", c=128)
            Vst = xgf[:, 2048:4096].rearrange("p (b c) -> p b c", c=128)
            Vc = ra[:, 0:8, :].rearrange("p a c -> p (a c)").rearrange("p (b c) -> p b c", c=128)
            dma(Kc[0:127, :, :], ck[l].rearrange("b j c -> j b c")[1:128], w=["xg"])
            dma(Kc[127:128, :, :], qk[0:NP, 8:10, :], r=["qk"], w=["xg"])
            dma(ksm[l].rearrange("b j c -> j b c"), Kc, r=["xg"])
            dma(Vst[0:127, :, :], cv[l].rearrange("b j c -> j b c")[1:128], w=["xg"])
            dma(Vst[127:128, :, :], atmp[0:NP, 0:2, :], r=["atmp"], w=["xg"])
            dma(vsm[l].rearrange("b j c -> j b c"), Vst, r=["xg"])
            A("act", lambda e: e.activation(out=Vc[:, 0:8, :], in_=Vst[:, 0:8, :], func=AF.Copy), ["xg"], ["ra"])
            A("dve", lambda e: e.tensor_copy(out=Vc[:, 8:16, :], in_=Vst[:, 8:16, :]), ["xg"], ["ra"])
            if sstage <= 2.2:
                return
            KTv = PT[:, :, :].rearrange("p a (b t) -> p (a b) t", t=128)
            pcdv = pCD[:, :, :].rearrange("p a (b c) -> p (a b) c", b=2)
            pbv = pB[:, :].rearrange("p (a c) -> p a c", a=4)
            for b in range(SB):
                dst = pcdv[:, b, :] if b < 8 else (pE[:, b - 8, :] if b < 12 else pbv[:, b - 12, :])
                dtok = "pCD" if b < 8 else ("pE" if b < 12 else "pB")
                A("pe", lambda e, b=b, dst=dst: e.transpose(out=dst, in_=Kc[:, b, :], identity=ident[:, :]), ["xg", "ident"], [dtok])
            cp(KTv[:, 0:4, :], pcdv[:, 0:4, :], ["pCD"], ["PT"])
            cp(KTv[:, 4:8, :], pcdv[:, 4:8, :], ["pCD"], ["PT"])
            A("dve", lambda e: e.tensor_copy(out=KTv[:, 8:12, :], in_=pE[:, :, :]), ["pE"], ["PT"])
            cp(KTv[:, 12:16, :], pbv, ["pB"], ["PT"])
            if sstage <= 2.4:
                return
            for t_ in range(4):
                A("pe", lambda e, t_=t_: e.transpose(out=pA[:, t_ * 128:t_ * 128 + NP],
                                                     in_=qk[0:NP, 2 * t_:2 * t_ + 2, :].rearrange("p h d -> p (h d)"),
                                                     identity=ident[0:NP, 0:NP]), ["qk", "ident"], ["pA"])
            cp(qT[:, :, 0:NP], pA[:, :].rearrange("p (t q) -> p t q", t=4)[:, :, 0:NP], ["pA"], ["qT"])
            for b in range(SB):
                for g in range(2):
                    A("pe", lambda e, b=b, g=g: e.matmul((pA, pB)[g][:, b * 4:b * 4 + 4], lhsT=KTv[g * 64:(g + 1) * 64, b, :],
                                                         rhs=qT[g * 64:(g + 1) * 64, :, b], start=True, stop=True),
                      ["PT", "qT"], ["pA" if g == 0 else "pB"])
            PTs = qk2[:, :, :].rearrange("p h d -> p (h d)")[:, 0:128]
            A("act", lambda e: e.activation(out=PTs[:, 0:64], in_=pA[:, 0:64], func=AF.Exp), ["pA"], ["qk2"])
            A("act", lambda e: e.activation(out=PTs[:, 64:128], in_=pB[:, 0:64], func=AF.Exp), ["pB"], ["qk2"])
            if sstage <= 2.6:
                return
            sgv = sg[:, :].rearrange("p (i c) -> p i c", i=4)
            ones2 = Us[:, 2, 2:4]
            A("pool", lambda e: e.memset(ones2, 1.0), [], ["Us/2"])
            A("pe", lambda e: e.matmul(pz[1][:, 128:130], lhsT=PTs, rhs=ones2, start=True, stop=True), ["qk2", "Us/2"], ["pz1"])
            for r_ in range(4):
                A("pool", lambda e: e.memset(sg[:, :], 0.0), [], ["sg"])
                base = sg[:, 16 * r_:16 * r_ + 4]
                diag = bass.AP(base.tensor, base.offset, [list(base.ap[0]), [132, 4], [64, 2], [1, 4]])
                pb_ = PTs[:, 16 * r_:16 * r_ + 4]
                src = bass.AP(pb_.tensor, pb_.offset, [list(pb_.ap[0]), [4, 4], [64, 2], [1, 4]])
                A("dve", lambda e, diag=diag, src=src: e.tensor_copy(out=diag, in_=src), ["qk2"], ["sg"])
                for i in range(4):
                    b = 4 * r_ + i
                    if sstage <= 2.7:
                        continue
                    A("pe", lambda e, b=b, i=i: e.matmul(pz[1][:, 0:128], lhsT=sgv[:, i, :], rhs=Vc[:, b, :], start=(b == 0), stop=(b == SB - 1)),
                      ["sg", "ra"], ["pz1"])
            den2 = Us[:, 2, 0:2]
            den = Us[:, 2, 0:1]
            if sstage <= 2.8:
                return
            t64 = Us[:, 3, :]
            A("dve", lambda e: e.tensor_tensor(out=den2, in0=pz[1][:, 128:130], in1=esinkP[:, l, :], op=ALU.add), ["pz1", "esinkP"], ["Us/2"])
            A("dve", lambda e: e.reciprocal(out=den2, in_=den2), ["Us/2"], ["Us/2"])
            if sstage <= 2.85:
                return
            A("dve", lambda e: e.tensor_scalar(out=t64, in0=pz[1][:, 0:64], scalar1=gsel[:, 0:1], scalar2=None, op0=ALU.mult),
              ["pz1", "smc"], ["Us/3"])
            A("dve", lambda e: e.scalar_tensor_tensor(out=t64, in0=pz[1][:, 64:128], scalar=gsel[:, 1:2], in1=t64, op0=ALU.mult, op1=ALU.add),
              ["pz1", "smc", "Us/3"], ["Us/3"])
            A("dve", lambda e: e.tensor_scalar(out=t64, in0=t64, scalar1=den, scalar2=None, op0=ALU.mult), ["Us/3", "Us/2"], ["Us/3"])
            if sstage <= 2.9:
                return
            for h8 in range(8):
                A("dve", lambda e, h8=h8: e.tensor_scalar(out=atmp[:, h8, :], in0=t64, scalar1=hmask8[:, h8:h8 + 1], scalar2=None, op0=ALU.mult),
                  ["Us/3", "smc"], ["atmp"])
            if sstage <= 2.95:
                return
            A("pe", lambda e: e.matmul(pz[0][0:NP, :], lhsT=indB, rhs=atmp[:, :, :].rearrange("p h d -> p (h d)"), start=True, stop=True),
              ["smc", "atmp"], ["pz0"])
            A("dve", lambda e: e.tensor_tensor(out=hm[0:NP, 256:768], in0=pz[0][0:NP, :], in1=gates[0:NP, 256:768], op=ALU.mult),
              ["pz0", "gates/1"], ["hm/1"])
            if sstage <= 3:
                return
            dma(xgf[0:NP, 2048:2880], sshift[l], w=["xg"])
            dma(shs[l], rin[0:NP, :], r=["rin"])
            A("pool", lambda e: e.tensor_tensor(out=xsb[0:NP, :], in0=xgf[0:NP, 2048:2880], in1=rin[0:NP, :], op=ALU.subtract), ["xg", "rin"], ["xsb"])
            A("pool", lambda e: e.tensor_tensor(out=xsb[0:NP, :], in0=xsb[0:NP, :], in1=muT[0:NP, :], op=ALU.mult), ["xsb", "muT"], ["xsb"])
            A("dve", lambda e: e.tensor_tensor(out=xsb[0:NP, :], in0=xsb[0:NP, :], in1=rin[0:NP, :], op=ALU.add), ["xsb", "rin"], ["xsb"])
            r_s = xsb[0:NP, 0:256]
            k_s = xsb[0:NP, 256:512]
            v_s = xsb[0:NP, 512:768]
            A("pe", lambda e: e.transpose(out=pA[0:32, 0:NP], in_=xsb[0:NP, 768:800], identity=ident[0:NP, 0:NP]), ["xsb", "ident"], ["pA"])
            A("pe", lambda e: e.transpose(out=pA[0:32, 128:128 + NP], in_=xsb[0:NP, 800:832], identity=ident[0:NP, 0:NP]), ["xsb", "ident"], ["pA"])
            lw0 = lwin[0:32, 0, 0:NP]
            A("act", lambda e: e.activation(out=lw0, in_=pA[0:32, 0:NP], func=AF.Exp, scale=2.0), ["pA"], ["lwin/w"])
            A("dve", lambda e: e.tensor_scalar(out=lw0, in0=lw0, scalar1=1.0, scalar2=None, op0=ALU.add), ["lwin/w"], ["lwin/w"])
            A("dve", lambda e: e.reciprocal(out=lw0, in_=lw0), ["lwin/w"], ["lwin/w"])
            A("dve", lambda e: e.tensor_scalar(out=lw0, in0=lw0, scalar1=-2.0, scalar2=1.0, op0=ALU.mult, op1=ALU.add), ["lwin/w"], ["lwin/w"])
            cp(lwin[0:32, 1, 0:NP], pA[0:32, 128:128 + NP], ["pA"], ["lwin/a"])
            for j in range(2):
                A("pe", lambda e, j=j: e.matmul(pB[0:NP, j * 256:(j + 1) * 256], lhsT=lwin[0:33, j, 0:NP], rhs=lora[0:33, l, j, :],
                                                start=True, stop=True), ["lwin", "lora"], ["pB"])
            sgs = sg[0:NP, :]
            sigmoid_act(sgs, pB[0:NP, :], ["pB"], ["sg"])
            lw = sg[0:NP, 0:256]
            a_ = sg[0:NP, 256:512]
            vec6 = AM[0:NP, 0:3, :, :].rearrange("p a h t -> p (a h t)").rearrange("p (q c) -> p q c", c=256)
            VKK, VW, VKA, VKM, VR, VV = [vec6[:, i, :] for i in range(6)]
            A("act", lambda e: e.activation(out=VW, in_=lw, func=AF.Exp, scale=-KAP), ["sg"], ["AM/0"])
            A("pool", lambda e: e.tensor_tensor(out=VKK, in0=k_s, in1=rows[0:NP, 0, :], op=ALU.mult), ["xsb", "rows/0"], ["AM/0"])
            TMPs = qk3[0:NP, :, :].rearrange("p h d -> p (h d)")[:, 0:256]
            A("pool", lambda e: e.tensor_tensor(out=TMPs, in0=VKK, in1=VKK, op=ALU.mult), ["AM/0"], ["qk3"])
            skk = stat[0:NP, 11:15]
            A("dve", lambda e: e.tensor_reduce(out=skk, in_=hd(TMPs), axis=AX.X, op=ALU.add), ["qk3"], ["stat/3"])
            A("dve", lambda e: e.tensor_scalar(out=skk, in0=skk, scalar1=1e-18, scalar2=None, op0=ALU.max), ["stat/3"], ["stat/3"])
            A("act", lambda e: e.activation(out=skk, in_=skk, func=AF.Ln), ["stat/3"], ["stat/3"])
            A("act", lambda e: e.activation(out=skk, in_=skk, func=AF.Exp, scale=-0.5), ["stat/3"], ["stat/3"])
            A("dve", lambda e: e.tensor_tensor(out=hd(VKK), in0=hd(VKK), in1=skk.unsqueeze(2).to_broadcast([NP, 4, 64]), op=ALU.mult),
              ["AM/0", "stat/3"], ["AM/0"])
            A("pool", lambda e: e.tensor_tensor(out=VKA, in0=VKK, in1=a_, op=ALU.mult), ["AM/0", "sg"], ["AM/1"])
            A("dve", lambda e: e.scalar_tensor_tensor(out=VKM, in0=a_, scalar=-1.0, in1=rows[0:NP, 1, :], op0=ALU.add, op1=ALU.mult),
              ["sg", "rows/1"], ["AM/1"])
            A("dve", lambda e: e.scalar_tensor_tensor(out=VKM, in0=VKM, scalar=1.0, in1=k_s, op0=ALU.add, op1=ALU.mult),
              ["AM/1", "xsb"], ["AM/1"])
            KMs = qk3[0:NP, :, :].rearrange("p h d -> p (h d)")[:, 256:512]
            A("pool", lambda e: e.tensor_copy(out=KMs, in_=VKM), ["AM/1"], ["qk3"])
            A("pool", lambda e: e.tensor_copy(out=VR, in_=r_s), ["xsb"], ["AM/2"])
            A("pool", lambda e: e.tensor_copy(out=VV, in_=v_s), ["xsb"], ["AM/2"])
            v6f = vec6.rearrange("p q c -> p (q c)")
            pcdf = pCD[:, :, :].rearrange("p a c -> p (a c)")
            for i in range(3):
                dst = pcdf[:, i * 512:(i + 1) * 512] if i < 2 else pE[:, :, :].rearrange("p a c -> p (a c)")
                A("pe", lambda e, i=i, dst=dst: e.matmul(dst, lhsT=indBT[:, :], rhs=v6f[:, i * 512:(i + 1) * 512], start=True, stop=True),
                  ["indBT", "AM/0", "AM/1", "AM/2"], ["pCD" if i < 2 else "pE"])
            seltmp = AM[:, 3:6, :, :].rearrange("p a h t -> p (a h t)").rearrange("p (q h c) -> p q h c", h=4, c=64)
            hb = lambda nq: hm4.unsqueeze(1).unsqueeze(3).to_broadcast([128, nq, 4, 64])
            for bk in range(2):
                A("dve", lambda e, bk=bk: e.tensor_tensor(out=seltmp[:, 2 * bk:2 * bk + 2],
                                                          in0=pcdf[:, bk * 512:(bk + 1) * 512].rearrange("p (q h c) -> p q h c", h=4, c=64),
                                                          in1=hb(2), op=ALU.mult), ["pCD", "smc"], ["AM/%d" % (3 + bk)])
            A("dve", lambda e: e.tensor_tensor(out=seltmp[:, 4:6], in0=pE[:, :, :].rearrange("p a c -> p (a c)").rearrange("p (q h c) -> p q h c", h=4, c=64),
                                               in1=hb(2), op=ALU.mult), ["pE", "smc"], ["AM/5"])
            vsel = qk[:, 0:6, :]
            A("dve", lambda e: e.tensor_reduce(out=vsel, in_=seltmp.rearrange("p q h c -> p q c h"), axis=AX.X, op=ALU.add),
              ["AM/3", "AM/4", "AM/5"], ["qk"])
            v32 = Us[:, 2, 0:32]
            A("dve", lambda e: e.tensor_scalar(out=v32, in0=vsel[:, 5, 0:32], scalar1=vhm[:, 0:1], scalar2=None, op0=ALU.mult), ["qk", "smc"], ["Us/2"])
            A("dve", lambda e: e.scalar_tensor_tensor(out=v32, in0=vsel[:, 5, 32:64], scalar=vhm[:, 1:2], in1=v32, op0=ALU.mult, op1=ALU.add),
              ["qk", "smc", "Us/2"], ["Us/2"])
            Sst = xgf[:, 0:2048].rearrange("p (v k) -> p v k", k=64)
            tmpS = AM[:, 0:4, :, :].rearrange("p a h t -> p (a h t)").rearrange("p (v k) -> p v k", k=64)
            stok = ["xg"]
            ttok = ["AM/0", "AM/1", "AM/2", "AM/3"]
            dma(Sst, swkv[l].rearrange("b h (vh v) k -> (b h vh) v k", vh=2), w=stok)
            bv = lambda q: vsel[:, q, :].unsqueeze(1).to_broadcast([128, 32, 64])
            sa = Us[:, 0, 0:32]
            osm = Us[:, 1, 0:32]
            A("dve", lambda e: e.tensor_tensor(out=tmpS, in0=Sst, in1=bv(0), op=ALU.mult), stok + ["qk"], ttok)
            A("dve", lambda e: e.tensor_reduce(out=sa, in_=tmpS, axis=AX.X, op=ALU.add), ttok, ["Us/0"])
            A("pool", lambda e: e.tensor_tensor(out=Sst, in0=Sst, in1=bv(1), op=ALU.mult), stok + ["qk"], stok)
            A("dve", lambda e: e.scalar_tensor_tensor(out=tmpS, in0=tmpS, scalar=0.0, in1=bv(2), op0=ALU.mult, op1=ALU.add), ttok + ["qk"], ttok)
            A("dve", lambda e: e.tensor_tensor(out=tmpS, in0=tmpS, in1=sa.unsqueeze(2).to_broadcast([128, 32, 64]), op=ALU.mult),
              ttok + ["Us/0"], ttok)
            A("pool", lambda e: e.tensor_tensor(out=Sst, in0=Sst, in1=tmpS, op=ALU.subtract), stok + ttok, stok)
            A("dve", lambda e: e.scalar_tensor_tensor(out=tmpS, in0=tmpS, scalar=0.0, in1=bv(3), op0=ALU.mult, op1=ALU.add), ttok + ["qk"], ttok)
            A("dve", lambda e: e.tensor_tensor(out=tmpS, in0=tmpS, in1=v32.unsqueeze(2).to_broadcast([128, 32, 64]), op=ALU.mult),
              ttok + ["Us/2"], ttok)
            A("pool", lambda e: e.tensor_tensor(out=Sst, in0=Sst, in1=tmpS, op=ALU.add), stok + ttok, stok)
            dma(wks[l].rearrange("b h (vh v) k -> (b h vh) v k", vh=2), Sst, r=stok)
            A("dve", lambda e: e.tensor_tensor(out=tmpS, in0=Sst, in1=bv(4), op=ALU.mult), stok + ["qk"], ttok)
            A("dve", lambda e: e.tensor_reduce(out=osm, in_=tmpS, axis=AX.X, op=ALU.add), ttok, ["Us/1"])
            for h8 in range(8):
                A("dve", lambda e, h8=h8: e.tensor_scalar(out=atmp[:, h8, 0:32], in0=osm, scalar1=hmask8r[:, h8:h8 + 1], scalar2=None, op0=ALU.mult),
                  ["Us/1", "smc"], ["atmp"])
            A("pe", lambda e: e.matmul(pz[1][0:NP, 0:256], lhsT=indBr, rhs=atmp[:, :, 0:32], start=True, stop=True), ["smc", "atmp"], ["pz1"])
            rwkv_post(NP, pz[1][0:NP, 0:256], "pz1", r_s, KMs, v_s, ["xsb"], ["qk3"], gates[0:NP, 768:1024], "gates/2")
            wout_apply(xt, NP, xtok)
            if l == 1:
                dma(ys[:, :], xt, r=[xtok])

        load_layer(0, True)
        sample_done = [False, False]
        if do_sample:
            sample_layer(0)
            sample_done[0] = True
        def run_rr(gens):
            gens = [g for g in gens if g is not None]
            while gens:
                for g in list(gens):
                    try:
                        next(g)
                    except StopIteration:
                        gens.remove(g)

        ngrp = (nchunks + GRP - 1) // GRP
        units = [(s, grp, l) for s in range(nseq) for grp in range(ngrp) for l in range(2)]
        front_done = False
        for ui, (s, grp, l) in enumerate(units):
            if cur_layer[0] != l:
                load_layer(l, False)
            cs = [grp * GRP + gi for gi in range(GRP) if grp * GRP + gi < nchunks]
            if not front_done:
                run_rr([chunk_front(s, cs[0], l, 0)])
                chunk_qk(s, cs[0], l)
            front_done = False
            nu = units[ui + 1] if ui + 1 < len(units) else None
            for gi, c in enumerate(cs):
                chunk_mid(s, c, l, gi)
                if gi + 1 < len(cs):
                    nxt = chunk_front(s, cs[gi + 1], l, gi + 1)
                elif nu is not None and len(cs) > 1:
                    if win_loaded[0] != nu[2]:
                        reload_win(nu[2])
                    nxt = chunk_front(nu[0], nu[1] * GRP, nu[2], 0)
                    front_done = True
                else:
                    nxt = None
                    if nu is not None and win_loaded[0] != nu[2]:
                        reload_win(nu[2])
                run_rr([rwkv_chunk(s, c, l, c == 0, c == nchunks - 1), nxt])
                if gi + 2 == len(cs) and nu is not None:
                    reload_win(nu[2])
                if gi + 1 < len(cs):
                    btw = (lambda s=s, c2=cs[gi + 1], l=l: (rwkv_xs(s, c2, l, 0), chunk_qk(s, c2, l), rwkv_xs(s, c2, l, 1)))
                elif front_done:
                    btw = (lambda nu=nu: chunk_qk(nu[0], nu[1] * GRP, nu[2]))
                else:
                    btw = None
                chunk_tail(s, c, l, gi, btw)
        if do_sample:
            if cur_layer[0] != 1:
                load_layer(1, False)
            sample_layer(1)
        S.build()
        with nc.allow_low_precision("float32r (fp32 container, 11-bit mantissa) matmul operands"):
            S.emit(nc)
    return nc


_CACHE = {}


def kernel(**inputs):
    do_sample = True
    if "prog" not in _CACHE:
        _CACHE["prog"] = build_program(do_sample)
    nc = _CACHE["prog"]
    consts = _consts()
    f = lambda a: np.ascontiguousarray(np.asarray(a, dtype=np.float32))
    in_maps = []
    for i in range(NCORES):
        m = {}
        m["xp"] = f(inputs["x_prompt"][i * SP:(i + 1) * SP])
        m["xs"] = f(inputs["x_sample"][i * SB:(i + 1) * SB, 0])
        m["spool"] = f(inputs["state_pool"][:, i * SB:(i + 1) * SB])
        m["ck"] = f(np.asarray(inputs["cache_swa_k"])[:, i * SB:(i + 1) * SB].reshape(2, SB, 128, 128))
        m["cv"] = f(np.asarray(inputs["cache_swa_v"])[:, i * SB:(i + 1) * SB].reshape(2, SB, 128, 128))
        m["sshift"] = f(inputs["state_rwkv_shift"][:, i * SB:(i + 1) * SB])
        m["swkv"] = f(inputs["state_rwkv_wkv"][:, i * SB:(i + 1) * SB])
        for n in PARAM_NAMES:
            m[n] = f(inputs[n])
        for n, v in consts.items():
            m["c_" + n] = v
        in_maps.append(m)
    res = run_bass_kernel_spmd(nc, in_maps, core_ids=list(range(NCORES)))
    R = res.results
    cat = lambda n, ax: np.concatenate([np.asarray(r[n], dtype=np.float32) for r in R], axis=ax)
    y_p = cat("yp", 0)
    y_s = cat("ys", 0).reshape(NCORES * SB, 1, D)
    return (y_p, y_s,
            cat("poolp", 1), cat("pools", 1),
            cat("kp", 1).reshape(2, NCORES * SP, 128, 2, 64), cat("ksm", 1).reshape(2, NCORES * SB, 128, 2, 64),
            cat("vp", 1).reshape(2, NCORES * SP, 128, 2, 64), cat("vsm", 1).reshape(2, NCORES * SB, 128, 2, 64),
            cat("shp", 1), cat("shs", 1), cat("wkp", 1), cat("wks", 1))
```
